# Optimizing a Trainium2 kernel written in Bass

```python
import math
import jax
import jax.numpy as jnp
from jax import lax
import numpy as np

D_MODEL = 1024
BATCH = 1
SEQ = 16384
DEPTH = 2

GRID_W = 64
CTX_LEN = 256

LRU_WIDTH = 512
LRU_BLOCKS = 8
LRU_BLOCK = LRU_WIDTH // LRU_BLOCKS
CONV_W = 4
LRU_C = 8.0
S5_WIDTH = 512
S5_GROUP = 16
S5_GROUPS = S5_WIDTH // S5_GROUP
S5_STATE = 64
DA_HEADS = 4
DA_HEAD_DIM = 64
DA_V_DIM = 2 * DA_HEAD_DIM
DA_QK_WIDTH = DA_HEADS * 2 * DA_HEAD_DIM
DA_WIDTH = DA_HEADS * DA_V_DIM
DA_SCALE = DA_HEAD_DIM ** -0.5
Q_BLOCK = 128
ROPE_BASE = 10000.0
ROPE_FREQS = DA_HEAD_DIM // 4
N_BRANCH = 3
BRANCH_WIDTH = 512
IN_SPLITS = [LRU_WIDTH, 2 * LRU_WIDTH, 2 * LRU_WIDTH + S5_WIDTH,
             2 * LRU_WIDTH + S5_WIDTH + DA_QK_WIDTH,
             2 * LRU_WIDTH + S5_WIDTH + 2 * DA_QK_WIDTH,
             2 * LRU_WIDTH + S5_WIDTH + 2 * DA_QK_WIDTH + DA_WIDTH]
IN_COLS = IN_SPLITS[-1] + N_BRANCH * D_MODEL
N_EXPERTS = 32
TOP_K = 4
D_FF = D_MODEL
SWIGLU_ALPHA = 1.702
SWIGLU_LIMIT = 7.0
MOE_BLOCK = 128
DN_ALPHA = (2 * DEPTH) ** 0.25
DN_BETA = (8 * DEPTH) ** -0.25
LN_EPS = 1e-5
RMS_EPS = 1e-6

kernel_name = "hybrid_rglru_s5_diffattn_moe_dit"


def layer_norm(x, g, b):
    xf = x.astype(jnp.float32)
    mu = jnp.mean(xf, -1, keepdims=True)
    var = jnp.mean(jnp.square(xf - mu), -1, keepdims=True)
    return ((xf - mu) * lax.rsqrt(var + LN_EPS) * g + b).astype(x.dtype)


def centred_dwconv(x, w, b):
    L = x.shape[1]
    left = CONV_W // 2
    xp = jnp.pad(x, ((0, 0), (left, CONV_W - 1 - left), (0, 0)))
    out = b
    for k in range(CONV_W):
        out = out + xp[:, k:k + L] * w[k]
    return out


def _lin_op(l, r):
    a1, b1 = l
    a2, b2 = r
    return a1 * a2, a2 * b1 + b2


def lin_scan(a, b, h0=None):
    A, H = lax.associative_scan(_lin_op, (a, b), axis=1)
    return H if h0 is None else H + A * h0[:, None]


def _cplx_op(l, r):
    a1r, a1i, b1r, b1i = l
    a2r, a2i, b2r, b2i = r
    return (a2r * a1r - a2i * a1i, a2r * a1i + a2i * a1r,
            a2r * b1r - a2i * b1i + b2r, a2r * b1i + a2i * b1r + b2i)


def cplx_scan(ar, ai, br, bi, h0=None):
    Ar, Ai, Hr, Hi = lax.associative_scan(_cplx_op, (ar, ai, br, bi), axis=1)
    if h0 is None:
        return Hr, Hi
    h0r, h0i = h0[0][:, None], h0[1][:, None]
    return Hr + Ar * h0r - Ai * h0i, Hi + Ar * h0i + Ai * h0r


def rope_2d(x, ang):
    xs = x.reshape(x.shape[:-1] + (2, 2, ROPE_FREQS))
    cos = jnp.cos(ang).astype(x.dtype)[None, :, None, None]
    sin = jnp.sin(ang).astype(x.dtype)[None, :, None, None]
    x1, x2 = xs[..., 0, :], xs[..., 1, :]
    return jnp.stack([x1 * cos - x2 * sin, x2 * cos + x1 * sin], axis=-2).reshape(x.shape)


def rglru_mixer(xa_c, xa_l, ga_c, ga_l, conv_w, conv_b, wr, br, wi, bi, lam, need_ctx):
    xc = centred_dwconv(xa_c, conv_w, conv_b)
    xl = centred_dwconv(xa_l, conv_w, conv_b)

    def coeffs(x, d):
        xb = x.reshape(x.shape[:-1] + (LRU_BLOCKS, LRU_BLOCK))
        r = jax.nn.sigmoid(jnp.einsum('blhi,hij->blhj', xb, wr[d]).reshape(x.shape) + br[d])
        i = jax.nn.sigmoid(jnp.einsum('blhi,hij->blhj', xb, wi[d]).reshape(x.shape) + bi[d])
        log_a = -LRU_C * r * jax.nn.softplus(-lam[d])
        return jnp.exp(log_a), jnp.sqrt(-jnp.expm1(2.0 * log_a)) * (i * x)

    def direction(d, seq_c, seq_l):
        h_c = lin_scan(*coeffs(seq_c, d))
        h_l = lin_scan(*coeffs(seq_l, d), h0=h_c[:, -1])
        return h_c, h_l

    hc_f, hl_f = direction(0, xc, xl)
    hc_b, hl_b = direction(1, xc[:, ::-1], xl[:, ::-1])
    y_l = (hl_f + hl_b[:, ::-1]) * jax.nn.gelu(ga_l)
    y_c = (hc_f + hc_b[:, ::-1]) * jax.nn.gelu(ga_c) if need_ctx else None
    return y_c, y_l


def s5_mixer(ub_c, ub_l, a_re, a_im, log_dt, b_re, b_im, c_re, c_im, d_skip, w_glu, b_glu, need_ctx):
    uc = ub_c.reshape(ub_c.shape[:-1] + (S5_GROUPS, S5_GROUP))
    ul = ub_l.reshape(ub_l.shape[:-1] + (S5_GROUPS, S5_GROUP))

    def direction(d, seq_c, seq_l):
        dt = jnp.exp(log_dt[d])[:, None]
        lr, li = a_re[d], a_im[d]
        ea = jnp.exp(lr * dt)
        ab_r, ab_i = ea * jnp.cos(li * dt), ea * jnp.sin(li * dt)
        den = lr * lr + li * li
        co_r = ((ab_r - 1.0) * lr + ab_i * li) / den
        co_i = (ab_i * lr - (ab_r - 1.0) * li) / den
        bb_r = co_r[..., None] * b_re[d] - co_i[..., None] * b_im[d]
        bb_i = co_r[..., None] * b_im[d] + co_i[..., None] * b_re[d]

        def run(u, h0):
            dr = jnp.einsum('blgc,gnc->blgn', u, bb_r)
            di = jnp.einsum('blgc,gnc->blgn', u, bb_i)
            return cplx_scan(jnp.broadcast_to(ab_r, dr.shape), jnp.broadcast_to(ab_i, dr.shape), dr, di, h0)

        def readout(h):
            return jnp.einsum('blgn,gcn->blgc', h[0], c_re[d]) - jnp.einsum('blgn,gcn->blgc', h[1], c_im[d])

        h_c = run(seq_c, None)
        h_l = run(seq_l, (h_c[0][:, -1], h_c[1][:, -1]))
        return (readout(h_c) if need_ctx else None), readout(h_l)

    yc_f, yl_f = direction(0, uc, ul)
    yc_b, yl_b = direction(1, uc[:, ::-1], ul[:, ::-1])

    def finish(y_f, y_b, u):
        y = y_f + y_b[:, ::-1] + d_skip.reshape(S5_GROUPS, S5_GROUP) * u
        y = y.reshape(y.shape[:2] + (S5_WIDTH,))
        a, g = jnp.split(jax.nn.gelu(y) @ w_glu + b_glu, 2, axis=-1)
        return a * jax.nn.sigmoid(g)

    y_l = finish(yl_f, yl_b, ul)
    y_c = finish(yc_f, yc_b, uc) if need_ctx else None
    return y_c, y_l


def diff_attention(q_c, k_c, v_c, q_l, k_l, v_l, lam_vec, subln_g, layer_idx, ang, need_ctx):
    split_qk = lambda t: t.reshape(t.shape[:-1] + (DA_HEADS, 2, DA_HEAD_DIM))
    split_v = lambda t: t.reshape(t.shape[:-1] + (DA_HEADS, DA_V_DIM))
    q_c, k_c, q_l, k_l = split_qk(q_c), split_qk(k_c), split_qk(q_l), split_qk(k_l)
    v_c, v_l = split_v(v_c), split_v(v_l)
    lam_init = 0.8 - 0.6 * math.exp(-0.3 * layer_idx)
    lv = lam_vec.astype(jnp.float32)
    lam = jnp.exp(jnp.sum(lv[0] * lv[1])) - jnp.exp(jnp.sum(lv[2] * lv[3])) + lam_init
    q_l, k_l = rope_2d(q_l, ang), rope_2d(k_l, ang)
    k_all = jnp.concatenate([k_c, k_l], axis=1)
    v_all = jnp.concatenate([v_c, v_l], axis=1)

    def attend(qb, kk, vv):
        s = jnp.einsum('bqhmd,bkhmd->bhmqk', qb, kk).astype(jnp.float32) * DA_SCALE
        p = jax.nn.softmax(s, axis=-1)
        w = (p[:, :, 0] - lam * p[:, :, 1]).astype(vv.dtype)
        return jnp.einsum('bhqk,bkhe->bqhe', w, vv)

    B, L = q_l.shape[:2]
    nb = L // Q_BLOCK
    qb = jnp.moveaxis(q_l.reshape((B, nb, Q_BLOCK) + q_l.shape[2:]), 1, 0)
    o_l = lax.map(lambda q: attend(q, k_all, v_all), qb)
    o_l = jnp.moveaxis(o_l, 0, 1).reshape(B, L, DA_HEADS, DA_V_DIM)

    def head_norm(o):
        of = o.astype(jnp.float32)
        of = of * lax.rsqrt(jnp.mean(of * of, -1, keepdims=True) + RMS_EPS) * subln_g * (1.0 - lam_init)
        return of.astype(o.dtype).reshape(o.shape[:2] + (DA_WIDTH,))

    y_l = head_norm(o_l)
    y_c = head_norm(attend(q_c, k_c, v_c)) if need_ctx else None
    return y_c, y_l


def token_mixer(u_c, u_l, layer_idx, need_ctx, ang, w_in, conv_w, conv_b, lru_wr, lru_br, lru_wi,
                lru_bi, lru_lam, s5_a_re, s5_a_im, s5_log_dt, s5_b_re, s5_b_im, s5_c_re, s5_c_im,
                s5_d, s5_w_glu, s5_b_glu, da_lam, da_subln_g, w_proj, b_gate, w_out):
    xa_c, ga_c, ub_c, q_c, k_c, v_c, gt_c = jnp.split(u_c @ w_in, IN_SPLITS, axis=-1)
    xa_l, ga_l, ub_l, q_l, k_l, v_l, gt_l = jnp.split(u_l @ w_in, IN_SPLITS, axis=-1)
    ya_c, ya_l = rglru_mixer(xa_c, xa_l, ga_c, ga_l, conv_w, conv_b, lru_wr, lru_br, lru_wi, lru_bi,
                             lru_lam, need_ctx)
    yb_c, yb_l = s5_mixer(ub_c, ub_l, s5_a_re, s5_a_im, s5_log_dt, s5_b_re, s5_b_im, s5_c_re, s5_c_im,
                          s5_d, s5_w_glu, s5_b_glu, need_ctx)
    yc_c, yc_l = diff_attention(q_c, k_c, v_c, q_l, k_l, v_l, da_lam, da_subln_g, layer_idx, ang, need_ctx)

    def merge(gt, ya, yb, yc):
        g = jax.nn.sigmoid(gt.reshape(gt.shape[:-1] + (N_BRANCH, D_MODEL)) + b_gate)
        z = (g[..., 0, :] * (ya @ w_proj[0]) + g[..., 1, :] * (yb @ w_proj[1])
             + g[..., 2, :] * (yc @ w_proj[2]))
        return z @ w_out

    m_l = merge(gt_l, ya_l, yb_l, yc_l)
    m_c = merge(gt_c, ya_c, yb_c, yc_c) if need_ctx else None
    return m_c, m_l


def moe(u, w_router, b_router, w_gu, b_gu, w_down, b_down):
    T = u.shape[0]
    n_assign = T * TOP_K
    n_slots = -(-n_assign // MOE_BLOCK) * MOE_BLOCK + N_EXPERTS * MOE_BLOCK
    logits = (u @ w_router + b_router).astype(jnp.float32)
    top_v, top_i = lax.top_k(logits, TOP_K)
    gate_w = jax.nn.softmax(top_v, axis=-1).astype(u.dtype)
    flat_e = top_i.reshape(-1)
    flat_t = jnp.arange(n_assign, dtype=jnp.int32) // TOP_K
    flat_w = gate_w.reshape(-1)
    order = jnp.argsort(flat_e)
    se = flat_e[order]
    counts = jnp.bincount(flat_e, length=N_EXPERTS)
    padded = (counts + MOE_BLOCK - 1) // MOE_BLOCK * MOE_BLOCK
    start = jnp.cumsum(counts) - counts
    pend = jnp.cumsum(padded)
    pstart = pend - padded
    dest = pstart[se] + jnp.arange(n_assign, dtype=jnp.int32) - start[se]
    slot_tok = jnp.zeros((n_slots,), jnp.int32).at[dest].set(flat_t[order])
    slot_w = jnp.zeros((n_slots,), u.dtype).at[dest].set(flat_w[order])
    nb = n_slots // MOE_BLOCK
    blk_e = jnp.clip(jnp.searchsorted(pend, jnp.arange(nb) * MOE_BLOCK, side='right'), 0, N_EXPERTS - 1)
    xs = u[slot_tok].reshape(nb, MOE_BLOCK, u.shape[-1])

    def expert_block(args):
        xb, e = args
        h = xb @ w_gu[e] + b_gu[e]
        hg, hl = h[:, :D_FF], h[:, D_FF:]
        hg = jnp.minimum(hg, SWIGLU_LIMIT)
        hl = jnp.clip(hl, -SWIGLU_LIMIT, SWIGLU_LIMIT)
        act = hg * jax.nn.sigmoid(SWIGLU_ALPHA * hg) * (hl + 1.0)
        return act @ w_down[e] + b_down[e]

    ys = lax.map(expert_block, (xs, blk_e)).reshape(n_slots, u.shape[-1])
    return jnp.zeros_like(u).at[slot_tok].add(ys * slot_w[:, None])


def setup_inputs(seed: int = 0) -> dict:
    key = jax.random.key(seed)
    ks = iter(jax.random.split(key, 48))
    f32 = jnp.float32

    def nrm(shape, scale):
        return scale * jax.random.normal(next(ks), shape, f32)

    a0 = jax.random.uniform(next(ks), (DEPTH, 2, LRU_WIDTH), f32, minval=0.9, maxval=0.999)
    s0 = a0 ** (1.0 / LRU_C)
    lru_lam = jnp.log(s0) - jnp.log1p(-s0)
    s5_a_re = -0.5 + nrm((DEPTH, 2, S5_GROUPS, S5_STATE), 0.01)
    s5_a_im = jnp.pi * jnp.arange(S5_STATE, dtype=f32) + nrm((DEPTH, 2, S5_GROUPS, S5_STATE), 0.01)
    s5_log_dt = jax.random.uniform(next(ks), (DEPTH, 2, S5_GROUPS), f32,
                                   minval=math.log(1e-3), maxval=math.log(1e-1))
    return {
        "x": nrm((BATCH, SEQ, D_MODEL), 1.0),
        "c": nrm((BATCH, D_MODEL), 1.0),
        "ctx": nrm((BATCH, CTX_LEN, D_MODEL), 1.0),
        "c_ctx": nrm((D_MODEL,), 1.0),
        "w_mod": nrm((DEPTH, D_MODEL, 6 * D_MODEL), 0.5 * D_MODEL ** -0.5),
        "b_mod": nrm((DEPTH, 6 * D_MODEL), 0.01),
        "w_in": nrm((DEPTH, D_MODEL, IN_COLS), D_MODEL ** -0.5),
        "conv_w": nrm((DEPTH, CONV_W, LRU_WIDTH), CONV_W ** -0.5),
        "conv_b": nrm((DEPTH, LRU_WIDTH), 0.01),
        "lru_wr": nrm((DEPTH, 2, LRU_BLOCKS, LRU_BLOCK, LRU_BLOCK), LRU_BLOCK ** -0.5),
        "lru_br": nrm((DEPTH, 2, LRU_WIDTH), 0.01),
        "lru_wi": nrm((DEPTH, 2, LRU_BLOCKS, LRU_BLOCK, LRU_BLOCK), LRU_BLOCK ** -0.5),
        "lru_bi": nrm((DEPTH, 2, LRU_WIDTH), 0.01),
        "lru_lam": lru_lam,
        "s5_a_re": s5_a_re,
        "s5_a_im": s5_a_im,
        "s5_log_dt": s5_log_dt,
        "s5_b_re": nrm((DEPTH, 2, S5_GROUPS, S5_STATE, S5_GROUP), (2 * S5_GROUP) ** -0.5),
        "s5_b_im": nrm((DEPTH, 2, S5_GROUPS, S5_STATE, S5_GROUP), (2 * S5_GROUP) ** -0.5),
        "s5_c_re": nrm((DEPTH, 2, S5_GROUPS, S5_GROUP, S5_STATE), S5_STATE ** -0.5),
        "s5_c_im": nrm((DEPTH, 2, S5_GROUPS, S5_GROUP, S5_STATE), S5_STATE ** -0.5),
        "s5_d": nrm((DEPTH, S5_WIDTH), 1.0),
        "s5_w_glu": nrm((DEPTH, S5_WIDTH, 2 * S5_WIDTH), S5_WIDTH ** -0.5),
        "s5_b_glu": nrm((DEPTH, 2 * S5_WIDTH), 0.01),
        "da_lam": nrm((DEPTH, 4, DA_HEAD_DIM), 0.1),
        "da_subln_g": 1.0 + nrm((DEPTH, DA_V_DIM), 0.01),
        "w_proj": nrm((DEPTH, N_BRANCH, BRANCH_WIDTH, D_MODEL), DN_BETA * BRANCH_WIDTH ** -0.5),
        "b_gate": nrm((DEPTH, N_BRANCH, D_MODEL), 0.01),
        "w_out": nrm((DEPTH, D_MODEL, D_MODEL), DN_BETA * D_MODEL ** -0.5),
        "ln_g": 1.0 + nrm((DEPTH, 2, D_MODEL), 0.01),
        "ln_b": nrm((DEPTH, 2, D_MODEL), 0.01),
        "w_router": nrm((DEPTH, D_MODEL, N_EXPERTS), D_MODEL ** -0.5),
        "b_router": nrm((DEPTH, N_EXPERTS), 0.01),
        "w_gu": nrm((DEPTH, N_EXPERTS, D_MODEL, 2 * D_FF), DN_BETA * D_MODEL ** -0.5),
        "b_gu": nrm((DEPTH, N_EXPERTS, 2 * D_FF), 0.01),
        "w_down": nrm((DEPTH, N_EXPERTS, D_FF, D_MODEL), DN_BETA * D_FF ** -0.5),
        "b_down": nrm((DEPTH, N_EXPERTS, D_MODEL), 0.01),
    }


def reference(x, c, ctx, c_ctx, w_mod, b_mod, w_in, conv_w, conv_b, lru_wr, lru_br, lru_wi, lru_bi,
              lru_lam, s5_a_re, s5_a_im, s5_log_dt, s5_b_re, s5_b_im, s5_c_re, s5_c_im, s5_d, s5_w_glu,
              s5_b_glu, da_lam, da_subln_g, w_proj, b_gate, w_out, ln_g, ln_b, w_router, b_router,
              w_gu, b_gu, w_down, b_down):
    B, L, D = x.shape
    ROWS = L // GRID_W
    rows = jnp.broadcast_to(jnp.arange(ROWS)[:, None], (ROWS, GRID_W)).reshape(-1)
    cols = jnp.broadcast_to(jnp.arange(GRID_W)[None, :], (ROWS, GRID_W)).reshape(-1)
    freqs = ROPE_BASE ** (-jnp.arange(ROPE_FREQS, dtype=jnp.float32) / ROPE_FREQS)
    ang = jnp.stack([rows, cols], axis=-1).astype(jnp.float32)[:, :, None] * freqs

    xl, xc = x, ctx
    for l in range(DEPTH):
        last = l == DEPTH - 1
        mod_l = jax.nn.silu(c) @ w_mod[l] + b_mod[l]
        mod_c = jax.nn.silu(c_ctx) @ w_mod[l] + b_mod[l]
        sh1_l, sc1_l, g1_l, sh2_l, sc2_l, g2_l = [m[:, None] for m in jnp.split(mod_l, 6, axis=-1)]
        sh1_c, sc1_c, g1_c, sh2_c, sc2_c, g2_c = jnp.split(mod_c, 6, axis=-1)

        u_c = xc * (1.0 + sc1_c) + sh1_c
        u_l = xl * (1.0 + sc1_l) + sh1_l
        m_c, m_l = token_mixer(
            u_c, u_l, l, not last, ang, w_in[l], conv_w[l], conv_b[l], lru_wr[l], lru_br[l],
            lru_wi[l], lru_bi[l], lru_lam[l], s5_a_re[l], s5_a_im[l], s5_log_dt[l], s5_b_re[l],
            s5_b_im[l], s5_c_re[l], s5_c_im[l], s5_d[l], s5_w_glu[l], s5_b_glu[l], da_lam[l],
            da_subln_g[l], w_proj[l], b_gate[l], w_out[l])
        xl = layer_norm(DN_ALPHA * xl + g1_l * m_l, ln_g[l, 0], ln_b[l, 0])
        if not last:
            xc = layer_norm(DN_ALPHA * xc + g1_c * m_c, ln_g[l, 0], ln_b[l, 0])

        v_l = xl * (1.0 + sc2_l) + sh2_l
        if not last:
            v_c = xc * (1.0 + sc2_c) + sh2_c
            tok = jnp.concatenate([v_c, v_l], axis=1).reshape(-1, D)
            f = moe(tok, w_router[l], b_router[l], w_gu[l], b_gu[l], w_down[l], b_down[l])
            f = f.reshape(B, CTX_LEN + L, D)
            xc = layer_norm(DN_ALPHA * xc + g2_c * f[:, :CTX_LEN], ln_g[l, 1], ln_b[l, 1])
            f_l = f[:, CTX_LEN:]
        else:
            f_l = moe(v_l.reshape(-1, D), w_router[l], b_router[l], w_gu[l], b_gu[l],
                      w_down[l], b_down[l]).reshape(B, L, D)
        xl = layer_norm(DN_ALPHA * xl + g2_l * f_l, ln_g[l, 1], ln_b[l, 1])
    return xl
```

```python
import numpy as np
import concourse.bass as bass
import concourse.mybir as mybir
from concourse.bass_utils import run_bass_kernel_spmd
from contextlib import ExitStack

F32 = mybir.dt.float32
BF16 = mybir.dt.bfloat16
I32 = mybir.dt.int32
ALU = mybir.AluOpType
AF = mybir.ActivationFunctionType
AX = mybir.AxisListType

ENGS = ['pe', 'dve', 'act', 'pool', 'sp']
SAME_ENGINE_SYNC = True


class Dep:
    __slots__ = ('w', 'r', 'dw', 'dr', 'name', 'dsem')

    def __init__(self, name=''):
        self.w = {}
        self.r = {}
        self.dw = set()
        self.dr = set()
        self.name = name
        self.dsem = None


class DSem:
    __slots__ = ('sem', 'total')

    def __init__(self, sem):
        self.sem = sem
        self.total = 0


class Prog:
    def __init__(self, nc, stack):
        self.nc = nc
        self.stack = stack
        self.eng = dict(pe=nc.tensor, dve=nc.vector, act=nc.scalar, pool=nc.gpsimd, sp=nc.sync)
        self.sem = {e: stack.enter_context(nc.semaphore('s_' + e)) for e in ENGS}
        self.cnt = {e: 0 for e in ENGS}
        self.prog = {e: [] for e in ENGS}
        self.clock = {e: {f: 0 for f in ENGS} for e in ENGS}
        self.hist = {e: [tuple(0 for _ in ENGS)] for e in ENGS}
        self.dknown = {e: {} for e in ENGS}
        self.dsems = []
        self.ninstr = 0

    def sbuf(self, name, shape, dt=F32):
        return self.stack.enter_context(self.nc.sbuf_tensor(name, list(shape), dt))

    def psum(self, name, shape, dt=F32):
        return self.stack.enter_context(self.nc.psum_tensor(name, list(shape), dt))

    def new_dsem(self, name):
        s = DSem(self.stack.enter_context(self.nc.semaphore(name)))
        self.dsems.append(s)
        return s

    def _collect(self, reads, writes, accum=False):
        need = {}
        dneed = set()
        for d in reads:
            for f, c in d.w.items():
                if need.get(f, 0) < c:
                    need[f] = c
            dneed |= d.dw
        for d in writes:
            if not accum:
                for f, c in d.w.items():
                    if need.get(f, 0) < c:
                        need[f] = c
                dneed |= d.dw
            for f, c in d.r.items():
                if need.get(f, 0) < c:
                    need[f] = c
            dneed |= d.dr
        return need, dneed

    def _emit_waits(self, e, need, dneed, pe_acc=False):
        ck = self.clock[e]
        for f, c in need.items():
            if ck[f] >= c:
                continue
            if f == e and (not SAME_ENGINE_SYNC or e == 'pe' and pe_acc):
                continue
            self.prog[e].append(('wait', self.sem[f], c))
            h = self.hist[f][c]
            for i, g in enumerate(ENGS):
                if ck[g] < h[i]:
                    ck[g] = h[i]
            if ck[f] < c:
                ck[f] = c
        dk = self.dknown[e]
        for s in dneed:
            if dk.get(s, 0) >= s.total:
                continue
            self.prog[e].append(('wait', s.sem, s.total))
            dk[s] = s.total

    def op(self, e, fn, reads=(), writes=(), pe_acc=False, accum=False):
        need, dneed = self._collect(reads, writes, accum)
        self._emit_waits(e, need, dneed, pe_acc)
        self.cnt[e] += 1
        j = self.cnt[e]
        self.prog[e].append(('op', fn))
        ck = self.clock[e]
        snap = tuple(j if g == e else ck[g] for g in ENGS)
        self.hist[e].append(snap)
        if not SAME_ENGINE_SYNC:
            ck[e] = j
        for d in writes:
            if accum:
                d.w[e] = j
            else:
                d.w = {e: j}
                d.dw = set()
            d.r = {}
            d.dr = set()
        for d in reads:
            if d.w.get(e) == j:
                continue
            d.r[e] = j
        self.ninstr += 1

    def dma(self, q, out, in_, reads=(), writes=(), key=None, accum=False, **kw):
        if key is None:
            key = writes[0] if writes else reads[0]
        if key.dsem is None:
            key.dsem = self.new_dsem('d_' + key.name)
        ds = key.dsem
        need, dneed = self._collect(reads, writes, accum)
        self._emit_waits(q, need, dneed)
        ds.total += 16
        self.prog[q].append(('dma', out, in_, ds.sem, kw))
        for d in writes:
            if accum:
                d.dw.add(ds)
            else:
                d.w = {}
                d.dw = {ds}
            d.r = {}
            d.dr = set()
        for d in reads:
            d.dr.add(ds)
        self.ninstr += 1

    def finish(self):
        nc = self.nc
        fin = self.prog['sp']
        for f in ENGS:
            if f != 'sp' and self.cnt[f] > 0:
                fin.append(('wait', self.sem[f], self.cnt[f]))
        for s in self.dsems:
            if s.total > 0:
                fin.append(('wait', s.sem, s.total))
        prog = self.prog
        sems = self.sem

        def run(e, engine):
            for it in prog[e]:
                if it[0] == 'wait':
                    engine.wait_ge(it[1], it[2])
                elif it[0] == 'op':
                    it[1](engine).then_inc(sems[e], 1)
                else:
                    engine.dma_start(out=it[1], in_=it[2], **it[4]).then_inc(it[3], 16)

        with nc.Block() as block:
            @block.tensor
            def _(eng):
                run('pe', eng)

            @block.vector
            def _(eng):
                run('dve', eng)

            @block.scalar
            def _(eng):
                run('act', eng)

            @block.gpsimd
            def _(eng):
                run('pool', eng)

            @block.sync
            def _(eng):
                run('sp', eng)


class Ring:
    def __init__(self, P, name, shape, dt, n, psum=False):
        mk = P.psum if psum else P.sbuf
        self.tiles = [mk(f"{name}{i}", shape, dt) for i in range(n)]
        self.deps = [Dep(f"{name}{i}") for i in range(n)]
        self.n = n
        self.i = 0

    def next(self):
        k = self.i % self.n
        self.i += 1
        return self.tiles[k], self.deps[k]


D_MODEL = 1024
SEQ = 16384
CTX = 256
TT = CTX + SEQ
NCORE = 8
LAT_PC = SEQ // NCORE
TOK = CTX + LAT_PC
BLOCKS = [(0, 256)] + [(256 + 512 * i, 512) for i in range(4)]
TWO_PI_HI = 6.28125
TWO_PI_LO = 0.0019353071795864769
PI_SAFE = 3.1415925
DN_ALPHA = 4.0 ** 0.25
LN_EPS = 1e-5
RMS_EPS = 1e-6


def sin_reduced(P, out_t, ang_t, shape, deps_in, dep_out, tmp, tmpi, dtmp, dtmpi, shift=0.0):
    P.op('dve', lambda e: e.tensor_scalar(tmp, ang_t, shift, 1.0 / (2 * np.pi), ALU.add, ALU.mult),
         reads=deps_in, writes=[dtmp])
    P.op('dve', lambda e: e.tensor_copy(tmpi, tmp), reads=[dtmp], writes=[dtmpi])
    P.op('dve', lambda e: e.tensor_copy(tmp, tmpi), reads=[dtmpi], writes=[dtmp])
    P.op('dve', lambda e: e.scalar_tensor_tensor(out_t, tmp, -TWO_PI_HI, ang_t, ALU.mult, ALU.add),
         reads=[dtmp] + list(deps_in), writes=[dep_out])
    P.op('dve', lambda e: e.scalar_tensor_tensor(out_t, tmp, -TWO_PI_LO, out_t, ALU.mult, ALU.add),
         reads=[dtmp, dep_out], writes=[dep_out])
    P.op('dve', lambda e: e.tensor_scalar(out_t, out_t, shift, PI_SAFE, ALU.add, ALU.min),
         reads=[dep_out], writes=[dep_out])
    P.op('dve', lambda e: e.tensor_scalar(out_t, out_t, -PI_SAFE, None, ALU.max),
         reads=[dep_out], writes=[dep_out])
    P.op('act', lambda e: e.activation(out_t, out_t, AF.Sin), reads=[dep_out], writes=[dep_out])


def build_A():
    nc = bass.Bass("TRN2", target_bir_lowering=False)
    dt_in = lambda n, s, d=F32: nc.dram_tensor(n, s, d, kind="ExternalInput").ap()
    dt_out = lambda n, s, d=F32: nc.dram_tensor(n, s, d, kind="ExternalOutput").ap()
    xT = dt_in("xT", [1024, TOK])
    cvec = dt_in("cvec", [128, 16])
    w_mod = dt_in("w_mod", [1024, 6144])
    b_modT = dt_in("b_modT", [128, 48])
    w_main = dt_in("w_main", [1024, 3072])
    w_rot = dt_in("w_rot", [1024, 1024])
    pos = dt_in("pos", [2, TOK])
    meta = dt_in("meta", [128, 2])
    modT_o = dt_out("modT", [128, 96])
    xgu_o = dt_out("xguT", [1536, TOK])
    qk_o = dt_out("qkT", [1024, TOK], BF16)
    v_o = dt_out("v", [TOK, 512], BF16)
    with ExitStack() as st:
        P = Prog(nc, st)
        cv = P.sbuf("cv", [128, 16]); dcv = Dep("cv")
        sg = P.sbuf("sg", [128, 16]); dsg = Dep("sg")
        bm = P.sbuf("bm", [128, 48]); dbm = Dep("bm")
        modT = P.sbuf("modTs", [128, 96]); dmod = Dep("mod")
        P.dma('sp', cv[:], cvec, writes=[dcv])
        P.dma('sp', bm[:], b_modT, writes=[dbm])
        P.op('act', lambda e: e.activation(sg[:], cv[:], AF.Sigmoid), reads=[dcv], writes=[dsg])
        P.op('dve', lambda e: e.tensor_tensor(sg[:], sg[:], cv[:], ALU.mult), reads=[dsg, dcv], writes=[dsg])
        pm = P.psum("pm", [128, 512]); dpm = Dep("pm")
        wm = Ring(P, "wm", [128, 8, 256], F32, 2)
        w_mod_v = w_mod.rearrange("(k p) j -> p k j", p=128)
        for g in range(24):
            wt, dw = wm.next()
            P.dma('sp' if g % 2 == 0 else 'act', wt[:], w_mod_v[:, :, g * 256:(g + 1) * 256], writes=[dw])
            for jj in range(2):
                jc = g * 2 + jj
                for k in range(8):
                    P.op('pe', lambda e, wt=wt, jj=jj, k=k, jc=jc: e.matmul(
                        pm[:, jc * 2:jc * 2 + 2], wt[:, k, jj * 128:(jj + 1) * 128], sg[:, k * 2:k * 2 + 2],
                        start=(k == 0), stop=(k == 7)), reads=[dw, dsg], writes=[dpm], pe_acc=True)
        bmv = bm[:].unsqueeze(2).to_broadcast([128, 48, 2])
        P.op('dve', lambda e: e.tensor_tensor(modT[:].rearrange("p (a b) -> p a b", b=2),
                                              pm[:, 0:96].rearrange("p (a b) -> p a b", b=2), bmv, ALU.add),
             reads=[dpm, dbm], writes=[dmod])
        P.dma('sp', modT_o, modT[:], reads=[dmod])
        sc1p = P.sbuf("sc1p", [128, 16]); dsc = Dep("sc1p")
        P.op('dve', lambda e: e.tensor_scalar(sc1p[:], modT[:, 16:32], 1.0, None, ALU.add), reads=[dmod], writes=[dsc])
        uT = P.sbuf("uT", [128, 8, TOK], BF16); du = [Dep(f"u{k}") for k in range(8)]
        xr = Ring(P, "xs", [128, TOK], F32, 2)
        xT_v = xT.rearrange("(k p) t -> p k t", p=128)
        for k in range(8):
            xt, dx = xr.next()
            P.dma('sp' if k % 2 == 0 else 'act', xt[:], xT_v[:, k, :], writes=[dx])
            P.op('dve', lambda e, xt=xt, k=k: e.tensor_scalar(uT[:, k, 0:CTX], xt[:, 0:CTX], sc1p[:, 2 * k + 1:2 * k + 2],
                                                             modT[:, 2 * k + 1:2 * k + 2], ALU.mult, ALU.add),
                 reads=[dx, dsc, dmod], writes=[du[k]])
            P.op('pool', lambda e, xt=xt, k=k: e.tensor_scalar(uT[:, k, CTX:TOK], xt[:, CTX:TOK], sc1p[:, 2 * k:2 * k + 1],
                                                              modT[:, 2 * k:2 * k + 1], ALU.mult, ALU.add),
                 reads=[dx, dsc, dmod], writes=[du[k]], accum=True)
        mt = P.sbuf("mt", [128, 2]); dmt = Dep("mt")
        P.dma('sp', mt[:], meta, writes=[dmt])
        frq = P.sbuf("frq", [128, 1]); dfr = Dep("frq")
        P.op('act', lambda e: e.activation(frq[:], mt[:, 0:1], AF.Exp, scale=-float(np.log(10000.0) / 16.0)),
             reads=[dmt], writes=[dfr])
        ang = P.sbuf("ang", [128, TOK]); dang = Dep("ang")
        for a in range(2):
            for hh in range(2):
                p0 = hh * 64 + a * 32
                P.dma('sp', ang[p0:p0 + 32, :], pos[a:a + 1, :].partition_broadcast(32), writes=[dang], accum=True)
        P.op('dve', lambda e: e.tensor_scalar(ang[:], ang[:], frq[:, 0:1], None, ALU.mult), reads=[dang, dfr], writes=[dang])
        tmp = P.sbuf("rtmp", [128, TOK]); tmpi = P.sbuf("rtmpi", [128, TOK], I32); dtmp = Dep("rtmp"); dtmpi = Dep("rtmpi")
        sinT = P.sbuf("sinT", [128, TOK]); dsin = Dep("sinT")
        cosT = P.sbuf("cosT", [128, TOK]); dcos = Dep("cosT")
        sin_reduced(P, sinT[:], ang[:], None, [dang], dsin, tmp[:], tmpi[:], dtmp, dtmpi, 0.0)
        sin_reduced(P, cosT[:], ang[:], None, [dang], dcos, tmp[:], tmpi[:], dtmp, dtmpi, float(np.pi / 2))
        P.op('dve', lambda e: e.tensor_scalar(sinT[:], sinT[:], mt[:, 1:2], None, ALU.mult), reads=[dsin, dmt], writes=[dsin])
        wmn = P.sbuf("wmn", [128, 8, 3072], BF16); dwm = [Dep(f"wmn{k}") for k in range(8)]
        wrt = P.sbuf("wrt", [128, 8, 1024], BF16); dwr = [Dep(f"wrt{k}") for k in range(8)]
        w_main_v = w_main.rearrange("(k p) j -> p k j", p=128)
        w_rot_v = w_rot.rearrange("(k p) j -> p k j", p=128)
        for k in range(8):
            for h in range(2):
                P.dma('pool', wmn[:, k, h * 1536:(h + 1) * 1536], w_main_v[:, k, h * 1536:(h + 1) * 1536],
                      writes=[dwm[k]], accum=True)
            P.dma('pool', wrt[:, k, :], w_rot_v[:, k, :], writes=[dwr[k]])
        psA = Ring(P, "psA", [128, 512], F32, 3, psum=True)
        psB = Ring(P, "psB", [128, 512], F32, 2, psum=True)
        ev = Ring(P, "ev", [128, 512], F32, 3)
        evb = Ring(P, "evb", [128, 512], BF16, 3)
        t1r = Ring(P, "t1r", [128, 512], F32, 2)
        qi = 0
        for (t0, n) in BLOCKS:
            for jc in range(20):
                pa, dpa = psA.next()
                for k in range(8):
                    P.op('pe', lambda e, pa=pa, k=k, jc=jc, t0=t0, n=n: e.matmul(
                        pa[:, 0:n], wmn[:, k, jc * 128:(jc + 1) * 128], uT[:, k, t0:t0 + n],
                        start=(k == 0), stop=(k == 7)), reads=[dwm[k], du[k]], writes=[dpa], pe_acc=True)
                if jc < 12:
                    et, de = ev.next()
                    P.op('act', lambda e, et=et, pa=pa, n=n: e.copy(et[:, 0:n], pa[:, 0:n]), reads=[dpa], writes=[de])
                    P.dma('sp', xgu_o[jc * 128:(jc + 1) * 128, t0:t0 + n], et[:, 0:n], reads=[de])
                else:
                    pb, dpb = psB.next()
                    jr = jc - 12
                    for k in range(8):
                        P.op('pe', lambda e, pb=pb, k=k, jr=jr, t0=t0, n=n: e.matmul(
                            pb[:, 0:n], wrt[:, k, jr * 128:(jr + 1) * 128], uT[:, k, t0:t0 + n],
                            start=(k == 0), stop=(k == 7)), reads=[dwr[k], du[k]], writes=[dpb], pe_acc=True)
                    isq = jc < 16
                    ct, dct = (cosT, dcos)
                    sn, dsn = (sinT, dsin)
                    qs = 0.125 if isq else 1.0
                    t1, dt1 = t1r.next()
                    et, de = ev.next()
                    eb, deb = evb.next()
                    P.op('dve', lambda e, t1=t1, pa=pa, ct=ct, t0=t0, n=n, qs=qs: e.scalar_tensor_tensor(
                        t1[:, 0:n], pa[:, 0:n], qs, ct[:, t0:t0 + n], ALU.mult, ALU.mult), reads=[dpa, dct], writes=[dt1])
                    P.op('dve', lambda e, et=et, pb=pb, sn=sn, t0=t0, n=n, qs=qs: e.scalar_tensor_tensor(
                        et[:, 0:n], pb[:, 0:n], qs, sn[:, t0:t0 + n], ALU.mult, ALU.mult), reads=[dpb, dsn], writes=[de])
                    P.op('pool', lambda e, eb=eb, et=et, t1=t1, n=n: e.tensor_tensor(
                        eb[:, 0:n], et[:, 0:n], t1[:, 0:n], ALU.add), reads=[de, dt1], writes=[deb])
                    P.dma('act', qk_o[jr * 128:(jr + 1) * 128, t0:t0 + n], eb[:, 0:n], reads=[deb])
        vb = Ring(P, "vb", [128, 512], BF16, 2)
        for tt in range(TOK // 128):
            pa, dpa = psA.next()
            for k in range(8):
                P.op('pe', lambda e, pa=pa, k=k, tt=tt: e.matmul(
                    pa[:], uT[:, k, tt * 128:(tt + 1) * 128], wmn[:, k, 2560:3072],
                    start=(k == 0), stop=(k == 7)), reads=[dwm[k], du[k]], writes=[dpa], pe_acc=True)
            vt, dv = vb.next()
            P.op('act', lambda e, vt=vt, pa=pa: e.copy(vt[:], pa[:]), reads=[dpa], writes=[dv])
            P.dma('sp', v_o[tt * 128:(tt + 1) * 128, :], vt[:], reads=[dv])
        P.finish()
    return nc


def host_A(inp, l, xT_full):
    cv = np.stack([inp['c'][0], inp['c_ctx']], -1).reshape(8, 128, 2).transpose(1, 0, 2).reshape(128, 16)
    b_modT = np.ascontiguousarray(inp['b_mod'][l].reshape(48, 128).T)
    w_in = inp['w_in'][l]
    perm = np.arange(1024) ^ 16
    w_rot = np.ascontiguousarray(w_in[:, 1536:2560][:, perm])
    w_main = np.ascontiguousarray(w_in[:, :3072])
    p = np.arange(128)
    meta = np.stack([(p % 16).astype(np.float32), np.where((p % 32) // 16 == 0, -1.0, 1.0).astype(np.float32)], -1)
    maps = []
    for c in range(NCORE):
        cols = np.concatenate([np.arange(CTX), CTX + c * LAT_PC + np.arange(LAT_PC)])
        tl = c * LAT_PC + np.arange(LAT_PC)
        pos = np.zeros((2, TOK), np.float32)
        pos[0, CTX:] = tl // 64
        pos[1, CTX:] = tl % 64
        maps.append(dict(xT=np.ascontiguousarray(xT_full[:, cols]), cvec=np.ascontiguousarray(cv, dtype=np.float32),
                         w_mod=inp['w_mod'][l], b_modT=b_modT, w_main=w_main, w_rot=w_rot, pos=pos,
                         meta=np.ascontiguousarray(meta)))
    return maps


SBLOCKS = [(0, 256)] + [(256 + 512 * i, 512) for i in range(32)]
XPAD = TT + 8


def build_B():
    nc = bass.Bass("TRN2", target_bir_lowering=False)
    dt_in = lambda n, s, d=F32: nc.dram_tensor(n, s, d, kind="ExternalInput").ap()
    dt_out = lambda n, s, d=F32: nc.dram_tensor(n, s, d, kind="ExternalOutput").ap()
    xa2 = dt_in("xa2", [128, XPAD])
    ub2 = dt_in("ub2", [128, TT])
    lru = dt_in("lru", [128, 9])
    wrbd = dt_in("wrbd", [128, 128])
    wibd = dt_in("wibd", [128, 128])
    s5par = dt_in("s5par", [128, 12])
    bfull = dt_in("bfull", [128, 4 * 2 * 32])
    cT = dt_in("cT", [128, 4 * 2 * 64])
    h_o = dt_out("h", [128, TT])
    y_o = dt_out("ys5", [128, TT])
    with ExitStack() as st:
        P = Prog(nc, st)
        lr = P.sbuf("lr", [128, 9]); dlr = Dep("lr")
        wr = P.sbuf("wr", [128, 128]); dwr = Dep("wr")
        wi = P.sbuf("wi", [128, 128]); dwi = Dep("wi")
        sp = P.sbuf("sp", [128, 12]); dsp = Dep("sp")
        bf = P.sbuf("bf", [128, 4, 2, 32]); dbf = Dep("bf")
        ct = P.sbuf("ct", [128, 4, 2, 64]); dct = Dep("ct")
        P.dma('sp', lr[:], lru, writes=[dlr])
        P.dma('sp', wr[:], wrbd, writes=[dwr])
        P.dma('sp', wi[:], wibd, writes=[dwi])
        P.dma('sp', sp[:], s5par, writes=[dsp])
        P.dma('sp', bf[:], bfull.rearrange("p (a b c) -> p a b c", a=4, b=2), writes=[dbf])
        P.dma('sp', ct[:], cT.rearrange("p (a b c) -> p a b c", a=4, b=2), writes=[dct])
        P.op('dve', lambda e: e.tensor_scalar(ct[:, :, 1, :], ct[:, :, 1, :], -1.0, None, ALU.mult), reads=[dct], writes=[dct])
        c8 = P.sbuf("c8", [128, 1]); dc8 = Dep("c8")
        P.op('act', lambda e: e.activation(c8[:], lr[:, 8:9], AF.Exp, scale=-1.0), reads=[dlr], writes=[dc8])
        P.op('act', lambda e: e.activation(c8[:], c8[:], AF.Ln, bias=1.0), reads=[dc8], writes=[dc8])
        P.op('dve', lambda e: e.tensor_scalar(c8[:], c8[:], -8.0, None, ALU.mult), reads=[dc8], writes=[dc8])
        idf = P.sbuf("idf", [128, 128]); did = Dep("id")
        ii = P.sbuf("ii", [128, 128], I32); dii = Dep("ii")
        cf = P.sbuf("cf", [128, 128]); dcf = Dep("cf")
        pf = P.sbuf("pf", [128, 128]); dpf = Dep("pf")
        P.op('pool', lambda e: e.iota(ii[:], pattern=[[1, 128]], base=0, channel_multiplier=0), writes=[dii])
        P.op('dve', lambda e: e.tensor_copy(cf[:], ii[:]), reads=[dii], writes=[dcf])
        P.op('pool', lambda e: e.iota(ii[:], pattern=[[0, 128]], base=0, channel_multiplier=1), reads=[dcf], writes=[dii])
        P.op('dve', lambda e: e.tensor_copy(pf[:], ii[:]), reads=[dii], writes=[dpf])
        P.op('dve', lambda e: e.tensor_tensor(idf[:], cf[:], pf[:], ALU.is_equal), reads=[dcf, dpf], writes=[did])
        tau = P.sbuf("tau", [128, 512]); dtau = Dep("tau")
        taui = P.sbuf("taui", [128, 512], I32); dtaui = Dep("taui")
        P.op('pool', lambda e: e.iota(taui[:], pattern=[[1, 512]], base=1, channel_multiplier=0), writes=[dtaui])
        P.op('dve', lambda e: e.tensor_copy(tau[:], taui[:]), reads=[dtaui], writes=[dtau])
        col = P.sbuf("col", [128, 4, 16]); dcol = Dep("col")
        cosT = P.sbuf("cosT", [128, 4, 512]); sinT = P.sbuf("sinT", [128, 4, 512])
        dcos = [Dep(f"cos{i}") for i in range(4)]; dsin = [Dep(f"sin{i}") for i in range(4)]
        angt = P.sbuf("angt", [128, 512]); dangt = Dep("angt")
        tmp = P.sbuf("stmp", [128, 512]); tmpi = P.sbuf("stmpi", [128, 512], I32); dtmp = Dep("stmp"); dtmpi = Dep("stmpi")
        bbp = P.sbuf("bbp", [128, 2, 128]); dbbp = Dep("bbp")
        bbT = P.sbuf("bbT", [128, 2, 128]); dbbT = Dep("bbT")
        bbT3 = P.sbuf("bbT3", [128, 2, 128]); dbbT3 = Dep("bbT3")
        tb = P.sbuf("tb", [128, 32]); dtb = Dep("tb")
        ci = P.sbuf("ci", [128, 4], I32); dci = Dep("ci")
        C = lambda ti, k: col[:, ti, k:k + 1]
        for ti in range(4):
            are, aim, ldt = sp[:, ti * 3:ti * 3 + 1], sp[:, ti * 3 + 1:ti * 3 + 2], sp[:, ti * 3 + 2:ti * 3 + 3]
            P.op('act', lambda e, ti=ti, ldt=ldt: e.activation(C(ti, 0), ldt, AF.Exp), reads=[dsp], writes=[dcol])
            P.op('dve', lambda e, ti=ti, are=are: e.tensor_tensor(C(ti, 1), are, C(ti, 0), ALU.mult), reads=[dsp, dcol], writes=[dcol])
            P.op('dve', lambda e, ti=ti, aim=aim: e.tensor_tensor(C(ti, 2), aim, C(ti, 0), ALU.mult), reads=[dsp, dcol], writes=[dcol])
            P.op('act', lambda e, ti=ti: e.activation(C(ti, 3), C(ti, 1), AF.Exp), reads=[dcol], writes=[dcol])
            P.op('dve', lambda e, ti=ti: e.tensor_scalar(C(ti, 5), C(ti, 2), 1.0 / (2 * np.pi), None, ALU.mult), reads=[dcol], writes=[dcol])
            P.op('dve', lambda e, ti=ti: e.tensor_copy(ci[:, ti:ti + 1], C(ti, 5)), reads=[dcol], writes=[dci])
            P.op('dve', lambda e, ti=ti: e.tensor_copy(C(ti, 5), ci[:, ti:ti + 1]), reads=[dci], writes=[dcol])
            P.op('dve', lambda e, ti=ti: e.scalar_tensor_tensor(C(ti, 4), C(ti, 5), -TWO_PI_HI, C(ti, 2), ALU.mult, ALU.add), reads=[dcol], writes=[dcol])
            P.op('dve', lambda e, ti=ti: e.scalar_tensor_tensor(C(ti, 4), C(ti, 5), -TWO_PI_LO, C(ti, 4), ALU.mult, ALU.add), reads=[dcol], writes=[dcol])
            P.op('dve', lambda e, ti=ti: e.tensor_scalar(angt[:], tau[:], C(ti, 4), None, ALU.mult), reads=[dtau, dcol], writes=[dangt])
            sin_reduced(P, sinT[:, ti, :], angt[:], None, [dangt], dsin[ti], tmp[:], tmpi[:], dtmp, dtmpi, 0.0)
            sin_reduced(P, cosT[:, ti, :], angt[:], None, [dangt], dcos[ti], tmp[:], tmpi[:], dtmp, dtmpi, float(np.pi / 2))
            P.op('dve', lambda e, ti=ti: e.tensor_scalar(C(ti, 6), cosT[:, ti, 0:1], C(ti, 3), -1.0, ALU.mult, ALU.add), reads=[dcos[ti], dcol], writes=[dcol])
            P.op('dve', lambda e, ti=ti: e.tensor_tensor(C(ti, 7), sinT[:, ti, 0:1], C(ti, 3), ALU.mult), reads=[dsin[ti], dcol], writes=[dcol])
            P.op('dve', lambda e, ti=ti, are=are: e.tensor_tensor(C(ti, 8), are, are, ALU.mult), reads=[dsp], writes=[dcol])
            P.op('dve', lambda e, ti=ti, aim=aim: e.scalar_tensor_tensor(C(ti, 8), aim, aim, C(ti, 8), ALU.mult, ALU.add), reads=[dsp, dcol], writes=[dcol])
            P.op('dve', lambda e, ti=ti: e.reciprocal(C(ti, 8), C(ti, 8)), reads=[dcol], writes=[dcol])
            P.op('dve', lambda e, ti=ti, are=are: e.tensor_tensor(C(ti, 11), C(ti, 6), are, ALU.mult), reads=[dsp, dcol], writes=[dcol])
            P.op('dve', lambda e, ti=ti, aim=aim: e.scalar_tensor_tensor(C(ti, 11), C(ti, 7), aim, C(ti, 11), ALU.mult, ALU.add), reads=[dsp, dcol], writes=[dcol])
            P.op('dve', lambda e, ti=ti: e.tensor_tensor(C(ti, 9), C(ti, 11), C(ti, 8), ALU.mult), reads=[dcol], writes=[dcol])
            P.op('dve', lambda e, ti=ti, aim=aim: e.tensor_tensor(C(ti, 12), C(ti, 6), aim, ALU.mult), reads=[dsp, dcol], writes=[dcol])
            P.op('dve', lambda e, ti=ti, are=are: e.scalar_tensor_tensor(C(ti, 12), C(ti, 7), are, C(ti, 12), ALU.mult, ALU.subtract), reads=[dsp, dcol], writes=[dcol])
            P.op('dve', lambda e, ti=ti: e.tensor_tensor(C(ti, 10), C(ti, 12), C(ti, 8), ALU.mult), reads=[dcol], writes=[dcol])
            if ti == 0:
                P.op('pool', lambda e: e.memset(bbp[:], 0.0), writes=[dbbp])
            P.op('dve', lambda e, ti=ti: e.tensor_scalar(tb[:], bf[:, ti, 1, :], C(ti, 10), None, ALU.mult), reads=[dbf, dcol], writes=[dtb])
            P.op('dve', lambda e, ti=ti: e.scalar_tensor_tensor(bbp[:, 0, ti * 32:(ti + 1) * 32], bf[:, ti, 0, :], C(ti, 9), tb[:], ALU.mult, ALU.subtract),
                 reads=[dbf, dcol, dtb], writes=[dbbp])
            P.op('dve', lambda e, ti=ti: e.tensor_scalar(tb[:], bf[:, ti, 0, :], C(ti, 10), None, ALU.mult), reads=[dbf, dcol, dbbp], writes=[dtb])
            P.op('dve', lambda e, ti=ti: e.scalar_tensor_tensor(bbp[:, 1, ti * 32:(ti + 1) * 32], bf[:, ti, 1, :], C(ti, 9), tb[:], ALU.mult, ALU.add),
                 reads=[dbf, dcol, dtb], writes=[dbbp])
        NOTE = """bbp[:, c, ti*32 + gc] holds tile ti's bb at that tile's state partitions; tiles differ in partition meaning,
        but transposing the [128,128] pad gives rows ti*32+gc = lhsT rows for tile ti (cols = that tile's states)."""
        ps = Ring(P, "ps", [128, 512], F32, 2, psum=True)
        psy = Ring(P, "psy", [128, 512], F32, 2, psum=True)
        psb = Ring(P, "psb", [128, 512], F32, 4, psum=True)
        for c in range(2):
            pt, dpt = ps.next()
            P.op('pe', lambda e, pt=pt, c=c: e.transpose(pt[:, 0:128], bbp[:, c, :], idf[:]), reads=[dbbp, did], writes=[dpt])
            P.op('act', lambda e, pt=pt, c=c: e.copy(bbT[:, c, :], pt[:, 0:128]), reads=[dpt], writes=[dbbT])
            P.op('act', lambda e, pt=pt, c=c: e.copy(bbT3[:, c, :], pt[:, 0:128]), reads=[dpt], writes=[dbbT3])
        P.op('dve', lambda e: e.memset(bbT3[64:96, :, :], 0.0), reads=[dbbT3], writes=[dbbT3])
        xr = Ring(P, "xr", [128, 516], F32, 3)
        ur = Ring(P, "ur", [128, 512], F32, 3)
        W = lambda nm, n=2: Ring(P, nm, [128, 512], F32, n)
        xc_r, r_r, i_r, a_r, m_r, b_r, h_r = W("xc"), W("rr"), W("ir"), W("ar"), W("mr"), W("br"), W("hr")
        t_r = [W(f"t{k}") for k in range(4)]
        bpr_r, bpi_r, gr_r, gi_r = W("bpr"), W("bpi"), W("gr"), W("gi")
        hsr = [W(f"hsr{ti}") for ti in range(4)]
        hsi = [W(f"hsi{ti}") for ti in range(4)]
        yev = Ring(P, "yev", [64, 512], F32, 2)
        prev_h = None
        prev_s = [None] * 4
        for (t0, n) in SBLOCKS:
            off = 2 if t0 < CTX else 6
            xt, dx = xr.next()
            P.dma('sp', xt[:, 0:n + 4], xa2[:, t0 + off - 2:t0 + off + n + 2], writes=[dx])
            ut, du = ur.next()
            P.dma('act', ut[:, 0:n], ub2[:, t0:t0 + n], writes=[du])
            xc, dxc = xc_r.next()
            P.op('dve', lambda e, xc=xc, xt=xt, n=n: e.tensor_scalar(xc[:, 0:n], xt[:, 0:n], lr[:, 0:1], lr[:, 5:6], ALU.mult, ALU.add),
                 reads=[dx, dlr], writes=[dxc])
            for oi in range(1, 5):
                P.op('dve', lambda e, xc=xc, xt=xt, n=n, oi=oi: e.scalar_tensor_tensor(
                    xc[:, 0:n], xt[:, oi:oi + n], lr[:, oi:oi + 1], xc[:, 0:n], ALU.mult, ALU.add), reads=[dx, dlr, dxc], writes=[dxc])
            pr, dpr = ps.next()
            pi_, dpi = ps.next()
            P.op('pe', lambda e, pr=pr, xc=xc, n=n: e.matmul(pr[:, 0:n], wr[:], xc[:, 0:n], start=True, stop=True), reads=[dwr, dxc], writes=[dpr])
            P.op('pe', lambda e, pi_=pi_, xc=xc, n=n: e.matmul(pi_[:, 0:n], wi[:], xc[:, 0:n], start=True, stop=True), reads=[dwi, dxc], writes=[dpi])
            rt, drt = r_r.next(); it, dit = i_r.next(); at, dat = a_r.next(); mt, dmt = m_r.next(); bt, dbt = b_r.next(); ht, dht = h_r.next()
            P.op('act', lambda e, rt=rt, pr=pr, n=n: e.activation(rt[:, 0:n], pr[:, 0:n], AF.Sigmoid, bias=lr[:, 6:7]), reads=[dpr, dlr], writes=[drt])
            P.op('act', lambda e, it=it, pi_=pi_, n=n: e.activation(it[:, 0:n], pi_[:, 0:n], AF.Sigmoid, bias=lr[:, 7:8]), reads=[dpi, dlr], writes=[dit])
            P.op('act', lambda e, at=at, rt=rt, n=n: e.activation(at[:, 0:n], rt[:, 0:n], AF.Exp, scale=c8[:, 0:1]), reads=[drt, dc8], writes=[dat])
            P.op('pool', lambda e, mt=mt, at=at, n=n: e.tensor_tensor(mt[:, 0:n], at[:, 0:n], at[:, 0:n], ALU.mult), reads=[dat], writes=[dmt])
            P.op('act', lambda e, mt=mt, n=n: e.activation(mt[:, 0:n], mt[:, 0:n], AF.Sqrt, scale=-1.0, bias=1.0), reads=[dmt], writes=[dmt])
            P.op('pool', lambda e, bt=bt, it=it, xc=xc, n=n: e.tensor_tensor(bt[:, 0:n], it[:, 0:n], xc[:, 0:n], ALU.mult), reads=[dit, dxc], writes=[dbt])
            P.op('pool', lambda e, bt=bt, mt=mt, n=n: e.tensor_tensor(bt[:, 0:n], bt[:, 0:n], mt[:, 0:n], ALU.mult), reads=[dbt, dmt], writes=[dbt])
            if prev_h is None:
                P.op('dve', lambda e, ht=ht, at=at, bt=bt, n=n: e.tensor_tensor_scan(ht[:, 0:n], at[:, 0:n], bt[:, 0:n], 0.0, ALU.mult, ALU.add),
                     reads=[dat, dbt], writes=[dht])
            else:
                ph, dph, pn = prev_h
                P.op('dve', lambda e, ht=ht, at=at, bt=bt, n=n, ph=ph, pn=pn: e.tensor_tensor_scan(
                    ht[:, 0:n], at[:, 0:n], bt[:, 0:n], ph[:, pn - 1:pn], ALU.mult, ALU.add), reads=[dat, dbt, dph], writes=[dht])
            prev_h = (ht, dht, n)
            P.dma('sp', h_o[:, t0:t0 + n], ht[:, 0:n], reads=[dht])
            py = [psy.next(), psy.next()]
            for ti in range(4):
                d = ti // 2
                pbr, dpbr = psb.next()
                pbi, dpbi = psb.next()
                rows = slice(ti * 32, (ti + 1) * 32) if ti < 3 else slice(64, 128)
                bsrc, dbsrc = (bbT, dbbT) if ti < 3 else (bbT3, dbbT3)
                P.op('pe', lambda e, pbr=pbr, ut=ut, rows=rows, n=n, bsrc=bsrc: e.matmul(pbr[:, 0:n], bsrc[rows, 0, :], ut[rows, 0:n], start=True, stop=True),
                     reads=[dbsrc, du], writes=[dpbr])
                P.op('pe', lambda e, pbi=pbi, ut=ut, rows=rows, n=n, bsrc=bsrc: e.matmul(pbi[:, 0:n], bsrc[rows, 1, :], ut[rows, 0:n], start=True, stop=True),
                     reads=[dbsrc, du], writes=[dpbi])
                cs, sn = cosT[:, ti, 0:n], sinT[:, ti, 0:n]
                (t1, d1), (t2, d2), (t3, d3), (t4, d4) = [r_.next() for r_ in t_r]
                P.op('dve', lambda e, t1=t1, pbr=pbr, cs=cs, n=n: e.tensor_tensor(t1[:, 0:n], pbr[:, 0:n], cs, ALU.mult), reads=[dpbr, dcos[ti]], writes=[d1])
                P.op('dve', lambda e, t2=t2, pbi=pbi, sn=sn, n=n: e.tensor_tensor(t2[:, 0:n], pbi[:, 0:n], sn, ALU.mult), reads=[dpbi, dsin[ti]], writes=[d2])
                P.op('dve', lambda e, t3=t3, pbi=pbi, cs=cs, n=n: e.tensor_tensor(t3[:, 0:n], pbi[:, 0:n], cs, ALU.mult), reads=[dpbi, dcos[ti]], writes=[d3])
                P.op('dve', lambda e, t4=t4, pbr=pbr, sn=sn, n=n: e.tensor_tensor(t4[:, 0:n], pbr[:, 0:n], sn, ALU.mult), reads=[dpbr, dsin[ti]], writes=[d4])
                bpr, dbpr = bpr_r.next(); bpi, dbpi = bpi_r.next()
                P.op('pool', lambda e, bpr=bpr, t1=t1, t2=t2, n=n: e.tensor_tensor(bpr[:, 0:n], t1[:, 0:n], t2[:, 0:n], ALU.add), reads=[d1, d2], writes=[dbpr])
                P.op('pool', lambda e, bpi=bpi, t3=t3, t4=t4, n=n: e.tensor_tensor(bpi[:, 0:n], t3[:, 0:n], t4[:, 0:n], ALU.subtract), reads=[d3, d4], writes=[dbpi])
                gr, dgr = gr_r.next(); gi, dgi = gi_r.next()
                rho = col[:, ti, 3:4].to_broadcast([128, n])
                if prev_s[ti] is None:
                    P.op('dve', lambda e, gr=gr, bpr=bpr, rho=rho, n=n: e.tensor_tensor_scan(gr[:, 0:n], rho, bpr[:, 0:n], 0.0, ALU.mult, ALU.add),
                         reads=[dcol, dbpr], writes=[dgr])
                    P.op('dve', lambda e, gi=gi, bpi=bpi, rho=rho, n=n: e.tensor_tensor_scan(gi[:, 0:n], rho, bpi[:, 0:n], 0.0, ALU.mult, ALU.add),
                         reads=[dcol, dbpi], writes=[dgi])
                else:
                    phr, dphr, phi, dphi, pn = prev_s[ti]
                    P.op('dve', lambda e, gr=gr, bpr=bpr, rho=rho, n=n, phr=phr, pn=pn: e.tensor_tensor_scan(
                        gr[:, 0:n], rho, bpr[:, 0:n], phr[:, pn - 1:pn], ALU.mult, ALU.add), reads=[dcol, dbpr, dphr], writes=[dgr])
                    P.op('dve', lambda e, gi=gi, bpi=bpi, rho=rho, n=n, phi=phi, pn=pn: e.tensor_tensor_scan(
                        gi[:, 0:n], rho, bpi[:, 0:n], phi[:, pn - 1:pn], ALU.mult, ALU.add), reads=[dcol, dbpi, dphi], writes=[dgi])
                (u1, e1), (u2, e2), (u3, e3), (u4, e4) = [r_.next() for r_ in t_r]
                P.op('pool', lambda e, u1=u1, gr=gr, cs=cs, n=n: e.tensor_tensor(u1[:, 0:n], gr[:, 0:n], cs, ALU.mult), reads=[dgr, dcos[ti]], writes=[e1])
                P.op('pool', lambda e, u2=u2, gi=gi, sn=sn, n=n: e.tensor_tensor(u2[:, 0:n], gi[:, 0:n], sn, ALU.mult), reads=[dgi, dsin[ti]], writes=[e2])
                P.op('pool', lambda e, u3=u3, gr=gr, sn=sn, n=n: e.tensor_tensor(u3[:, 0:n], gr[:, 0:n], sn, ALU.mult), reads=[dgr, dsin[ti]], writes=[e3])
                P.op('pool', lambda e, u4=u4, gi=gi, cs=cs, n=n: e.tensor_tensor(u4[:, 0:n], gi[:, 0:n], cs, ALU.mult), reads=[dgi, dcos[ti]], writes=[e4])
                hr_, dhr = hsr[ti].next(); hi_, dhi = hsi[ti].next()
                P.op('dve', lambda e, hr_=hr_, u1=u1, u2=u2, n=n: e.tensor_tensor(hr_[:, 0:n], u1[:, 0:n], u2[:, 0:n], ALU.subtract), reads=[e1, e2], writes=[dhr])
                P.op('dve', lambda e, hi_=hi_, u3=u3, u4=u4, n=n: e.tensor_tensor(hi_[:, 0:n], u3[:, 0:n], u4[:, 0:n], ALU.add), reads=[e3, e4], writes=[dhi])
                prev_s[ti] = (hr_, dhr, hi_, dhi, n)
                yp, dyp = py[d]
                first = (ti % 2 == 0)
                P.op('pe', lambda e, yp=yp, hr_=hr_, ti=ti, n=n, first=first: e.matmul(yp[0:64, 0:n], ct[:, ti, 0, :], hr_[:, 0:n], start=first, stop=False),
                     reads=[dct, dhr], writes=[dyp], pe_acc=True)
                P.op('pe', lambda e, yp=yp, hi_=hi_, ti=ti, n=n, first=first: e.matmul(yp[0:64, 0:n], ct[:, ti, 1, :], hi_[:, 0:n], start=False, stop=(not first)),
                     reads=[dct, dhi], writes=[dyp], pe_acc=True)
                if not first:
                    yt, dyt = yev.next()
                    P.op('act', lambda e, yt=yt, yp=yp, n=n: e.copy(yt[:, 0:n], yp[0:64, 0:n]), reads=[dyp], writes=[dyt])
                    P.dma('act', y_o[d * 64:(d + 1) * 64, t0:t0 + n], yt[:, 0:n], reads=[dyt])
        P.finish()
    return nc


def host_B(inp, l, xguT_full):
    rev = np.concatenate([np.arange(CTX)[::-1], CTX + np.arange(SEQ)[::-1]])
    maps = []
    cw = inp['conv_w'][l]
    for j in range(NCORE):
        ch = slice(64 * j, 64 * j + 64)
        xa = xguT_full[0:512][ch]
        ub = xguT_full[1024:1536][ch]
        xa2 = np.zeros((128, XPAD), np.float32)
        for d, src in enumerate((xa, xa[:, rev])):
            xa2[d * 64:(d + 1) * 64, 2:2 + CTX] = src[:, :CTX]
            xa2[d * 64:(d + 1) * 64, 6 + CTX:6 + TT] = src[:, CTX:]
        ub2 = np.concatenate([ub, ub[:, rev]], 0)
        lru = np.zeros((128, 9), np.float32)
        lru[0:64, 0:4] = cw[:, ch].T
        lru[64:128, 1:5] = cw[::-1, ch].T
        lru[:, 5] = np.tile(inp['conv_b'][l][ch], 2)
        for d in range(2):
            lru[d * 64:(d + 1) * 64, 6] = inp['lru_br'][l, d, ch]
            lru[d * 64:(d + 1) * 64, 7] = inp['lru_bi'][l, d, ch]
            lru[d * 64:(d + 1) * 64, 8] = inp['lru_lam'][l, d, ch]
        wrbd = np.zeros((128, 128), np.float32); wibd = np.zeros((128, 128), np.float32)
        for d in range(2):
            wrbd[d * 64:(d + 1) * 64, d * 64:(d + 1) * 64] = inp['lru_wr'][l, d, j]
            wibd[d * 64:(d + 1) * 64, d * 64:(d + 1) * 64] = inp['lru_wi'][l, d, j]
        s5par = np.zeros((128, 12), np.float32)
        bfull = np.zeros((128, 4, 2, 32), np.float32)
        cT = np.zeros((128, 4, 2, 64), np.float32)
        for ti in range(4):
            d, pr = ti // 2, ti % 2
            for gl in range(2):
                g = 4 * j + pr * 2 + gl
                rows = slice(gl * 64, gl * 64 + 64)
                s5par[rows, ti * 3 + 0] = inp['s5_a_re'][l, d, g]
                s5par[rows, ti * 3 + 1] = inp['s5_a_im'][l, d, g]
                s5par[rows, ti * 3 + 2] = inp['s5_log_dt'][l, d, g]
                bfull[rows, ti, 0, gl * 16:(gl + 1) * 16] = inp['s5_b_re'][l, d, g]
                bfull[rows, ti, 1, gl * 16:(gl + 1) * 16] = inp['s5_b_im'][l, d, g]
                cc = pr * 32 + gl * 16
                cT[rows, ti, 0, cc:cc + 16] = inp['s5_c_re'][l, d, g].T
                cT[rows, ti, 1, cc:cc + 16] = inp['s5_c_im'][l, d, g].T
        maps.append(dict(xa2=xa2, ub2=np.ascontiguousarray(ub2), lru=lru, wrbd=wrbd, wibd=wibd, s5par=s5par,
                         bfull=bfull.reshape(128, -1), cT=cT.reshape(128, -1)))
    return maps


NKT = TT // 128


def build_C1():
    nc = bass.Bass("TRN2", target_bir_lowering=False)
    dt_in = lambda n, s, d=F32: nc.dram_tensor(n, s, d, kind="ExternalInput").ap()
    dt_out = lambda n, s, d=F32: nc.dram_tensor(n, s, d, kind="ExternalOutput").ap()
    qT = dt_in("qT", [512, TOK], BF16)
    kT = dt_in("kT", [512, TT], BF16)
    vv = dt_in("v", [TT, 512], BF16)
    dlam = dt_in("dlam", [1, 256])
    aux = dt_in("aux", [128, 3])
    yc_o = dt_out("ycT", [512, TOK], BF16)
    with ExitStack() as st:
        P = Prog(nc, st)
        q = P.sbuf("q", [128, 4, TOK], BF16); dq = Dep("q")
        P.dma('sp', q[:], qT.rearrange("(h p) t -> p h t", p=128), writes=[dq])
        ax = P.sbuf("ax", [128, 3]); dax = Dep("ax")
        P.dma('sp', ax[:], aux, writes=[dax])
        dl = P.sbuf("dl", [128, 256]); ddl = Dep("dl")
        P.dma('sp', dl[:], dlam.partition_broadcast(128), writes=[ddl])
        sc = P.sbuf("sc", [128, 8]); dsc = Dep("sc")
        pr = P.sbuf("pr", [128, 128]); dpr = Dep("pr")
        P.op('dve', lambda e: e.tensor_tensor(pr[:, 0:64], dl[:, 0:64], dl[:, 64:128], ALU.mult), reads=[ddl], writes=[dpr])
        P.op('dve', lambda e: e.tensor_tensor(pr[:, 64:128], dl[:, 128:192], dl[:, 192:256], ALU.mult), reads=[ddl, dpr], writes=[dpr])
        P.op('dve', lambda e: e.reduce_sum(sc[:, 0:1], pr[:, 0:64], AX.X), reads=[dpr], writes=[dsc])
        P.op('dve', lambda e: e.reduce_sum(sc[:, 1:2], pr[:, 64:128], AX.X), reads=[dpr, dsc], writes=[dsc])
        P.op('act', lambda e: e.activation(sc[:, 2:4], sc[:, 0:2], AF.Exp), reads=[dsc], writes=[dsc])
        P.op('dve', lambda e: e.tensor_tensor(sc[:, 4:5], sc[:, 3:4], sc[:, 2:3], ALU.subtract), reads=[dsc], writes=[dsc])
        P.op('dve', lambda e: e.tensor_tensor(sc[:, 4:5], sc[:, 4:5], ax[:, 1:2], ALU.subtract), reads=[dsc, dax], writes=[dsc])
        P.op('dve', lambda e: e.tensor_tensor(sc[:, 5:6], ax[:, 0:1], ax[:, 2:3], ALU.mult), reads=[dsc, dax], writes=[dsc])
        ones_b = P.sbuf("ones_b", [128, 128], BF16); dob = Dep("ones_b")
        ones_f = P.sbuf("ones_f", [128, 128]); dof = Dep("ones_f")
        P.op('pool', lambda e: e.memset(ones_b[:], 1.0), writes=[dob])
        P.op('pool', lambda e: e.memset(ones_f[:], 1.0 / 128.0), writes=[dof])
        kh = P.sbuf("kh", [128, TT], BF16); dkh = Dep("kh")
        vh = P.sbuf("vh", [128, NKT, 128], BF16); dvh = Dep("vh")
        pss = Ring(P, "pss", [128, 512], F32, 4, psum=True)
        pso = [Ring(P, f"pso{m}", [128, 512], F32, 1, psum=True) for m in range(2)]
        psl = [Ring(P, f"psl{m}", [128, 512], F32, 1, psum=True) for m in range(2)]
        pt_r = Ring(P, "pt", [128, 512], BF16, 6)
        W = lambda nm, n=2, d=F32: Ring(P, nm, [128, 512], d, n)
        r_r, o0_r, o1_r, sq_r, rs_r, yc_r = W("rr"), W("o0"), W("o1"), W("sq"), W("rs"), W("yc", 2, BF16)
        kT_v = kT.rearrange("(h p) t -> h p t", p=128)
        v_v = vv.rearrange("(kt p) (h e) -> h p kt e", p=128, e=128)
        for h in range(4):
            for part in range(4):
                c0, c1 = part * (TT // 4), (part + 1) * (TT // 4)
                P.dma('sp', kh[:, c0:c1], kT_v[h, :, c0:c1], writes=[dkh], accum=(part > 0))
            for part in range(5):
                k0, k1 = part * 26, (part + 1) * 26
                P.dma('act', vh[:, k0:k1, :], v_v[h, :, k0:k1, :], writes=[dvh], accum=(part > 0))
            for (t0, n) in BLOCKS:
                nkt = 2 if t0 == 0 else NKT
                po = [pso[m].next() for m in range(2)]
                pl = [psl[m].next() for m in range(2)]
                for kt in range(nkt):
                    for m in range(2):
                        ps_, dps = pss.next()
                        rows = slice(m * 64, (m + 1) * 64)
                        P.op('pe', lambda e, ps_=ps_, rows=rows, kt=kt, h=h, t0=t0, n=n: e.matmul(
                            ps_[:, 0:n], kh[rows, kt * 128:(kt + 1) * 128], q[rows, h, t0:t0 + n], start=True, stop=True),
                            reads=[dkh, dq], writes=[dps])
                        pt, dpt = pt_r.next()
                        P.op('act', lambda e, pt=pt, ps_=ps_, n=n: e.activation(pt[:, 0:n], ps_[:, 0:n], AF.Exp), reads=[dps], writes=[dpt])
                        P.op('pe', lambda e, m=m, pt=pt, kt=kt, n=n, po=po, nkt=nkt: e.matmul(
                            po[m][0][:, 0:n], vh[:, kt, :], pt[:, 0:n], start=(kt == 0), stop=(kt == nkt - 1)),
                            reads=[dvh, dpt], writes=[po[m][1]], pe_acc=True)
                        P.op('pe', lambda e, m=m, pt=pt, kt=kt, n=n, pl=pl, nkt=nkt: e.matmul(
                            pl[m][0][:, 0:n], ones_b[:], pt[:, 0:n], start=(kt == 0), stop=(kt == nkt - 1)),
                            reads=[dob, dpt], writes=[pl[m][1]], pe_acc=True)
                os_ = []
                for m, orr in ((0, o0_r), (1, o1_r)):
                    rt, drt = r_r.next()
                    P.op('dve', lambda e, rt=rt, m=m, n=n, pl=pl: e.reciprocal(rt[:, 0:n], pl[m][0][:, 0:n]), reads=[pl[m][1]], writes=[drt])
                    ot, dot = orr.next()
                    P.op('dve', lambda e, ot=ot, rt=rt, m=m, n=n, po=po: e.tensor_tensor(ot[:, 0:n], po[m][0][:, 0:n], rt[:, 0:n], ALU.mult),
                         reads=[po[m][1], drt], writes=[dot])
                    os_.append((ot, dot))
                (o0, do0), (o1, do1) = os_
                P.op('dve', lambda e, o0=o0, o1=o1, n=n: e.scalar_tensor_tensor(o0[:, 0:n], o1[:, 0:n], sc[:, 4:5], o0[:, 0:n], ALU.mult, ALU.add),
                     reads=[do0, do1, dsc], writes=[do0])
                sq, dsq = sq_r.next()
                P.op('pool', lambda e, sq=sq, o0=o0, n=n: e.tensor_tensor(sq[:, 0:n], o0[:, 0:n], o0[:, 0:n], ALU.mult), reads=[do0], writes=[dsq])
                pm, dpm = pss.next()
                P.op('pe', lambda e, pm=pm, sq=sq, n=n: e.matmul(pm[:, 0:n], ones_f[:], sq[:, 0:n], start=True, stop=True), reads=[dof, dsq], writes=[dpm])
                rs, drs = rs_r.next()
                P.op('act', lambda e, rs=rs, pm=pm, n=n: e.activation(rs[:, 0:n], pm[:, 0:n], AF.Sqrt, bias=RMS_EPS), reads=[dpm], writes=[drs])
                P.op('dve', lambda e, rs=rs, n=n: e.reciprocal(rs[:, 0:n], rs[:, 0:n]), reads=[drs], writes=[drs])
                yc, dyc = yc_r.next()
                P.op('dve', lambda e, yc=yc, o0=o0, rs=rs, n=n: e.scalar_tensor_tensor(yc[:, 0:n], o0[:, 0:n], sc[:, 5:6], rs[:, 0:n], ALU.mult, ALU.mult),
                     reads=[do0, drs, dsc], writes=[dyc])
                P.dma('sp', yc_o[h * 128:(h + 1) * 128, t0:t0 + n], yc[:, 0:n], reads=[dyc])
        P.finish()
    return nc


def host_C1(inp, l, qk_full, v_full):
    lam_init = 0.8 - 0.6 * float(np.exp(-0.3 * l))
    aux = np.stack([inp['da_subln_g'][l], np.full(128, lam_init, np.float32), np.full(128, 1.0 - lam_init, np.float32)], -1).astype(np.float32)
    dlam = np.ascontiguousarray(inp['da_lam'][l].reshape(1, 256))
    kT = np.ascontiguousarray(qk_full[512:])
    maps = []
    for c in range(NCORE):
        cols = np.concatenate([np.arange(CTX), CTX + c * LAT_PC + np.arange(LAT_PC)])
        maps.append(dict(qT=np.ascontiguousarray(qk_full[:512][:, cols]), kT=kT, v=v_full, dlam=dlam, aux=aux))
    return maps


def ln_fm(P, X, dX, n, onesf, dof, gcol, bcol, dpar, out, dout, ps_r, tmp):
    pm, dpm = ps_r.next()
    for k in range(8):
        P.op('pe', lambda e, k=k, pm=pm: e.matmul(pm[:, 0:n], onesf[:], X[:, k, 0:n], start=(k == 0), stop=(k == 7)),
             reads=[dof, dX], writes=[dpm], pe_acc=True)
    pq, dpq = ps_r.next()
    for k in range(8):
        sq, dsq = tmp['sq'].next()
        P.op('act', lambda e, k=k, sq=sq: e.activation(sq[:, 0:n], X[:, k, 0:n], AF.Square), reads=[dX], writes=[dsq])
        P.op('pe', lambda e, k=k, pq=pq, sq=sq: e.matmul(pq[:, 0:n], onesf[:], sq[:, 0:n], start=(k == 0), stop=(k == 7)),
             reads=[dof, dsq], writes=[dpq], pe_acc=True)
    mean, dmean = tmp['mean'].next()
    var, dvar = tmp['var'].next()
    P.op('act', lambda e: e.copy(mean[:, 0:n], pm[:, 0:n]), reads=[dpm], writes=[dmean])
    P.op('dve', lambda e: e.tensor_tensor(var[:, 0:n], mean[:, 0:n], mean[:, 0:n], ALU.mult), reads=[dmean], writes=[dvar])
    P.op('dve', lambda e: e.tensor_tensor(var[:, 0:n], pq[:, 0:n], var[:, 0:n], ALU.subtract), reads=[dpq, dvar], writes=[dvar])
    P.op('act', lambda e: e.activation(var[:, 0:n], var[:, 0:n], AF.Sqrt, bias=LN_EPS), reads=[dvar], writes=[dvar])
    P.op('dve', lambda e: e.reciprocal(var[:, 0:n], var[:, 0:n]), reads=[dvar], writes=[dvar])
    for k in range(8):
        t, dt_ = tmp['t'].next()
        P.op('dve', lambda e, k=k, t=t: e.tensor_tensor(t[:, 0:n], X[:, k, 0:n], mean[:, 0:n], ALU.subtract), reads=[dX, dmean], writes=[dt_])
        P.op('pool', lambda e, t=t: e.tensor_tensor(t[:, 0:n], t[:, 0:n], var[:, 0:n], ALU.mult), reads=[dt_, dvar], writes=[dt_])
        P.op('act', lambda e, k=k, t=t: e.activation(out[:, k, 0:n], t[:, 0:n], AF.Identity, scale=gcol(k), bias=bcol(k)),
             reads=[dt_, dpar], writes=[dout], accum=(k > 0))


def gelu_fm(P, out, dout, x, dx, n, tmp):
    a, da = tmp['g1'].next()
    b, db = tmp['g2'].next()
    P.op('act', lambda e: e.activation(a[:, 0:n], x, AF.Square), reads=[dx], writes=[da])
    P.op('dve', lambda e: e.tensor_scalar(a[:, 0:n], a[:, 0:n], 0.044715, 1.0, ALU.mult, ALU.add), reads=[da], writes=[da])
    P.op('dve', lambda e: e.tensor_tensor(a[:, 0:n], a[:, 0:n], x, ALU.mult), reads=[da, dx], writes=[da])
    P.op('act', lambda e: e.activation(b[:, 0:n], a[:, 0:n], AF.Sigmoid, scale=1.5957691216057308), reads=[da], writes=[db])
    P.op('pool', lambda e: e.tensor_tensor(out, b[:, 0:n], x, ALU.mult), reads=[db, dx], writes=[dout])


C2BLK = [(i * 256, 256) for i in range(TOK // 256)]


def build_C2():
    nc = bass.Bass("TRN2", target_bir_lowering=False)
    dt_in = lambda n, s, d=F32: nc.dram_tensor(n, s, d, kind="ExternalInput").ap()
    dt_out = lambda n, s, d=F32: nc.dram_tensor(n, s, d, kind="ExternalOutput").ap()
    xT = dt_in("xT", [1024, TOK])
    modT_i = dt_in("modT", [128, 96])
    br6 = dt_in("br6", [6, 512, TOK])
    ycT = dt_in("ycT", [512, TOK], BF16)
    w_gate = dt_in("w_gate", [1024, 3072])
    w_proj = dt_in("w_proj", [3, 512, 1024])
    w_out = dt_in("w_out", [1024, 1024])
    w_glu = dt_in("w_glu", [512, 1024])
    cols_i = dt_in("cols", [128, 64])
    w_rt = dt_in("w_rt", [1024, 32])
    b_rt = dt_in("b_rt", [1, 32])
    acc_o = dt_out("accT", [1024, TOK])
    v_o = dt_out("vT", [1024, TOK], BF16)
    g_o = dt_out("gateT", [32, TOK])
    with ExitStack() as st:
        P = Prog(nc, st)
        modT = P.sbuf("modTs2", [128, 96]); dmod = Dep("mod")
        P.dma('sp', modT[:], modT_i, writes=[dmod])
        cl = P.sbuf("cl", [128, 64]); dcl = Dep("cl")
        P.dma('sp', cl[:], cols_i, writes=[dcl])
        brt = P.sbuf("brt", [128, 32]); dbrt = Dep("brt")
        P.dma('sp', brt[:], b_rt.partition_broadcast(128), writes=[dbrt])
        wrt = P.sbuf("wrt", [128, 8, 32]); dwrt = Dep("wrt")
        P.dma('sp', wrt[:], w_rt.rearrange("(k p) e -> p k e", p=128), writes=[dwrt])
        scp = P.sbuf("scp", [128, 32]); dscp = Dep("scp")
        P.op('dve', lambda e: e.tensor_scalar(scp[:, 0:16], modT[:, 16:32], 1.0, None, ALU.add), reads=[dmod], writes=[dscp])
        P.op('dve', lambda e: e.tensor_scalar(scp[:, 16:32], modT[:, 64:80], 1.0, None, ALU.add), reads=[dmod, dscp], writes=[dscp])
        onesf = P.sbuf("onesf", [128, 128]); dof = Dep("onesf")
        P.op('pool', lambda e: e.memset(onesf[:], 1.0 / 1024.0), writes=[dof])
        idf = P.sbuf("idf", [128, 128]); did = Dep("id")
        ii = P.sbuf("ii", [128, 128], I32); dii = Dep("ii")
        cf = P.sbuf("cf", [128, 128]); dcf = Dep("cf")
        pf = P.sbuf("pf", [128, 128]); dpf = Dep("pf")
        P.op('pool', lambda e: e.iota(ii[:], pattern=[[1, 128]], base=0, channel_multiplier=0), writes=[dii])
        P.op('dve', lambda e: e.tensor_copy(cf[:], ii[:]), reads=[dii], writes=[dcf])
        P.op('pool', lambda e: e.iota(ii[:], pattern=[[0, 128]], base=0, channel_multiplier=1), reads=[dcf], writes=[dii])
        P.op('dve', lambda e: e.tensor_copy(pf[:], ii[:]), reads=[dii], writes=[dpf])
        P.op('dve', lambda e: e.tensor_tensor(idf[:], cf[:], pf[:], ALU.is_equal), reads=[dcf, dpf], writes=[did])
        wg = P.sbuf("wg", [128, 8, 3072], BF16); dwg = [Dep(f"wg{k}") for k in range(8)]
        wp = P.sbuf("wp", [128, 3, 4, 1024], BF16); dwp = Dep("wp")
        wo = P.sbuf("wo", [128, 8, 1024], BF16); dwo = Dep("wo")
        wl = P.sbuf("wl", [128, 4, 1024], BF16); dwl = Dep("wl")
        w_gate_v = w_gate.rearrange("(k p) j -> p k j", p=128)
        for k in range(8):
            for hh in range(2):
                P.dma('pool', wg[:, k, hh * 1536:(hh + 1) * 1536], w_gate_v[:, k, hh * 1536:(hh + 1) * 1536], writes=[dwg[k]], accum=True)
        for i in range(3):
            for c in range(4):
                P.dma('pool', wp[:, i, c, :], w_proj[i, c * 128:(c + 1) * 128, :], writes=[dwp], accum=True)
        for k in range(8):
            P.dma('pool', wo[:, k, :], w_out[k * 128:(k + 1) * 128, :], writes=[dwo], accum=True)
        for c in range(4):
            P.dma('pool', wl[:, c, :], w_glu[c * 128:(c + 1) * 128, :], writes=[dwl], accum=True)
        NB = 256
        R = lambda nm, n=2, d=F32, w=NB: Ring(P, nm, [128, w], d, n)
        xb_r = Ring(P, "xb", [128, 8, NB], F32, 1)
        vf_r = Ring(P, "vf", [128, 8, NB], F32, 1)
        ub_r = Ring(P, "ubf", [128, 8, NB], BF16, 1)
        vb_r = Ring(P, "vb", [128, 8, NB], BF16, 1)
        z_r = Ring(P, "z", [128, 8, NB], BF16, 1)
        y_r = [Ring(P, f"y{i}", [128, 4, NB], BF16, 2) for i in range(3)]
        gy_r = Ring(P, "gy", [128, 4, NB], BF16, 2)
        in_r = [R(f"in{i}", 2) for i in range(6)]
        tmp = dict(g1=R("g1"), g2=R("g2"), sq=R("sq"), mean=R("mean"), var=R("var"), t=R("t", 3))
        s_r, gs_r, zt_r, zz_r, ax_r = R("s"), R("gs"), R("zt"), R("zz"), R("ax")
        psA = Ring(P, "psA", [128, 512], F32, 3, psum=True)
        psB = Ring(P, "psB", [128, 512], F32, 3, psum=True)
        psS = Ring(P, "psS", [128, 512], F32, 2, psum=True)
        lg_r = Ring(P, "lg", [128, 32], F32, 2); ex_r = Ring(P, "ex", [128, 32], F32, 2); mk_r = Ring(P, "mk", [128, 32], F32, 2)
        m8_r = Ring(P, "m8", [128, 8], F32, 2); sm_r = Ring(P, "sm", [128, 2], F32, 2)
        gT_r = Ring(P, "gT", [32, NB], F32, 2)
        xT_v = xT.rearrange("(k p) t -> p k t", p=128)
        yc_v = ycT.rearrange("(c p) t -> p c t", p=128)
        CB = lambda i: cl[:, i:i + 1]
        def c2_block(t0, n):
            j = 1 if t0 < CTX else 0
            xb, dxb = xb_r.next()
            P.dma('sp', xb[:], xT_v[:, :, t0:t0 + n], writes=[dxb])
            ubf, dub = ub_r.next()
            for k in range(8):
                P.op('dve' if k % 2 else 'pool', lambda e, k=k: e.tensor_scalar(ubf[:, k, :], xb[:, k, :], scp[:, 2 * k + j:2 * k + j + 1],
                     modT[:, 2 * k + j:2 * k + j + 1], ALU.mult, ALU.add), reads=[dxb, dscp, dmod], writes=[dub], accum=(k > 0))
            ya, dya = y_r[0].next()
            for c in range(4):
                tl = []
                for i in range(3):
                    tt_, dd_ = in_r[i].next()
                    P.dma('act', tt_[:], br6[i, c * 128:(c + 1) * 128, t0:t0 + n], writes=[dd_])
                    tl.append((tt_, dd_))
                (ga, dga), (hf, dhf), (hb, dhb) = tl
                s_, ds_ = s_r.next()
                P.op('pool', lambda e, s_=s_, hf=hf, hb=hb: e.tensor_tensor(s_[:], hf[:], hb[:], ALU.add), reads=[dhf, dhb], writes=[ds_])
                gs, dgs = gs_r.next()
                gelu_fm(P, gs[:], dgs, ga[:], dga, n, tmp)
                P.op('dve', lambda e, c=c, s_=s_, gs=gs: e.tensor_tensor(ya[:, c, :], s_[:], gs[:], ALU.mult), reads=[ds_, dgs], writes=[dya], accum=(c > 0))
            gy, dgy = gy_r.next()
            for c in range(4):
                tl = []
                for i in range(3, 6):
                    tt_, dd_ = in_r[i].next()
                    P.dma('act', tt_[:], br6[i, c * 128:(c + 1) * 128, t0:t0 + n], writes=[dd_])
                    tl.append((tt_, dd_))
                (ub, dubb), (yf, dyf), (ybk, dybk) = tl
                s_, ds_ = s_r.next()
                P.op('pool', lambda e, s_=s_, yf=yf, ybk=ybk: e.tensor_tensor(s_[:], yf[:], ybk[:], ALU.add), reads=[dyf, dybk], writes=[ds_])
                P.op('dve', lambda e, s_=s_, ub=ub, c=c: e.scalar_tensor_tensor(s_[:], ub[:], CB(32 + c), s_[:], ALU.mult, ALU.add),
                     reads=[ds_, dubb, dcl], writes=[ds_])
                gelu_fm(P, gy[:, c, :], dgy, s_[:], ds_, n, tmp)
            yb, dyb = y_r[1].next()
            for oc in range(4):
                pa, dpa = psA.next()
                pb, dpb = psB.next()
                for c in range(4):
                    P.op('pe', lambda e, pa=pa, c=c, oc=oc: e.matmul(pa[:, 0:n], wl[:, c, oc * 128:(oc + 1) * 128], gy[:, c, :], start=(c == 0), stop=(c == 3)),
                         reads=[dwl, dgy], writes=[dpa], pe_acc=True)
                for c in range(4):
                    P.op('pe', lambda e, pb=pb, c=c, oc=oc: e.matmul(pb[:, 0:n], wl[:, c, 512 + oc * 128:512 + (oc + 1) * 128], gy[:, c, :], start=(c == 0), stop=(c == 3)),
                         reads=[dwl, dgy], writes=[dpb], pe_acc=True)
                gs, dgs = gs_r.next()
                P.op('act', lambda e, gs=gs, pb=pb, oc=oc: e.activation(gs[:], pb[:, 0:n], AF.Sigmoid, bias=CB(24 + 4 + oc)), reads=[dpb, dcl], writes=[dgs])
                P.op('dve', lambda e, gs=gs, pa=pa, oc=oc: e.scalar_tensor_tensor(yb[:, oc, :], pa[:, 0:n], CB(24 + oc), gs[:], ALU.add, ALU.mult),
                     reads=[dpa, dgs, dcl], writes=[dyb], accum=(oc > 0))
            yc, dyc = y_r[2].next()
            P.dma('sp', yc[:], yc_v[:, :, t0:t0 + n], writes=[dyc])
            ys = [(ya, dya), (yb, dyb), (yc, dyc)]
            z, dz = z_r.next()
            for oc in range(8):
                zz, dzz = zz_r.next()
                for i in range(3):
                    pa, dpa = psA.next()
                    for k in range(8):
                        P.op('pe', lambda e, pa=pa, k=k, i=i, oc=oc: e.matmul(pa[:, 0:n], wg[:, k, i * 1024 + oc * 128:i * 1024 + (oc + 1) * 128], ubf[:, k, :],
                                                                       start=(k == 0), stop=(k == 7)), reads=[dwg[k], dub], writes=[dpa], pe_acc=True)
                    gs, dgs = gs_r.next()
                    P.op('act', lambda e, gs=gs, pa=pa, i=i, oc=oc: e.activation(gs[:], pa[:, 0:n], AF.Sigmoid, bias=CB(i * 8 + oc)), reads=[dpa, dcl], writes=[dgs])
                    pb, dpb = psB.next()
                    yi, dyi = ys[i]
                    for c in range(4):
                        P.op('pe', lambda e, pb=pb, c=c, i=i, oc=oc, yi=yi: e.matmul(pb[:, 0:n], wp[:, i, c, oc * 128:(oc + 1) * 128], yi[:, c, :],
                                                                              start=(c == 0), stop=(c == 3)), reads=[dwp, dyi], writes=[dpb], pe_acc=True)
                    if i == 0:
                        P.op('dve', lambda e, zz=zz, gs=gs, pb=pb: e.tensor_tensor(zz[:], pb[:, 0:n], gs[:], ALU.mult), reads=[dpb, dgs], writes=[dzz])
                    else:
                        zt, dzt = zt_r.next()
                        P.op('dve', lambda e, zt=zt, gs=gs, pb=pb: e.tensor_tensor(zt[:], pb[:, 0:n], gs[:], ALU.mult), reads=[dpb, dgs], writes=[dzt])
                        if i == 1:
                            P.op('pool', lambda e, zz=zz, zt=zt: e.tensor_tensor(zz[:], zz[:], zt[:], ALU.add), reads=[dzz, dzt], writes=[dzz])
                        else:
                            P.op('pool', lambda e, zz=zz, zt=zt, oc=oc: e.tensor_tensor(z[:, oc, :], zz[:], zt[:], ALU.add), reads=[dzz, dzt], writes=[dz], accum=(oc > 0))
            for oc in range(8):
                pa, dpa = psA.next()
                for k in range(8):
                    P.op('pe', lambda e, pa=pa, k=k, oc=oc: e.matmul(pa[:, 0:n], wo[:, k, oc * 128:(oc + 1) * 128], z[:, k, :], start=(k == 0), stop=(k == 7)),
                         reads=[dwo, dz], writes=[dpa], pe_acc=True)
                ax, dax = ax_r.next()
                P.op('pool', lambda e, ax=ax, oc=oc: e.tensor_scalar(ax[:], xb[:, oc, :], DN_ALPHA, None, ALU.mult), reads=[dxb], writes=[dax])
                P.op('dve', lambda e, ax=ax, pa=pa, oc=oc: e.scalar_tensor_tensor(xb[:, oc, :], pa[:, 0:n], modT[:, 32 + 2 * oc + j:32 + 2 * oc + j + 1], ax[:],
                                                                             ALU.mult, ALU.add), reads=[dpa, dax, dmod, dxb], writes=[dxb])
            vf, dvf = vf_r.next()
            ln_fm(P, xb, dxb, n, onesf, dof, lambda k: CB(36 + k), lambda k: CB(44 + k), dcl, vf, dvf, psS, tmp)
            P.op('pool', lambda e: e.tensor_scalar(xb[:], vf[:], DN_ALPHA, None, ALU.mult), reads=[dvf, dxb], writes=[dxb])
            P.dma('sp', acc_o.rearrange("(k p) t -> p k t", p=128)[:, :, t0:t0 + n], xb[:], reads=[dxb])
            for k in range(8):
                P.op('dve', lambda e, k=k: e.tensor_scalar(vf[:, k, :], vf[:, k, :], scp[:, 16 + 2 * k + j:16 + 2 * k + j + 1],
                     modT[:, 48 + 2 * k + j:48 + 2 * k + j + 1], ALU.mult, ALU.add), reads=[dvf, dscp, dmod, dxb], writes=[dvf])
            vb, dvb = vb_r.next()
            P.op('act', lambda e: e.copy(vb[:], vf[:]), reads=[dvf], writes=[dvb])
            P.dma('act', v_o.rearrange("(k p) t -> p k t", p=128)[:, :, t0:t0 + n], vb[:], reads=[dvb])
            gT, dgT = gT_r.next()
            for tt in range(n // 128):
                pl, dpl = psS.next()
                for k in range(8):
                    P.op('pe', lambda e, pl=pl, k=k, tt=tt: e.matmul(pl[:, 0:32], vf[:, k, tt * 128:(tt + 1) * 128], wrt[:, k, :], start=(k == 0), stop=(k == 7)),
                         reads=[dvf, dwrt], writes=[dpl], pe_acc=True)
                lg, dlg = lg_r.next(); ex, dex = ex_r.next(); mk, dmk = mk_r.next(); m8, dm8 = m8_r.next(); sm, dsm = sm_r.next()
                P.op('dve', lambda e, lg=lg, pl=pl: e.tensor_tensor(lg[:], pl[:, 0:32], brt[:], ALU.add), reads=[dpl, dbrt], writes=[dlg])
                P.op('dve', lambda e, lg=lg, m8=m8: e.max(m8[:], lg[:]), reads=[dlg], writes=[dm8])
                P.op('dve', lambda e, lg=lg, m8=m8, mk=mk: e.tensor_scalar(mk[:], lg[:], m8[:, 3:4], None, ALU.is_ge), reads=[dlg, dm8], writes=[dmk])
                P.op('dve', lambda e, m8=m8, sm=sm: e.tensor_scalar(sm[:, 0:1], m8[:, 0:1], -1.0, None, ALU.mult), reads=[dm8], writes=[dsm])
                P.op('act', lambda e, ex=ex, lg=lg, sm=sm: e.activation(ex[:], lg[:], AF.Exp, bias=sm[:, 0:1]), reads=[dlg, dsm], writes=[dex])
                P.op('dve', lambda e, ex=ex, mk=mk: e.tensor_tensor(ex[:], ex[:], mk[:], ALU.mult), reads=[dex, dmk], writes=[dex])
                P.op('dve', lambda e, ex=ex, sm=sm: e.reduce_sum(sm[:, 1:2], ex[:], AX.X), reads=[dex, dsm], writes=[dsm])
                P.op('dve', lambda e, sm=sm: e.reciprocal(sm[:, 1:2], sm[:, 1:2]), reads=[dsm], writes=[dsm])
                P.op('dve', lambda e, ex=ex, sm=sm: e.tensor_scalar(ex[:], ex[:], sm[:, 1:2], None, ALU.mult), reads=[dex, dsm], writes=[dex])
                pt, dpt = psS.next()
                P.op('pe', lambda e, pt=pt, ex=ex: e.transpose(pt[0:32, 0:128], ex[:], idf[:]), reads=[dex, did], writes=[dpt])
                P.op('act', lambda e, pt=pt, tt=tt: e.copy(gT[:, tt * 128:(tt + 1) * 128], pt[0:32, 0:128]), reads=[dpt], writes=[dgT], accum=(tt > 0))
            P.dma('sp', g_o[:, t0:t0 + n], gT[:], reads=[dgT])

        for (t0_, n_) in C2BLK:
            c2_block(t0_, n_)
        P.finish()
    return nc


def build_C3():
    nc = bass.Bass("TRN2", target_bir_lowering=False)
    dt_in = lambda n, s, d=F32: nc.dram_tensor(n, s, d, kind="ExternalInput").ap()
    dt_out = lambda n, s, d=F32: nc.dram_tensor(n, s, d, kind="ExternalOutput").ap()
    accT = dt_in("accT", [1024, TOK])
    vT = dt_in("vT", [1024, TOK], BF16)
    gateT = dt_in("gateT", [32, TOK])
    w_gu = dt_in("w_gu", [32, 1024, 2048])
    w_dn = dt_in("w_dn", [32, 1024, 1024])
    b_guT = dt_in("b_guT", [128, 32 * 16])
    b_dn = dt_in("b_dn", [32, 1024])
    cols_i = dt_in("cols", [128, 32])
    x2_o = dt_out("x2T", [1024, TOK])
    with ExitStack() as st:
        P = Prog(nc, st)
        acc = P.sbuf("acc", [128, 8, TOK]); dacc = [Dep(f"acc{b}") for b in range(len(BLOCKS))]
        vb = P.sbuf("vb", [128, 8, TOK], BF16); dvb = Dep("vb")
        acc_v = accT.rearrange("(k p) t -> p k t", p=128)
        v_v = vT.rearrange("(k p) t -> p k t", p=128)
        for bi, (t0, n) in enumerate(BLOCKS):
            P.dma('sp', acc[:, :, t0:t0 + n], acc_v[:, :, t0:t0 + n], writes=[dacc[bi]])
            P.dma('act', vb[:, :, t0:t0 + n], v_v[:, :, t0:t0 + n], writes=[dvb], accum=(bi > 0))
        gt = P.sbuf("gt", [32, TOK]); dgt = Dep("gt")
        P.dma('sp', gt[:], gateT, writes=[dgt])
        bg = P.sbuf("bg", [128, 512]); dbg = Dep("bg")
        P.dma('sp', bg[:], b_guT, writes=[dbg])
        bgv = bg[:].rearrange("p (e c) -> p e c", c=16)
        P.op('dve', lambda e: e.tensor_scalar(bgv[:, :, 8:16], bgv[:, :, 8:16], 1.0, None, ALU.add), reads=[dbg], writes=[dbg])
        bd = P.sbuf("bd", [32, 1024]); dbd = Dep("bd")
        P.dma('sp', bd[:], b_dn, writes=[dbd])
        cl = P.sbuf("cl", [128, 32]); dcl = Dep("cl")
        P.dma('sp', cl[:], cols_i, writes=[dcl])
        onesf = P.sbuf("onesf", [128, 128]); dof = Dep("onesf")
        P.op('pool', lambda e: e.memset(onesf[:], 1.0 / 1024.0), writes=[dof])
        ii = P.sbuf("ii", [32, 32], I32); dii = Dep("ii")
        cf = P.sbuf("cf", [32, 32]); dcf = Dep("cf")
        pf = P.sbuf("pf", [32, 32]); dpf = Dep("pf")
        idn = P.sbuf("idn", [32, 32]); did = Dep("idn")
        P.op('pool', lambda e: e.iota(ii[:], pattern=[[1, 32]], base=0, channel_multiplier=0), writes=[dii])
        P.op('dve', lambda e: e.tensor_copy(cf[:], ii[:]), reads=[dii], writes=[dcf])
        P.op('pool', lambda e: e.iota(ii[:], pattern=[[0, 32]], base=0, channel_multiplier=1), reads=[dcf], writes=[dii])
        P.op('dve', lambda e: e.tensor_copy(pf[:], ii[:]), reads=[dii], writes=[dpf])
        P.op('dve', lambda e: e.tensor_tensor(idn[:], cf[:], pf[:], ALU.is_equal), reads=[dcf, dpf], writes=[did])
        dsel = did
        psG = Ring(P, "psG", [128, 512], F32, 2, psum=True)
        psL = Ring(P, "psL", [128, 512], F32, 2, psum=True)
        psD = Ring(P, "psD", [128, 512], F32, 2, psum=True)
        psX = Ring(P, "psX", [128, 512], F32, 2, psum=True)
        gbc_r = Ring(P, "gbc", [128, TOK], F32, 1)
        wgl_r = Ring(P, "wgl", [128, 2, 8, 128], BF16, 2)
        wd_r = Ring(P, "wd", [128, 4, 1024], BF16, 2)
        act_r = Ring(P, "act", [128, 4, TOK], BF16, 1)
        R = lambda nm, n=2, d=F32: Ring(P, nm, [128, 512], d, n)
        hg_r, s1_r, hl_r, t_r = R("hg"), R("s1"), R("hl"), R("tt")
        for ex in range(32):
            gbc, dgbc = gbc_r.next()
            for bi, (t0, n) in enumerate(BLOCKS):
                px, dpx = psX.next()
                P.op('pe', lambda e, px=px, ex=ex, t0=t0, n=n: e.matmul(px[:, 0:n], idn[:, ex:ex + 1].to_broadcast([32, 128]), gt[:, t0:t0 + n], start=True, stop=True),
                     reads=[dsel, dgt], writes=[dpx])
                P.op('act', lambda e, px=px, gbc=gbc, t0=t0, n=n: e.activation(gbc[:, t0:t0 + n], px[:, 0:n], AF.Copy, scale=1.0 / 1.702),
                     reads=[dpx], writes=[dgbc], accum=(bi > 0))
            for half in range(2):
                wd, dwd = wd_r.next()
                for f4 in range(4):
                    fc = half * 4 + f4
                    P.dma('pool', wd[:, f4, :], w_dn[ex, fc * 128:(fc + 1) * 128, :], writes=[dwd], accum=(f4 > 0))
                at, dat = act_r.next()
                for f4 in range(4):
                    fc = half * 4 + f4
                    wgl, dwgl = wgl_r.next()
                    P.dma('pool', wgl[:, 0, :, :], w_gu[ex, :, fc * 128:(fc + 1) * 128].rearrange("(k p) j -> p k j", p=128), writes=[dwgl])
                    P.dma('pool', wgl[:, 1, :, :], w_gu[ex, :, 1024 + fc * 128:1024 + (fc + 1) * 128].rearrange("(k p) j -> p k j", p=128),
                          writes=[dwgl], accum=True)
                    for bi, (t0, n) in enumerate(BLOCKS):
                        pg, dpg = psG.next()
                        pl, dpl = psL.next()
                        for k in range(8):
                            P.op('pe', lambda e, pg=pg, k=k, wgl=wgl, t0=t0, n=n: e.matmul(pg[:, 0:n], wgl[:, 0, k, :], vb[:, k, t0:t0 + n], start=(k == 0), stop=(k == 7)),
                                 reads=[dwgl, dvb], writes=[dpg], pe_acc=True)
                        for k in range(8):
                            P.op('pe', lambda e, pl=pl, k=k, wgl=wgl, t0=t0, n=n: e.matmul(pl[:, 0:n], wgl[:, 1, k, :], vb[:, k, t0:t0 + n], start=(k == 0), stop=(k == 7)),
                                 reads=[dwgl, dvb], writes=[dpl], pe_acc=True)
                        hg, dhg = hg_r.next(); s1, ds1 = s1_r.next(); hl, dhl = hl_r.next(); tt, dtt = t_r.next()
                        P.op('dve', lambda e, hg=hg, pg=pg, n=n, ex=ex, fc=fc: e.tensor_scalar(hg[:, 0:n], pg[:, 0:n], bgv[:, ex, fc:fc + 1], 7.0, ALU.add, ALU.min),
                             reads=[dpg, dbg], writes=[dhg])
                        P.op('act', lambda e, s1=s1, hg=hg, n=n: e.activation(s1[:, 0:n], hg[:, 0:n], AF.Silu, scale=1.702), reads=[dhg], writes=[ds1])
                        P.op('dve', lambda e, hl=hl, pl=pl, n=n, ex=ex, fc=fc: e.tensor_scalar(hl[:, 0:n], pl[:, 0:n], bgv[:, ex, 8 + fc:9 + fc], 8.0, ALU.add, ALU.min),
                             reads=[dpl, dbg], writes=[dhl])
                        P.op('dve', lambda e, tt=tt, hl=hl, s1=s1, n=n: e.scalar_tensor_tensor(tt[:, 0:n], hl[:, 0:n], -6.0, s1[:, 0:n], ALU.max, ALU.mult),
                             reads=[dhl, ds1], writes=[dtt])
                        P.op('pool', lambda e, at=at, tt=tt, gbc=gbc, f4=f4, t0=t0, n=n: e.tensor_tensor(at[:, f4, t0:t0 + n], tt[:, 0:n], gbc[:, t0:t0 + n], ALU.mult),
                             reads=[dtt, dgbc], writes=[dat], accum=not (f4 == 0 and bi == 0))
                for bi, (t0, n) in enumerate(BLOCKS):
                    jj = 1 if t0 < CTX else 0
                    for oc in range(8):
                        pd, dpd = psD.next()
                        for f4 in range(4):
                            P.op('pe', lambda e, pd=pd, f4=f4, oc=oc, t0=t0, n=n, wd=wd, at=at: e.matmul(pd[:, 0:n], wd[:, f4, oc * 128:(oc + 1) * 128], at[:, f4, t0:t0 + n],
                                 start=(f4 == 0), stop=(f4 == 3)), reads=[dwd, dat], writes=[dpd], pe_acc=True)
                        P.op('dve', lambda e, pd=pd, oc=oc, t0=t0, n=n, jj=jj: e.scalar_tensor_tensor(acc[:, oc, t0:t0 + n], pd[:, 0:n], cl[:, jj * 8 + oc:jj * 8 + oc + 1],
                             acc[:, oc, t0:t0 + n], ALU.mult, ALU.add), reads=[dpd, dcl, dacc[bi]], writes=[dacc[bi]])
        for bi, (t0, n) in enumerate(BLOCKS):
            jj = 1 if t0 < CTX else 0
            for oc in range(8):
                pd, dpd = psD.next()
                P.op('pe', lambda e, pd=pd, oc=oc, t0=t0, n=n: e.matmul(pd[:, 0:n], bd[:, oc * 128:(oc + 1) * 128], gt[:, t0:t0 + n], start=True, stop=True),
                     reads=[dbd, dgt], writes=[dpd])
                P.op('dve', lambda e, pd=pd, oc=oc, t0=t0, n=n, jj=jj: e.scalar_tensor_tensor(acc[:, oc, t0:t0 + n], pd[:, 0:n], cl[:, jj * 8 + oc:jj * 8 + oc + 1],
                     acc[:, oc, t0:t0 + n], ALU.mult, ALU.add), reads=[dpd, dcl, dacc[bi]], writes=[dacc[bi]])
        tmp = dict(sq=R("sq"), mean=R("mean", 1), var=R("var", 1), t=R("t", 2))
        x2_v = x2_o.rearrange("(k p) t -> p k t", p=128)
        for bi, (t0, n) in enumerate(BLOCKS):
            av = acc[:, :, t0:t0 + n]
            ln_fm(P, av, dacc[bi], n, onesf, dof, lambda k: cl[:, 16 + k:17 + k], lambda k: cl[:, 24 + k:25 + k], dcl, av, dacc[bi], psX, tmp)
            P.dma('sp', x2_v[:, :, t0:t0 + n], av, reads=[dacc[bi]])
        P.finish()
    return nc


def _colT(vec, n):
    return np.ascontiguousarray(np.asarray(vec, np.float32).reshape(n, 128).T)


def host_C2(inp, l, xT_full, modT, xgu_full, Hf, Hb, Yf, Yb, yc_cores):
    cols = np.zeros((128, 64), np.float32)
    for i in range(3):
        cols[:, i * 8:(i + 1) * 8] = _colT(inp['b_gate'][l, i], 8)
    cols[:, 24:32] = _colT(inp['s5_b_glu'][l], 8)
    cols[:, 32:36] = _colT(inp['s5_d'][l], 4)
    cols[:, 36:44] = _colT(inp['ln_g'][l, 0], 8)
    cols[:, 44:52] = _colT(inp['ln_b'][l, 0], 8)
    w_gate = np.ascontiguousarray(inp['w_in'][l][:, 3072:6144])
    maps = []
    for c in range(NCORE):
        cc = np.concatenate([np.arange(CTX), CTX + c * LAT_PC + np.arange(LAT_PC)])
        br6 = np.stack([xgu_full[512:1024][:, cc], Hf[:, cc], Hb[:, cc], xgu_full[1024:1536][:, cc], Yf[:, cc], Yb[:, cc]], 0)
        maps.append(dict(xT=np.ascontiguousarray(xT_full[:, cc]), modT=modT, br6=np.ascontiguousarray(br6, dtype=np.float32),
                         ycT=yc_cores[c], w_gate=w_gate, w_proj=inp['w_proj'][l], w_out=inp['w_out'][l],
                         w_glu=inp['s5_w_glu'][l], cols=cols, w_rt=inp['w_router'][l],
                         b_rt=np.ascontiguousarray(inp['b_router'][l][None, :])))
    return maps


def host_C3(inp, l, modT, c2_res):
    cols = np.zeros((128, 32), np.float32)
    m3 = modT.reshape(128, 48, 2)
    cols[:, 0:8] = m3[:, 40:48, 0]
    cols[:, 8:16] = m3[:, 40:48, 1]
    cols[:, 16:24] = _colT(inp['ln_g'][l, 1], 8)
    cols[:, 24:32] = _colT(inp['ln_b'][l, 1], 8)
    bgu = inp['b_gu'][l]
    b_guT = np.ascontiguousarray(bgu.reshape(32, 16, 128).transpose(2, 0, 1).reshape(128, 512))
    maps = []
    for c in range(NCORE):
        r = c2_res[c]
        maps.append(dict(accT=r['accT'], vT=r['vT'], gateT=r['gateT'], w_gu=inp['w_gu'][l], w_dn=inp['w_down'][l],
                         b_guT=b_guT, b_dn=inp['b_down'][l], cols=cols))
    return maps


def _gather_cols(per_core, rows, dtype):
    out = np.zeros((rows, TT), dtype)
    for c in range(NCORE):
        r = per_core[c]
        if c == 0:
            out[:, :CTX] = r[:, :CTX]
        out[:, CTX + c * LAT_PC:CTX + (c + 1) * LAT_PC] = r[:, CTX:]
    return out


_PROGS = {}


def _prog(name):
    if name not in _PROGS:
        _PROGS[name] = dict(A=build_A, B=build_B, C1=build_C1, C2=build_C2, C3=build_C3)[name]()
    return _PROGS[name]


def _run(name, maps):
    res = run_bass_kernel_spmd(_prog(name), maps, core_ids=list(range(NCORE)))
    return [{k: np.asarray(v) for k, v in r.items()} for r in res.results]


def kernel(**inp):
    inp = {k: np.asarray(v) for k, v in inp.items()}
    xT_full = np.ascontiguousarray(np.concatenate([inp['ctx'][0], inp['x'][0]], 0).T.astype(np.float32))
    rev = np.concatenate([np.arange(CTX)[::-1], CTX + np.arange(SEQ)[::-1]])
    for l in range(2):
        ra = _run('A', host_A(inp, l, xT_full))
        modT = ra[0]['modT']
        xgu_full = _gather_cols([r['xguT'] for r in ra], 1536, np.float32)
        qk_full = _gather_cols([r['qkT'] for r in ra], 1024, ra[0]['qkT'].dtype)
        v_full = np.zeros((TT, 512), ra[0]['v'].dtype)
        for c in range(NCORE):
            v_full[:CTX] = ra[c]['v'][:CTX]
            v_full[CTX + c * LAT_PC:CTX + (c + 1) * LAT_PC] = ra[c]['v'][CTX:]
        rb = _run('B', host_B(inp, l, xgu_full))
        Hf = np.concatenate([r['h'][0:64] for r in rb], 0)
        Hb = np.concatenate([r['h'][64:128][:, rev] for r in rb], 0)
        Yf = np.concatenate([r['ys5'][0:64] for r in rb], 0)
        Yb = np.concatenate([r['ys5'][64:128][:, rev] for r in rb], 0)
        rc1 = _run('C1', host_C1(inp, l, qk_full, v_full))
        rc2 = _run('C2', host_C2(inp, l, xT_full, modT, xgu_full, Hf, Hb, Yf, Yb, [r['ycT'] for r in rc1]))
        rc3 = _run('C3', host_C3(inp, l, modT, rc2))
        xT_full = _gather_cols([r['x2T'] for r in rc3], 1024, np.float32)
    out = np.ascontiguousarray(xT_full[:, CTX:].T)[None].astype(np.float32)
    return out
```

```python
import numpy as np
import concourse.bass as bass
import concourse.mybir as mybir
from concourse.bass_utils import run_bass_kernel_spmd
from contextlib import ExitStack

F32 = mybir.dt.float32
BF16 = mybir.dt.bfloat16
I32 = mybir.dt.int32
ALU = mybir.AluOpType
AF = mybir.ActivationFunctionType
AX = mybir.AxisListType

ENGS = ['pe', 'dve', 'act', 'pool', 'sp']
SAME_ENGINE_SYNC = True


class Dep:
    __slots__ = ('w', 'r', 'dw', 'dr', 'name', 'dsem')

    def __init__(self, name=''):
        self.w = {}
        self.r = {}
        self.dw = set()
        self.dr = set()
        self.name = name
        self.dsem = None


class DSem:
    __slots__ = ('sem', 'total')

    def __init__(self, sem):
        self.sem = sem
        self.total = 0


class Prog:
    def __init__(self, nc, stack):
        self.nc = nc
        self.stack = stack
        self.eng = dict(pe=nc.tensor, dve=nc.vector, act=nc.scalar, pool=nc.gpsimd, sp=nc.sync)
        self.sem = {e: stack.enter_context(nc.semaphore('s_' + e)) for e in ENGS}
        self.cnt = {e: 0 for e in ENGS}
        self.prog = {e: [] for e in ENGS}
        self.clock = {e: {f: 0 for f in ENGS} for e in ENGS}
        self.hist = {e: [tuple(0 for _ in ENGS)] for e in ENGS}
        self.dknown = {e: {} for e in ENGS}
        self.dsems = []
        self.free_dsems = []
        self.top = stack
        self.prefix = ''
        self.ninstr = 0

    def sbuf(self, name, shape, dt=F32):
        return self.stack.enter_context(self.nc.sbuf_tensor(self.prefix + name, list(shape), dt))

    def psum(self, name, shape, dt=F32):
        return self.stack.enter_context(self.nc.psum_tensor(self.prefix + name, list(shape), dt))

    def new_dsem(self, name):
        if self.free_dsems:
            return self.free_dsems.pop()
        s = DSem(self.top.enter_context(self.nc.semaphore('dsem%d' % len(self.dsems))))
        self.dsems.append(s)
        return s

    def _collect(self, reads, writes, accum=False):
        need = {}
        dneed = set()
        for d in reads:
            for f, c in d.w.items():
                if need.get(f, 0) < c:
                    need[f] = c
            dneed |= d.dw
        for d in writes:
            if not accum:
                for f, c in d.w.items():
                    if need.get(f, 0) < c:
                        need[f] = c
                dneed |= d.dw
            for f, c in d.r.items():
                if need.get(f, 0) < c:
                    need[f] = c
            dneed |= d.dr
        return need, dneed

    def _emit_waits(self, e, need, dneed, pe_acc=False):
        ck = self.clock[e]
        for f, c in need.items():
            if ck[f] >= c:
                continue
            if f == e and (not SAME_ENGINE_SYNC or e == 'pe' and pe_acc):
                continue
            self.prog[e].append(('wait', self.sem[f], c))
            h = self.hist[f][c]
            for i, g in enumerate(ENGS):
                if ck[g] < h[i]:
                    ck[g] = h[i]
            if ck[f] < c:
                ck[f] = c
        dk = self.dknown[e]
        for s in dneed:
            if dk.get(s, 0) >= s.total:
                continue
            self.prog[e].append(('wait', s.sem, s.total))
            dk[s] = s.total

    def op(self, e, fn, reads=(), writes=(), pe_acc=False, accum=False):
        need, dneed = self._collect(reads, writes, accum)
        self._emit_waits(e, need, dneed, pe_acc)
        self.cnt[e] += 1
        j = self.cnt[e]
        self.prog[e].append(('op', fn))
        ck = self.clock[e]
        snap = tuple(j if g == e else ck[g] for g in ENGS)
        self.hist[e].append(snap)
        if not SAME_ENGINE_SYNC:
            ck[e] = j
        for d in writes:
            if accum:
                d.w[e] = j
            else:
                d.w = {e: j}
                d.dw = set()
            d.r = {}
            d.dr = set()
        for d in reads:
            if d.w.get(e) == j:
                continue
            d.r[e] = j
        self.ninstr += 1

    def dma(self, q, out, in_, reads=(), writes=(), key=None, accum=False, **kw):
        if key is None:
            key = writes[0] if writes else reads[0]
        if key.dsem is None:
            key.dsem = self.new_dsem('d_' + key.name)
        ds = key.dsem
        need, dneed = self._collect(reads, writes, accum)
        self._emit_waits(q, need, dneed)
        ds.total += 16
        self.prog[q].append(('dma', out, in_, ds.sem, kw))
        for d in writes:
            if accum:
                d.dw.add(ds)
            else:
                d.w = {}
                d.dw = {ds}
            d.r = {}
            d.dr = set()
        for d in reads:
            d.dr.add(ds)
        self.ninstr += 1

    def custom(self, q, fn, reads=(), writes=(), key=None):
        if key is None:
            key = writes[0] if writes else reads[0]
        if key.dsem is None:
            key.dsem = self.new_dsem('d_' + key.name)
        ds = key.dsem
        need, dneed = self._collect(reads, writes)
        self._emit_waits(q, need, dneed)
        ds.total += 16
        self.prog[q].append(('cdma', fn, ds.sem))
        for d in writes:
            d.w = {}
            d.dw = {ds}
            d.r = {}
            d.dr = set()
        for d in reads:
            d.dr.add(ds)
        self.ninstr += 1

    def flush(self):
        self.finish()
        for e in ENGS:
            self.prog[e] = []
            for f in ENGS:
                self.clock[e][f] = self.cnt[f]
            for s_ in self.dsems:
                self.dknown[e][s_] = s_.total
        self.free_dsems = list(self.dsems)

    def finish(self):
        nc = self.nc
        fin = self.prog['sp']
        for f in ENGS:
            if f != 'sp' and self.cnt[f] > 0:
                fin.append(('wait', self.sem[f], self.cnt[f]))
        for s in self.dsems:
            if s.total > 0:
                fin.append(('wait', s.sem, s.total))
        prog = self.prog
        sems = self.sem

        def run(e, engine):
            for it in prog[e]:
                if it[0] == 'wait':
                    engine.wait_ge(it[1], it[2])
                elif it[0] == 'op':
                    it[1](engine).then_inc(sems[e], 1)
                elif it[0] == 'cdma':
                    it[1](engine).then_inc(it[2], 16)
                else:
                    engine.dma_start(out=it[1], in_=it[2], **it[4]).then_inc(it[3], 16)

        with nc.Block() as block:
            @block.tensor
            def _(eng):
                run('pe', eng)

            @block.vector
            def _(eng):
                run('dve', eng)

            @block.scalar
            def _(eng):
                run('act', eng)

            @block.gpsimd
            def _(eng):
                run('pool', eng)

            @block.sync
            def _(eng):
                run('sp', eng)


class SoloEnv:
    def __init__(self):
        self.nc = bass.Bass("TRN2", target_bir_lowering=False)
        self._st = None

    def dt_in(self, n, s, d=F32):
        return self.nc.dram_tensor(n, s, d, kind="ExternalInput").ap()

    def dt_out(self, n, s, d=F32):
        return self.nc.dram_tensor(n, s, d, kind="ExternalOutput").ap()

    def scope(self):
        env = self

        class _S:
            def __enter__(self_):
                env._st = ExitStack()
                env._st.__enter__()
                return Prog(env.nc, env._st)

            def __exit__(self_, *a):
                return env._st.__exit__(*a)
        return _S()

    def end(self, P):
        P.finish()


class FusedEnv:
    def __init__(self):
        self.nc = bass.Bass("TRN2", target_bir_lowering=False)
        self.top = ExitStack()
        self.top.__enter__()
        self.P = Prog(self.nc, self.top)
        self.prefix = ''
        self.links = {}

    def scratch(self, name, shape, dt):
        return self.nc.dram_tensor(name, list(shape), dt, kind="Internal").ap()

    def begin(self, prefix, links):
        self.prefix = prefix
        self.links = links

    def dt_in(self, n, s, d=F32):
        if n in self.links:
            return self.links[n]
        return self.nc.dram_tensor(self.prefix + n, s, d, kind="ExternalInput").ap()

    def dt_out(self, n, s, d=F32):
        if n in self.links:
            return self.links[n]
        return self.nc.dram_tensor(self.prefix + n, s, d, kind="ExternalOutput").ap()

    def scope(self):
        env = self

        class _S:
            def __enter__(self_):
                env._st = ExitStack()
                env._st.__enter__()
                env.P.stack = env._st
                env.P.prefix = env.prefix
                return env.P

            def __exit__(self_, *a):
                return env._st.__exit__(*a)
        return _S()

    def end(self, P):
        P.flush()

    def close(self):
        self.top.__exit__(None, None, None)
        return self.nc


class Ring:
    def __init__(self, P, name, shape, dt, n, psum=False):
        mk = P.psum if psum else P.sbuf
        self.tiles = [mk(f"{name}{i}", shape, dt) for i in range(n)]
        self.deps = [Dep(f"{name}{i}") for i in range(n)]
        self.n = n
        self.i = 0

    def next(self):
        k = self.i % self.n
        self.i += 1
        return self.tiles[k], self.deps[k]


D_MODEL = 1024
SEQ = 16384
CTX = 256
TT = CTX + SEQ
NCORE = 8
LAT_PC = SEQ // NCORE
TOK = CTX + LAT_PC
BLOCKS = [(0, 256)] + [(256 + 512 * i, 512) for i in range(4)]
TWO_PI_HI = 6.28125
TWO_PI_LO = 0.0019353071795864769
PI_SAFE = 3.1415925
DN_ALPHA = 4.0 ** 0.25
LN_EPS = 1e-5
RMS_EPS = 1e-6


def sin_reduced(P, out_t, ang_t, shape, deps_in, dep_out, tmp, tmpi, dtmp, dtmpi, shift=0.0):
    P.op('dve', lambda e: e.tensor_scalar(tmp, ang_t, shift, 1.0 / (2 * np.pi), ALU.add, ALU.mult),
         reads=deps_in, writes=[dtmp])
    P.op('dve', lambda e: e.tensor_copy(tmpi, tmp), reads=[dtmp], writes=[dtmpi])
    P.op('dve', lambda e: e.tensor_copy(tmp, tmpi), reads=[dtmpi], writes=[dtmp])
    P.op('dve', lambda e: e.scalar_tensor_tensor(out_t, tmp, -TWO_PI_HI, ang_t, ALU.mult, ALU.add),
         reads=[dtmp] + list(deps_in), writes=[dep_out])
    P.op('dve', lambda e: e.scalar_tensor_tensor(out_t, tmp, -TWO_PI_LO, out_t, ALU.mult, ALU.add),
         reads=[dtmp, dep_out], writes=[dep_out])
    P.op('dve', lambda e: e.tensor_scalar(out_t, out_t, shift, PI_SAFE, ALU.add, ALU.min),
         reads=[dep_out], writes=[dep_out])
    P.op('dve', lambda e: e.tensor_scalar(out_t, out_t, -PI_SAFE, None, ALU.max),
         reads=[dep_out], writes=[dep_out])
    P.op('act', lambda e: e.activation(out_t, out_t, AF.Sin), reads=[dep_out], writes=[dep_out])


def build_A(env=None):
    env = env or SoloEnv()
    nc = env.nc
    dt_in, dt_out = env.dt_in, env.dt_out
    xT = dt_in("xT", [1024, TOK])
    cvec = dt_in("cvec", [128, 16])
    w_mod = dt_in("w_mod", [1024, 6144])
    b_modT = dt_in("b_modT", [128, 48])
    w_main = dt_in("w_main", [1024, 3072])
    w_rot = dt_in("w_rot", [1024, 1024])
    pos = dt_in("pos", [2, TOK])
    meta = dt_in("meta", [128, 2])
    modT_o = dt_out("modT", [128, 96])
    xgu_o = dt_out("xguT", [1536, TOK])
    qk_o = dt_out("qkT", [1024, TOK], BF16)
    v_o = dt_out("v", [TOK, 512], BF16)
    with env.scope() as P:
        cv = P.sbuf("cv", [128, 16]); dcv = Dep("cv")
        sg = P.sbuf("sg", [128, 16]); dsg = Dep("sg")
        bm = P.sbuf("bm", [128, 48]); dbm = Dep("bm")
        modT = P.sbuf("modTs", [128, 96]); dmod = Dep("mod")
        P.dma('sp', cv[:], cvec, writes=[dcv])
        P.dma('sp', bm[:], b_modT, writes=[dbm])
        P.op('act', lambda e: e.activation(sg[:], cv[:], AF.Sigmoid), reads=[dcv], writes=[dsg])
        P.op('dve', lambda e: e.tensor_tensor(sg[:], sg[:], cv[:], ALU.mult), reads=[dsg, dcv], writes=[dsg])
        pm = P.psum("pm", [128, 512]); dpm = Dep("pm")
        wm = Ring(P, "wm", [128, 8, 256], F32, 2)
        w_mod_v = w_mod.rearrange("(k p) j -> p k j", p=128)
        for g in range(24):
            wt, dw = wm.next()
            P.dma('sp' if g % 2 == 0 else 'act', wt[:], w_mod_v[:, :, g * 256:(g + 1) * 256], writes=[dw])
            for jj in range(2):
                jc = g * 2 + jj
                for k in range(8):
                    P.op('pe', lambda e, wt=wt, jj=jj, k=k, jc=jc: e.matmul(
                        pm[:, jc * 2:jc * 2 + 2], wt[:, k, jj * 128:(jj + 1) * 128], sg[:, k * 2:k * 2 + 2],
                        start=(k == 0), stop=(k == 7)), reads=[dw, dsg], writes=[dpm], pe_acc=True)
        bmv = bm[:].unsqueeze(2).to_broadcast([128, 48, 2])
        P.op('dve', lambda e: e.tensor_tensor(modT[:].rearrange("p (a b) -> p a b", b=2),
                                              pm[:, 0:96].rearrange("p (a b) -> p a b", b=2), bmv, ALU.add),
             reads=[dpm, dbm], writes=[dmod])
        P.dma('sp', modT_o, modT[:], reads=[dmod])
        sc1p = P.sbuf("sc1p", [128, 16]); dsc = Dep("sc1p")
        P.op('dve', lambda e: e.tensor_scalar(sc1p[:], modT[:, 16:32], 1.0, None, ALU.add), reads=[dmod], writes=[dsc])
        uT = P.sbuf("uT", [128, 8, TOK], BF16); du = [Dep(f"u{k}") for k in range(8)]
        xr = Ring(P, "xs", [128, TOK], F32, 2)
        xT_v = xT.rearrange("(k p) t -> p k t", p=128)
        for k in range(8):
            xt, dx = xr.next()
            P.dma('sp' if k % 2 == 0 else 'act', xt[:], xT_v[:, k, :], writes=[dx])
            P.op('dve', lambda e, xt=xt, k=k: e.tensor_scalar(uT[:, k, 0:CTX], xt[:, 0:CTX], sc1p[:, 2 * k + 1:2 * k + 2],
                                                             modT[:, 2 * k + 1:2 * k + 2], ALU.mult, ALU.add),
                 reads=[dx, dsc, dmod], writes=[du[k]])
            P.op('pool', lambda e, xt=xt, k=k: e.tensor_scalar(uT[:, k, CTX:TOK], xt[:, CTX:TOK], sc1p[:, 2 * k:2 * k + 1],
                                                              modT[:, 2 * k:2 * k + 1], ALU.mult, ALU.add),
                 reads=[dx, dsc, dmod], writes=[du[k]], accum=True)
        mt = P.sbuf("mt", [128, 2]); dmt = Dep("mt")
        P.dma('sp', mt[:], meta, writes=[dmt])
        frq = P.sbuf("frq", [128, 1]); dfr = Dep("frq")
        P.op('act', lambda e: e.activation(frq[:], mt[:, 0:1], AF.Exp, scale=-float(np.log(10000.0) / 16.0)),
             reads=[dmt], writes=[dfr])
        ang = P.sbuf("ang", [128, TOK]); dang = Dep("ang")
        for a in range(2):
            for hh in range(2):
                p0 = hh * 64 + a * 32
                P.dma('sp', ang[p0:p0 + 32, :], pos[a:a + 1, :].partition_broadcast(32), writes=[dang], accum=True)
        P.op('dve', lambda e: e.tensor_scalar(ang[:], ang[:], frq[:, 0:1], None, ALU.mult), reads=[dang, dfr], writes=[dang])
        tmp = P.sbuf("rtmp", [128, TOK]); tmpi = P.sbuf("rtmpi", [128, TOK], I32); dtmp = Dep("rtmp"); dtmpi = Dep("rtmpi")
        sinT = P.sbuf("sinT", [128, TOK]); dsin = Dep("sinT")
        cosT = P.sbuf("cosT", [128, TOK]); dcos = Dep("cosT")
        sin_reduced(P, sinT[:], ang[:], None, [dang], dsin, tmp[:], tmpi[:], dtmp, dtmpi, 0.0)
        sin_reduced(P, cosT[:], ang[:], None, [dang], dcos, tmp[:], tmpi[:], dtmp, dtmpi, float(np.pi / 2))
        P.op('dve', lambda e: e.tensor_scalar(sinT[:], sinT[:], mt[:, 1:2], None, ALU.mult), reads=[dsin, dmt], writes=[dsin])
        wmn = P.sbuf("wmn", [128, 8, 3072], BF16); dwm = [Dep(f"wmn{k}") for k in range(8)]
        wrt = P.sbuf("wrt", [128, 8, 1024], BF16); dwr = [Dep(f"wrt{k}") for k in range(8)]
        w_main_v = w_main.rearrange("(k p) j -> p k j", p=128)
        w_rot_v = w_rot.rearrange("(k p) j -> p k j", p=128)
        for k in range(8):
            for h in range(2):
                P.dma('pool', wmn[:, k, h * 1536:(h + 1) * 1536], w_main_v[:, k, h * 1536:(h + 1) * 1536],
                      writes=[dwm[k]], accum=True)
            P.dma('pool', wrt[:, k, :], w_rot_v[:, k, :], writes=[dwr[k]])
        psA = Ring(P, "psA", [128, 512], F32, 3, psum=True)
        psB = Ring(P, "psB", [128, 512], F32, 2, psum=True)
        ev = Ring(P, "ev", [128, 512], F32, 3)
        evb = Ring(P, "evb", [128, 512], BF16, 3)
        t1r = Ring(P, "t1r", [128, 512], F32, 2)
        qi = 0
        for (t0, n) in BLOCKS:
            for jc in range(20):
                pa, dpa = psA.next()
                for k in range(8):
                    P.op('pe', lambda e, pa=pa, k=k, jc=jc, t0=t0, n=n: e.matmul(
                        pa[:, 0:n], wmn[:, k, jc * 128:(jc + 1) * 128], uT[:, k, t0:t0 + n],
                        start=(k == 0), stop=(k == 7)), reads=[dwm[k], du[k]], writes=[dpa], pe_acc=True)
                if jc < 12:
                    et, de = ev.next()
                    P.op('act', lambda e, et=et, pa=pa, n=n: e.copy(et[:, 0:n], pa[:, 0:n]), reads=[dpa], writes=[de])
                    P.dma('sp', xgu_o[jc * 128:(jc + 1) * 128, t0:t0 + n], et[:, 0:n], reads=[de])
                else:
                    pb, dpb = psB.next()
                    jr = jc - 12
                    for k in range(8):
                        P.op('pe', lambda e, pb=pb, k=k, jr=jr, t0=t0, n=n: e.matmul(
                            pb[:, 0:n], wrt[:, k, jr * 128:(jr + 1) * 128], uT[:, k, t0:t0 + n],
                            start=(k == 0), stop=(k == 7)), reads=[dwr[k], du[k]], writes=[dpb], pe_acc=True)
                    isq = jc < 16
                    ct, dct = (cosT, dcos)
                    sn, dsn = (sinT, dsin)
                    qs = 0.125 if isq else 1.0
                    t1, dt1 = t1r.next()
                    et, de = ev.next()
                    eb, deb = evb.next()
                    P.op('dve', lambda e, t1=t1, pa=pa, ct=ct, t0=t0, n=n, qs=qs: e.scalar_tensor_tensor(
                        t1[:, 0:n], pa[:, 0:n], qs, ct[:, t0:t0 + n], ALU.mult, ALU.mult), reads=[dpa, dct], writes=[dt1])
                    P.op('dve', lambda e, et=et, pb=pb, sn=sn, t0=t0, n=n, qs=qs: e.scalar_tensor_tensor(
                        et[:, 0:n], pb[:, 0:n], qs, sn[:, t0:t0 + n], ALU.mult, ALU.mult), reads=[dpb, dsn], writes=[de])
                    P.op('pool', lambda e, eb=eb, et=et, t1=t1, n=n: e.tensor_tensor(
                        eb[:, 0:n], et[:, 0:n], t1[:, 0:n], ALU.add), reads=[de, dt1], writes=[deb])
                    P.dma('act', qk_o[jr * 128:(jr + 1) * 128, t0:t0 + n], eb[:, 0:n], reads=[deb])
        vb = Ring(P, "vb", [128, 512], BF16, 2)
        for tt in range(TOK // 128):
            pa, dpa = psA.next()
            for k in range(8):
                P.op('pe', lambda e, pa=pa, k=k, tt=tt: e.matmul(
                    pa[:], uT[:, k, tt * 128:(tt + 1) * 128], wmn[:, k, 2560:3072],
                    start=(k == 0), stop=(k == 7)), reads=[dwm[k], du[k]], writes=[dpa], pe_acc=True)
            vt, dv = vb.next()
            P.op('act', lambda e, vt=vt, pa=pa: e.copy(vt[:], pa[:]), reads=[dpa], writes=[dv])
            P.dma('sp', v_o[tt * 128:(tt + 1) * 128, :], vt[:], reads=[dv])
        env.end(P)
    return nc


def host_A(inp, l, xT_full):
    cv = np.stack([inp['c'][0], inp['c_ctx']], -1).reshape(8, 128, 2).transpose(1, 0, 2).reshape(128, 16)
    b_modT = np.ascontiguousarray(inp['b_mod'][l].reshape(48, 128).T)
    w_in = inp['w_in'][l]
    perm = np.arange(1024) ^ 16
    w_rot = np.ascontiguousarray(w_in[:, 1536:2560][:, perm])
    w_main = np.ascontiguousarray(w_in[:, :3072])
    p = np.arange(128)
    meta = np.stack([(p % 16).astype(np.float32), np.where((p % 32) // 16 == 0, -1.0, 1.0).astype(np.float32)], -1)
    maps = []
    for c in range(NCORE):
        cols = np.concatenate([np.arange(CTX), CTX + c * LAT_PC + np.arange(LAT_PC)])
        tl = c * LAT_PC + np.arange(LAT_PC)
        pos = np.zeros((2, TOK), np.float32)
        pos[0, CTX:] = tl // 64
        pos[1, CTX:] = tl % 64
        maps.append(dict(xT=np.ascontiguousarray(xT_full[:, cols]), cvec=np.ascontiguousarray(cv, dtype=np.float32),
                         w_mod=inp['w_mod'][l], b_modT=b_modT, w_main=w_main, w_rot=w_rot, pos=pos,
                         meta=np.ascontiguousarray(meta)))
    return maps


SBLOCKS = [(0, 256)] + [(256 + 512 * i, 512) for i in range(32)]
XPAD = TT + 8


def build_B(env=None):
    env = env or SoloEnv()
    nc = env.nc
    dt_in, dt_out = env.dt_in, env.dt_out
    xa2 = dt_in("xa2", [128, XPAD])
    ub2 = dt_in("ub2", [128, TT])
    lru = dt_in("lru", [128, 9])
    wrbd = dt_in("wrbd", [128, 128])
    wibd = dt_in("wibd", [128, 128])
    s5par = dt_in("s5par", [128, 12])
    bfull = dt_in("bfull", [128, 4 * 2 * 32])
    cT = dt_in("cT", [128, 4 * 2 * 64])
    h_o = dt_out("h", [128, TT])
    y_o = dt_out("ys5", [128, TT])
    with env.scope() as P:
        lr = P.sbuf("lr", [128, 9]); dlr = Dep("lr")
        wr = P.sbuf("wr", [128, 128]); dwr = Dep("wr")
        wi = P.sbuf("wi", [128, 128]); dwi = Dep("wi")
        sp = P.sbuf("sp", [128, 12]); dsp = Dep("sp")
        bf = P.sbuf("bf", [128, 4, 2, 32]); dbf = Dep("bf")
        ct = P.sbuf("ct", [128, 4, 2, 64]); dct = Dep("ct")
        P.dma('sp', lr[:], lru, writes=[dlr])
        P.dma('sp', wr[:], wrbd, writes=[dwr])
        P.dma('sp', wi[:], wibd, writes=[dwi])
        P.dma('sp', sp[:], s5par, writes=[dsp])
        P.dma('sp', bf[:], bfull.rearrange("p (a b c) -> p a b c", a=4, b=2), writes=[dbf])
        P.dma('sp', ct[:], cT.rearrange("p (a b c) -> p a b c", a=4, b=2), writes=[dct])
        P.op('dve', lambda e: e.tensor_scalar(ct[:, :, 1, :], ct[:, :, 1, :], -1.0, None, ALU.mult), reads=[dct], writes=[dct])
        c8 = P.sbuf("c8", [128, 1]); dc8 = Dep("c8")
        P.op('act', lambda e: e.activation(c8[:], lr[:, 8:9], AF.Exp, scale=-1.0), reads=[dlr], writes=[dc8])
        P.op('act', lambda e: e.activation(c8[:], c8[:], AF.Ln, bias=1.0), reads=[dc8], writes=[dc8])
        P.op('dve', lambda e: e.tensor_scalar(c8[:], c8[:], -8.0, None, ALU.mult), reads=[dc8], writes=[dc8])
        idf = P.sbuf("idf", [128, 128]); did = Dep("id")
        ii = P.sbuf("ii", [128, 128], I32); dii = Dep("ii")
        cf = P.sbuf("cf", [128, 128]); dcf = Dep("cf")
        pf = P.sbuf("pf", [128, 128]); dpf = Dep("pf")
        P.op('pool', lambda e: e.iota(ii[:], pattern=[[1, 128]], base=0, channel_multiplier=0), writes=[dii])
        P.op('dve', lambda e: e.tensor_copy(cf[:], ii[:]), reads=[dii], writes=[dcf])
        P.op('pool', lambda e: e.iota(ii[:], pattern=[[0, 128]], base=0, channel_multiplier=1), reads=[dcf], writes=[dii])
        P.op('dve', lambda e: e.tensor_copy(pf[:], ii[:]), reads=[dii], writes=[dpf])
        P.op('dve', lambda e: e.tensor_tensor(idf[:], cf[:], pf[:], ALU.is_equal), reads=[dcf, dpf], writes=[did])
        tau = P.sbuf("tau", [128, 512]); dtau = Dep("tau")
        taui = P.sbuf("taui", [128, 512], I32); dtaui = Dep("taui")
        P.op('pool', lambda e: e.iota(taui[:], pattern=[[1, 512]], base=1, channel_multiplier=0), writes=[dtaui])
        P.op('dve', lambda e: e.tensor_copy(tau[:], taui[:]), reads=[dtaui], writes=[dtau])
        col = P.sbuf("col", [128, 4, 16]); dcol = Dep("col")
        cosT = P.sbuf("cosT", [128, 4, 512]); sinT = P.sbuf("sinT", [128, 4, 512])
        dcos = [Dep(f"cos{i}") for i in range(4)]; dsin = [Dep(f"sin{i}") for i in range(4)]
        angt = P.sbuf("angt", [128, 512]); dangt = Dep("angt")
        tmp = P.sbuf("stmp", [128, 512]); tmpi = P.sbuf("stmpi", [128, 512], I32); dtmp = Dep("stmp"); dtmpi = Dep("stmpi")
        bbp = P.sbuf("bbp", [128, 2, 128]); dbbp = Dep("bbp")
        bbT = P.sbuf("bbT", [128, 2, 128]); dbbT = Dep("bbT")
        bbT3 = P.sbuf("bbT3", [128, 2, 128]); dbbT3 = Dep("bbT3")
        tb = P.sbuf("tb", [128, 32]); dtb = Dep("tb")
        ci = P.sbuf("ci", [128, 4], I32); dci = Dep("ci")
        C = lambda ti, k: col[:, ti, k:k + 1]
        for ti in range(4):
            are, aim, ldt = sp[:, ti * 3:ti * 3 + 1], sp[:, ti * 3 + 1:ti * 3 + 2], sp[:, ti * 3 + 2:ti * 3 + 3]
            P.op('act', lambda e, ti=ti, ldt=ldt: e.activation(C(ti, 0), ldt, AF.Exp), reads=[dsp], writes=[dcol])
            P.op('dve', lambda e, ti=ti, are=are: e.tensor_tensor(C(ti, 1), are, C(ti, 0), ALU.mult), reads=[dsp, dcol], writes=[dcol])
            P.op('dve', lambda e, ti=ti, aim=aim: e.tensor_tensor(C(ti, 2), aim, C(ti, 0), ALU.mult), reads=[dsp, dcol], writes=[dcol])
            P.op('act', lambda e, ti=ti: e.activation(C(ti, 3), C(ti, 1), AF.Exp), reads=[dcol], writes=[dcol])
            P.op('dve', lambda e, ti=ti: e.tensor_scalar(C(ti, 5), C(ti, 2), 1.0 / (2 * np.pi), None, ALU.mult), reads=[dcol], writes=[dcol])
            P.op('dve', lambda e, ti=ti: e.tensor_copy(ci[:, ti:ti + 1], C(ti, 5)), reads=[dcol], writes=[dci])
            P.op('dve', lambda e, ti=ti: e.tensor_copy(C(ti, 5), ci[:, ti:ti + 1]), reads=[dci], writes=[dcol])
            P.op('dve', lambda e, ti=ti: e.scalar_tensor_tensor(C(ti, 4), C(ti, 5), -TWO_PI_HI, C(ti, 2), ALU.mult, ALU.add), reads=[dcol], writes=[dcol])
            P.op('dve', lambda e, ti=ti: e.scalar_tensor_tensor(C(ti, 4), C(ti, 5), -TWO_PI_LO, C(ti, 4), ALU.mult, ALU.add), reads=[dcol], writes=[dcol])
            P.op('dve', lambda e, ti=ti: e.tensor_scalar(angt[:], tau[:], C(ti, 4), None, ALU.mult), reads=[dtau, dcol], writes=[dangt])
            sin_reduced(P, sinT[:, ti, :], angt[:], None, [dangt], dsin[ti], tmp[:], tmpi[:], dtmp, dtmpi, 0.0)
            sin_reduced(P, cosT[:, ti, :], angt[:], None, [dangt], dcos[ti], tmp[:], tmpi[:], dtmp, dtmpi, float(np.pi / 2))
            P.op('dve', lambda e, ti=ti: e.tensor_scalar(C(ti, 6), cosT[:, ti, 0:1], C(ti, 3), -1.0, ALU.mult, ALU.add), reads=[dcos[ti], dcol], writes=[dcol])
            P.op('dve', lambda e, ti=ti: e.tensor_tensor(C(ti, 7), sinT[:, ti, 0:1], C(ti, 3), ALU.mult), reads=[dsin[ti], dcol], writes=[dcol])
            P.op('dve', lambda e, ti=ti, are=are: e.tensor_tensor(C(ti, 8), are, are, ALU.mult), reads=[dsp], writes=[dcol])
            P.op('dve', lambda e, ti=ti, aim=aim: e.scalar_tensor_tensor(C(ti, 8), aim, aim, C(ti, 8), ALU.mult, ALU.add), reads=[dsp, dcol], writes=[dcol])
            P.op('dve', lambda e, ti=ti: e.reciprocal(C(ti, 8), C(ti, 8)), reads=[dcol], writes=[dcol])
            P.op('dve', lambda e, ti=ti, are=are: e.tensor_tensor(C(ti, 11), C(ti, 6), are, ALU.mult), reads=[dsp, dcol], writes=[dcol])
            P.op('dve', lambda e, ti=ti, aim=aim: e.scalar_tensor_tensor(C(ti, 11), C(ti, 7), aim, C(ti, 11), ALU.mult, ALU.add), reads=[dsp, dcol], writes=[dcol])
            P.op('dve', lambda e, ti=ti: e.tensor_tensor(C(ti, 9), C(ti, 11), C(ti, 8), ALU.mult), reads=[dcol], writes=[dcol])
            P.op('dve', lambda e, ti=ti, aim=aim: e.tensor_tensor(C(ti, 12), C(ti, 6), aim, ALU.mult), reads=[dsp, dcol], writes=[dcol])
            P.op('dve', lambda e, ti=ti, are=are: e.scalar_tensor_tensor(C(ti, 12), C(ti, 7), are, C(ti, 12), ALU.mult, ALU.subtract), reads=[dsp, dcol], writes=[dcol])
            P.op('dve', lambda e, ti=ti: e.tensor_tensor(C(ti, 10), C(ti, 12), C(ti, 8), ALU.mult), reads=[dcol], writes=[dcol])
            if ti == 0:
                P.op('pool', lambda e: e.memset(bbp[:], 0.0), writes=[dbbp])
            P.op('dve', lambda e, ti=ti: e.tensor_scalar(tb[:], bf[:, ti, 1, :], C(ti, 10), None, ALU.mult), reads=[dbf, dcol], writes=[dtb])
            P.op('dve', lambda e, ti=ti: e.scalar_tensor_tensor(bbp[:, 0, ti * 32:(ti + 1) * 32], bf[:, ti, 0, :], C(ti, 9), tb[:], ALU.mult, ALU.subtract),
                 reads=[dbf, dcol, dtb], writes=[dbbp])
            P.op('dve', lambda e, ti=ti: e.tensor_scalar(tb[:], bf[:, ti, 0, :], C(ti, 10), None, ALU.mult), reads=[dbf, dcol, dbbp], writes=[dtb])
            P.op('dve', lambda e, ti=ti: e.scalar_tensor_tensor(bbp[:, 1, ti * 32:(ti + 1) * 32], bf[:, ti, 1, :], C(ti, 9), tb[:], ALU.mult, ALU.add),
                 reads=[dbf, dcol, dtb], writes=[dbbp])
        NOTE = """bbp[:, c, ti*32 + gc] holds tile ti's bb at that tile's state partitions; tiles differ in partition meaning,
        but transposing the [128,128] pad gives rows ti*32+gc = lhsT rows for tile ti (cols = that tile's states)."""
        ps = Ring(P, "ps", [128, 512], F32, 2, psum=True)
        psy = Ring(P, "psy", [128, 512], F32, 2, psum=True)
        psb = Ring(P, "psb", [128, 512], F32, 4, psum=True)
        for c in range(2):
            pt, dpt = ps.next()
            P.op('pe', lambda e, pt=pt, c=c: e.transpose(pt[:, 0:128], bbp[:, c, :], idf[:]), reads=[dbbp, did], writes=[dpt])
            P.op('act', lambda e, pt=pt, c=c: e.copy(bbT[:, c, :], pt[:, 0:128]), reads=[dpt], writes=[dbbT])
            P.op('act', lambda e, pt=pt, c=c: e.copy(bbT3[:, c, :], pt[:, 0:128]), reads=[dpt], writes=[dbbT3])
        P.op('dve', lambda e: e.memset(bbT3[64:96, :, :], 0.0), reads=[dbbT3], writes=[dbbT3])
        xr = Ring(P, "xr", [128, 516], F32, 3)
        ur = Ring(P, "ur", [128, 512], F32, 3)
        W = lambda nm, n=2: Ring(P, nm, [128, 512], F32, n)
        xc_r, r_r, i_r, a_r, m_r, b_r, h_r = W("xc"), W("rr"), W("ir"), W("ar"), W("mr"), W("br"), W("hr")
        t_r = [W(f"t{k}") for k in range(4)]
        bpr_r, bpi_r, gr_r, gi_r = W("bpr"), W("bpi"), W("gr"), W("gi")
        hsr = [W(f"hsr{ti}") for ti in range(4)]
        hsi = [W(f"hsi{ti}") for ti in range(4)]
        yev = Ring(P, "yev", [64, 512], F32, 2)
        prev_h = None
        prev_s = [None] * 4
        for (t0, n) in SBLOCKS:
            off = 2 if t0 < CTX else 6
            xt, dx = xr.next()
            P.dma('sp', xt[:, 0:n + 4], xa2[:, t0 + off - 2:t0 + off + n + 2], writes=[dx])
            ut, du = ur.next()
            P.dma('act', ut[:, 0:n], ub2[:, t0:t0 + n], writes=[du])
            xc, dxc = xc_r.next()
            P.op('dve', lambda e, xc=xc, xt=xt, n=n: e.tensor_scalar(xc[:, 0:n], xt[:, 0:n], lr[:, 0:1], lr[:, 5:6], ALU.mult, ALU.add),
                 reads=[dx, dlr], writes=[dxc])
            for oi in range(1, 5):
                P.op('dve', lambda e, xc=xc, xt=xt, n=n, oi=oi: e.scalar_tensor_tensor(
                    xc[:, 0:n], xt[:, oi:oi + n], lr[:, oi:oi + 1], xc[:, 0:n], ALU.mult, ALU.add), reads=[dx, dlr, dxc], writes=[dxc])
            pr, dpr = ps.next()
            pi_, dpi = ps.next()
            P.op('pe', lambda e, pr=pr, xc=xc, n=n: e.matmul(pr[:, 0:n], wr[:], xc[:, 0:n], start=True, stop=True), reads=[dwr, dxc], writes=[dpr])
            P.op('pe', lambda e, pi_=pi_, xc=xc, n=n: e.matmul(pi_[:, 0:n], wi[:], xc[:, 0:n], start=True, stop=True), reads=[dwi, dxc], writes=[dpi])
            rt, drt = r_r.next(); it, dit = i_r.next(); at, dat = a_r.next(); mt, dmt = m_r.next(); bt, dbt = b_r.next(); ht, dht = h_r.next()
            P.op('act', lambda e, rt=rt, pr=pr, n=n: e.activation(rt[:, 0:n], pr[:, 0:n], AF.Sigmoid, bias=lr[:, 6:7]), reads=[dpr, dlr], writes=[drt])
            P.op('act', lambda e, it=it, pi_=pi_, n=n: e.activation(it[:, 0:n], pi_[:, 0:n], AF.Sigmoid, bias=lr[:, 7:8]), reads=[dpi, dlr], writes=[dit])
            P.op('act', lambda e, at=at, rt=rt, n=n: e.activation(at[:, 0:n], rt[:, 0:n], AF.Exp, scale=c8[:, 0:1]), reads=[drt, dc8], writes=[dat])
            P.op('pool', lambda e, mt=mt, at=at, n=n: e.tensor_tensor(mt[:, 0:n], at[:, 0:n], at[:, 0:n], ALU.mult), reads=[dat], writes=[dmt])
            P.op('act', lambda e, mt=mt, n=n: e.activation(mt[:, 0:n], mt[:, 0:n], AF.Sqrt, scale=-1.0, bias=1.0), reads=[dmt], writes=[dmt])
            P.op('pool', lambda e, bt=bt, it=it, xc=xc, n=n: e.tensor_tensor(bt[:, 0:n], it[:, 0:n], xc[:, 0:n], ALU.mult), reads=[dit, dxc], writes=[dbt])
            P.op('pool', lambda e, bt=bt, mt=mt, n=n: e.tensor_tensor(bt[:, 0:n], bt[:, 0:n], mt[:, 0:n], ALU.mult), reads=[dbt, dmt], writes=[dbt])
            if prev_h is None:
                P.op('dve', lambda e, ht=ht, at=at, bt=bt, n=n: e.tensor_tensor_scan(ht[:, 0:n], at[:, 0:n], bt[:, 0:n], 0.0, ALU.mult, ALU.add),
                     reads=[dat, dbt], writes=[dht])
            else:
                ph, dph, pn = prev_h
                P.op('dve', lambda e, ht=ht, at=at, bt=bt, n=n, ph=ph, pn=pn: e.tensor_tensor_scan(
                    ht[:, 0:n], at[:, 0:n], bt[:, 0:n], ph[:, pn - 1:pn], ALU.mult, ALU.add), reads=[dat, dbt, dph], writes=[dht])
            prev_h = (ht, dht, n)
            P.dma('sp', h_o[:, t0:t0 + n], ht[:, 0:n], reads=[dht])
            py = [psy.next(), psy.next()]
            for ti in range(4):
                d = ti // 2
                pbr, dpbr = psb.next()
                pbi, dpbi = psb.next()
                rows = slice(ti * 32, (ti + 1) * 32) if ti < 3 else slice(64, 128)
                bsrc, dbsrc = (bbT, dbbT) if ti < 3 else (bbT3, dbbT3)
                P.op('pe', lambda e, pbr=pbr, ut=ut, rows=rows, n=n, bsrc=bsrc: e.matmul(pbr[:, 0:n], bsrc[rows, 0, :], ut[rows, 0:n], start=True, stop=True),
                     reads=[dbsrc, du], writes=[dpbr])
                P.op('pe', lambda e, pbi=pbi, ut=ut, rows=rows, n=n, bsrc=bsrc: e.matmul(pbi[:, 0:n], bsrc[rows, 1, :], ut[rows, 0:n], start=True, stop=True),
                     reads=[dbsrc, du], writes=[dpbi])
                cs, sn = cosT[:, ti, 0:n], sinT[:, ti, 0:n]
                (t1, d1), (t2, d2), (t3, d3), (t4, d4) = [r_.next() for r_ in t_r]
                P.op('dve', lambda e, t1=t1, pbr=pbr, cs=cs, n=n: e.tensor_tensor(t1[:, 0:n], pbr[:, 0:n], cs, ALU.mult), reads=[dpbr, dcos[ti]], writes=[d1])
                P.op('dve', lambda e, t2=t2, pbi=pbi, sn=sn, n=n: e.tensor_tensor(t2[:, 0:n], pbi[:, 0:n], sn, ALU.mult), reads=[dpbi, dsin[ti]], writes=[d2])
                P.op('dve', lambda e, t3=t3, pbi=pbi, cs=cs, n=n: e.tensor_tensor(t3[:, 0:n], pbi[:, 0:n], cs, ALU.mult), reads=[dpbi, dcos[ti]], writes=[d3])
                P.op('dve', lambda e, t4=t4, pbr=pbr, sn=sn, n=n: e.tensor_tensor(t4[:, 0:n], pbr[:, 0:n], sn, ALU.mult), reads=[dpbr, dsin[ti]], writes=[d4])
                bpr, dbpr = bpr_r.next(); bpi, dbpi = bpi_r.next()
                P.op('pool', lambda e, bpr=bpr, t1=t1, t2=t2, n=n: e.tensor_tensor(bpr[:, 0:n], t1[:, 0:n], t2[:, 0:n], ALU.add), reads=[d1, d2], writes=[dbpr])
                P.op('pool', lambda e, bpi=bpi, t3=t3, t4=t4, n=n: e.tensor_tensor(bpi[:, 0:n], t3[:, 0:n], t4[:, 0:n], ALU.subtract), reads=[d3, d4], writes=[dbpi])
                gr, dgr = gr_r.next(); gi, dgi = gi_r.next()
                rho = col[:, ti, 3:4].to_broadcast([128, n])
                if prev_s[ti] is None:
                    P.op('dve', lambda e, gr=gr, bpr=bpr, rho=rho, n=n: e.tensor_tensor_scan(gr[:, 0:n], rho, bpr[:, 0:n], 0.0, ALU.mult, ALU.add),
                         reads=[dcol, dbpr], writes=[dgr])
                    P.op('dve', lambda e, gi=gi, bpi=bpi, rho=rho, n=n: e.tensor_tensor_scan(gi[:, 0:n], rho, bpi[:, 0:n], 0.0, ALU.mult, ALU.add),
                         reads=[dcol, dbpi], writes=[dgi])
                else:
                    phr, dphr, phi, dphi, pn = prev_s[ti]
                    P.op('dve', lambda e, gr=gr, bpr=bpr, rho=rho, n=n, phr=phr, pn=pn: e.tensor_tensor_scan(
                        gr[:, 0:n], rho, bpr[:, 0:n], phr[:, pn - 1:pn], ALU.mult, ALU.add), reads=[dcol, dbpr, dphr], writes=[dgr])
                    P.op('dve', lambda e, gi=gi, bpi=bpi, rho=rho, n=n, phi=phi, pn=pn: e.tensor_tensor_scan(
                        gi[:, 0:n], rho, bpi[:, 0:n], phi[:, pn - 1:pn], ALU.mult, ALU.add), reads=[dcol, dbpi, dphi], writes=[dgi])
                (u1, e1), (u2, e2), (u3, e3), (u4, e4) = [r_.next() for r_ in t_r]
                P.op('pool', lambda e, u1=u1, gr=gr, cs=cs, n=n: e.tensor_tensor(u1[:, 0:n], gr[:, 0:n], cs, ALU.mult), reads=[dgr, dcos[ti]], writes=[e1])
                P.op('pool', lambda e, u2=u2, gi=gi, sn=sn, n=n: e.tensor_tensor(u2[:, 0:n], gi[:, 0:n], sn, ALU.mult), reads=[dgi, dsin[ti]], writes=[e2])
                P.op('pool', lambda e, u3=u3, gr=gr, sn=sn, n=n: e.tensor_tensor(u3[:, 0:n], gr[:, 0:n], sn, ALU.mult), reads=[dgr, dsin[ti]], writes=[e3])
                P.op('pool', lambda e, u4=u4, gi=gi, cs=cs, n=n: e.tensor_tensor(u4[:, 0:n], gi[:, 0:n], cs, ALU.mult), reads=[dgi, dcos[ti]], writes=[e4])
                hr_, dhr = hsr[ti].next(); hi_, dhi = hsi[ti].next()
                P.op('dve', lambda e, hr_=hr_, u1=u1, u2=u2, n=n: e.tensor_tensor(hr_[:, 0:n], u1[:, 0:n], u2[:, 0:n], ALU.subtract), reads=[e1, e2], writes=[dhr])
                P.op('dve', lambda e, hi_=hi_, u3=u3, u4=u4, n=n: e.tensor_tensor(hi_[:, 0:n], u3[:, 0:n], u4[:, 0:n], ALU.add), reads=[e3, e4], writes=[dhi])
                prev_s[ti] = (hr_, dhr, hi_, dhi, n)
                yp, dyp = py[d]
                first = (ti % 2 == 0)
                P.op('pe', lambda e, yp=yp, hr_=hr_, ti=ti, n=n, first=first: e.matmul(yp[0:64, 0:n], ct[:, ti, 0, :], hr_[:, 0:n], start=first, stop=False),
                     reads=[dct, dhr], writes=[dyp], pe_acc=True)
                P.op('pe', lambda e, yp=yp, hi_=hi_, ti=ti, n=n, first=first: e.matmul(yp[0:64, 0:n], ct[:, ti, 1, :], hi_[:, 0:n], start=False, stop=(not first)),
                     reads=[dct, dhi], writes=[dyp], pe_acc=True)
                if not first:
                    yt, dyt = yev.next()
                    P.op('act', lambda e, yt=yt, yp=yp, n=n: e.copy(yt[:, 0:n], yp[0:64, 0:n]), reads=[dyp], writes=[dyt])
                    P.dma('act', y_o[d * 64:(d + 1) * 64, t0:t0 + n], yt[:, 0:n], reads=[dyt])
        env.end(P)
    return nc


def host_B(inp, l, xguT_full):
    rev = np.concatenate([np.arange(CTX)[::-1], CTX + np.arange(SEQ)[::-1]])
    maps = []
    cw = inp['conv_w'][l]
    for j in range(NCORE):
        ch = slice(64 * j, 64 * j + 64)
        xa = xguT_full[0:512][ch]
        ub = xguT_full[1024:1536][ch]
        xa2 = np.zeros((128, XPAD), np.float32)
        for d, src in enumerate((xa, xa[:, rev])):
            xa2[d * 64:(d + 1) * 64, 2:2 + CTX] = src[:, :CTX]
            xa2[d * 64:(d + 1) * 64, 6 + CTX:6 + TT] = src[:, CTX:]
        ub2 = np.concatenate([ub, ub[:, rev]], 0)
        lru = np.zeros((128, 9), np.float32)
        lru[0:64, 0:4] = cw[:, ch].T
        lru[64:128, 1:5] = cw[::-1, ch].T
        lru[:, 5] = np.tile(inp['conv_b'][l][ch], 2)
        for d in range(2):
            lru[d * 64:(d + 1) * 64, 6] = inp['lru_br'][l, d, ch]
            lru[d * 64:(d + 1) * 64, 7] = inp['lru_bi'][l, d, ch]
            lru[d * 64:(d + 1) * 64, 8] = inp['lru_lam'][l, d, ch]
        wrbd = np.zeros((128, 128), np.float32); wibd = np.zeros((128, 128), np.float32)
        for d in range(2):
            wrbd[d * 64:(d + 1) * 64, d * 64:(d + 1) * 64] = inp['lru_wr'][l, d, j]
            wibd[d * 64:(d + 1) * 64, d * 64:(d + 1) * 64] = inp['lru_wi'][l, d, j]
        s5par = np.zeros((128, 12), np.float32)
        bfull = np.zeros((128, 4, 2, 32), np.float32)
        cT = np.zeros((128, 4, 2, 64), np.float32)
        for ti in range(4):
            d, pr = ti // 2, ti % 2
            for gl in range(2):
                g = 4 * j + pr * 2 + gl
                rows = slice(gl * 64, gl * 64 + 64)
                s5par[rows, ti * 3 + 0] = inp['s5_a_re'][l, d, g]
                s5par[rows, ti * 3 + 1] = inp['s5_a_im'][l, d, g]
                s5par[rows, ti * 3 + 2] = inp['s5_log_dt'][l, d, g]
                bfull[rows, ti, 0, gl * 16:(gl + 1) * 16] = inp['s5_b_re'][l, d, g]
                bfull[rows, ti, 1, gl * 16:(gl + 1) * 16] = inp['s5_b_im'][l, d, g]
                cc = pr * 32 + gl * 16
                cT[rows, ti, 0, cc:cc + 16] = inp['s5_c_re'][l, d, g].T
                cT[rows, ti, 1, cc:cc + 16] = inp['s5_c_im'][l, d, g].T
        maps.append(dict(xa2=xa2, ub2=np.ascontiguousarray(ub2), lru=lru, wrbd=wrbd, wibd=wibd, s5par=s5par,
                         bfull=bfull.reshape(128, -1), cT=cT.reshape(128, -1)))
    return maps


NKT = TT // 128


def build_C1(env=None):
    env = env or SoloEnv()
    nc = env.nc
    dt_in, dt_out = env.dt_in, env.dt_out
    qT = dt_in("qT", [512, TOK], BF16)
    kT = dt_in("kT", [512, TT], BF16)
    vv = dt_in("v", [TT, 512], BF16)
    dlam = dt_in("dlam", [1, 256])
    aux = dt_in("aux", [128, 3])
    yc_o = dt_out("ycT", [512, TOK], BF16)
    with env.scope() as P:
        q = P.sbuf("q", [128, 4, TOK], BF16); dq = Dep("q")
        P.dma('sp', q[:], qT.rearrange("(h p) t -> p h t", p=128), writes=[dq])
        ax = P.sbuf("ax", [128, 3]); dax = Dep("ax")
        P.dma('sp', ax[:], aux, writes=[dax])
        dl = P.sbuf("dl", [128, 256]); ddl = Dep("dl")
        P.dma('sp', dl[:], dlam.partition_broadcast(128), writes=[ddl])
        sc = P.sbuf("sc", [128, 8]); dsc = Dep("sc")
        pr = P.sbuf("pr", [128, 128]); dpr = Dep("pr")
        P.op('dve', lambda e: e.tensor_tensor(pr[:, 0:64], dl[:, 0:64], dl[:, 64:128], ALU.mult), reads=[ddl], writes=[dpr])
        P.op('dve', lambda e: e.tensor_tensor(pr[:, 64:128], dl[:, 128:192], dl[:, 192:256], ALU.mult), reads=[ddl, dpr], writes=[dpr])
        P.op('dve', lambda e: e.reduce_sum(sc[:, 0:1], pr[:, 0:64], AX.X), reads=[dpr], writes=[dsc])
        P.op('dve', lambda e: e.reduce_sum(sc[:, 1:2], pr[:, 64:128], AX.X), reads=[dpr, dsc], writes=[dsc])
        P.op('act', lambda e: e.activation(sc[:, 2:4], sc[:, 0:2], AF.Exp), reads=[dsc], writes=[dsc])
        P.op('dve', lambda e: e.tensor_tensor(sc[:, 4:5], sc[:, 3:4], sc[:, 2:3], ALU.subtract), reads=[dsc], writes=[dsc])
        P.op('dve', lambda e: e.tensor_tensor(sc[:, 4:5], sc[:, 4:5], ax[:, 1:2], ALU.subtract), reads=[dsc, dax], writes=[dsc])
        P.op('dve', lambda e: e.tensor_tensor(sc[:, 5:6], ax[:, 0:1], ax[:, 2:3], ALU.mult), reads=[dsc, dax], writes=[dsc])
        ones_b = P.sbuf("ones_b", [128, 128], BF16); dob = Dep("ones_b")
        ones_f = P.sbuf("ones_f", [128, 128]); dof = Dep("ones_f")
        P.op('pool', lambda e: e.memset(ones_b[:], 1.0), writes=[dob])
        P.op('pool', lambda e: e.memset(ones_f[:], 1.0 / 128.0), writes=[dof])
        kh = P.sbuf("kh", [128, TT], BF16); dkh = Dep("kh")
        vh = P.sbuf("vh", [128, NKT, 128], BF16); dvh = Dep("vh")
        pss = Ring(P, "pss", [128, 512], F32, 4, psum=True)
        pso = [Ring(P, f"pso{m}", [128, 512], F32, 1, psum=True) for m in range(2)]
        psl = [Ring(P, f"psl{m}", [128, 512], F32, 1, psum=True) for m in range(2)]
        pt_r = Ring(P, "pt", [128, 512], BF16, 6)
        W = lambda nm, n=2, d=F32: Ring(P, nm, [128, 512], d, n)
        r_r, o0_r, o1_r, sq_r, rs_r, yc_r = W("rr"), W("o0"), W("o1"), W("sq"), W("rs"), W("yc", 2, BF16)
        kT_v = kT.rearrange("(h p) t -> h p t", p=128)
        v_v = vv.rearrange("(kt p) (h e) -> h p kt e", p=128, e=128)
        for h in range(4):
            for part in range(4):
                c0, c1 = part * (TT // 4), (part + 1) * (TT // 4)
                P.dma('sp', kh[:, c0:c1], kT_v[h, :, c0:c1], writes=[dkh], accum=(part > 0))
            for part in range(5):
                k0, k1 = part * 26, (part + 1) * 26
                P.dma('act', vh[:, k0:k1, :], v_v[h, :, k0:k1, :], writes=[dvh], accum=(part > 0))
            for (t0, n) in BLOCKS:
                nkt = 2 if t0 == 0 else NKT
                po = [pso[m].next() for m in range(2)]
                pl = [psl[m].next() for m in range(2)]
                for kt in range(nkt):
                    for m in range(2):
                        ps_, dps = pss.next()
                        rows = slice(m * 64, (m + 1) * 64)
                        P.op('pe', lambda e, ps_=ps_, rows=rows, kt=kt, h=h, t0=t0, n=n: e.matmul(
                            ps_[:, 0:n], kh[rows, kt * 128:(kt + 1) * 128], q[rows, h, t0:t0 + n], start=True, stop=True),
                            reads=[dkh, dq], writes=[dps])
                        pt, dpt = pt_r.next()
                        P.op('act', lambda e, pt=pt, ps_=ps_, n=n: e.activation(pt[:, 0:n], ps_[:, 0:n], AF.Exp), reads=[dps], writes=[dpt])
                        P.op('pe', lambda e, m=m, pt=pt, kt=kt, n=n, po=po, nkt=nkt: e.matmul(
                            po[m][0][:, 0:n], vh[:, kt, :], pt[:, 0:n], start=(kt == 0), stop=(kt == nkt - 1)),
                            reads=[dvh, dpt], writes=[po[m][1]], pe_acc=True)
                        P.op('pe', lambda e, m=m, pt=pt, kt=kt, n=n, pl=pl, nkt=nkt: e.matmul(
                            pl[m][0][:, 0:n], ones_b[:], pt[:, 0:n], start=(kt == 0), stop=(kt == nkt - 1)),
                            reads=[dob, dpt], writes=[pl[m][1]], pe_acc=True)
                os_ = []
                for m, orr in ((0, o0_r), (1, o1_r)):
                    rt, drt = r_r.next()
                    P.op('dve', lambda e, rt=rt, m=m, n=n, pl=pl: e.reciprocal(rt[:, 0:n], pl[m][0][:, 0:n]), reads=[pl[m][1]], writes=[drt])
                    ot, dot = orr.next()
                    P.op('dve', lambda e, ot=ot, rt=rt, m=m, n=n, po=po: e.tensor_tensor(ot[:, 0:n], po[m][0][:, 0:n], rt[:, 0:n], ALU.mult),
                         reads=[po[m][1], drt], writes=[dot])
                    os_.append((ot, dot))
                (o0, do0), (o1, do1) = os_
                P.op('dve', lambda e, o0=o0, o1=o1, n=n: e.scalar_tensor_tensor(o0[:, 0:n], o1[:, 0:n], sc[:, 4:5], o0[:, 0:n], ALU.mult, ALU.add),
                     reads=[do0, do1, dsc], writes=[do0])
                sq, dsq = sq_r.next()
                P.op('pool', lambda e, sq=sq, o0=o0, n=n: e.tensor_tensor(sq[:, 0:n], o0[:, 0:n], o0[:, 0:n], ALU.mult), reads=[do0], writes=[dsq])
                pm, dpm = pss.next()
                P.op('pe', lambda e, pm=pm, sq=sq, n=n: e.matmul(pm[:, 0:n], ones_f[:], sq[:, 0:n], start=True, stop=True), reads=[dof, dsq], writes=[dpm])
                rs, drs = rs_r.next()
                P.op('act', lambda e, rs=rs, pm=pm, n=n: e.activation(rs[:, 0:n], pm[:, 0:n], AF.Sqrt, bias=RMS_EPS), reads=[dpm], writes=[drs])
                P.op('dve', lambda e, rs=rs, n=n: e.reciprocal(rs[:, 0:n], rs[:, 0:n]), reads=[drs], writes=[drs])
                yc, dyc = yc_r.next()
                P.op('dve', lambda e, yc=yc, o0=o0, rs=rs, n=n: e.scalar_tensor_tensor(yc[:, 0:n], o0[:, 0:n], sc[:, 5:6], rs[:, 0:n], ALU.mult, ALU.mult),
                     reads=[do0, drs, dsc], writes=[dyc])
                P.dma('sp', yc_o[h * 128:(h + 1) * 128, t0:t0 + n], yc[:, 0:n], reads=[dyc])
        env.end(P)
    return nc


def host_C1(inp, l, qk_full, v_full):
    lam_init = 0.8 - 0.6 * float(np.exp(-0.3 * l))
    aux = np.stack([inp['da_subln_g'][l], np.full(128, lam_init, np.float32), np.full(128, 1.0 - lam_init, np.float32)], -1).astype(np.float32)
    dlam = np.ascontiguousarray(inp['da_lam'][l].reshape(1, 256))
    kT = np.ascontiguousarray(qk_full[512:])
    maps = []
    for c in range(NCORE):
        cols = np.concatenate([np.arange(CTX), CTX + c * LAT_PC + np.arange(LAT_PC)])
        maps.append(dict(qT=np.ascontiguousarray(qk_full[:512][:, cols]), kT=kT, v=v_full, dlam=dlam, aux=aux))
    return maps


def ln_fm(P, X, dX, n, onesf, dof, gcol, bcol, dpar, out, dout, ps_r, tmp):
    pm, dpm = ps_r.next()
    for k in range(8):
        P.op('pe', lambda e, k=k, pm=pm: e.matmul(pm[:, 0:n], onesf[:], X[:, k, 0:n], start=(k == 0), stop=(k == 7)),
             reads=[dof, dX], writes=[dpm], pe_acc=True)
    pq, dpq = ps_r.next()
    for k in range(8):
        sq, dsq = tmp['sq'].next()
        P.op('act', lambda e, k=k, sq=sq: e.activation(sq[:, 0:n], X[:, k, 0:n], AF.Square), reads=[dX], writes=[dsq])
        P.op('pe', lambda e, k=k, pq=pq, sq=sq: e.matmul(pq[:, 0:n], onesf[:], sq[:, 0:n], start=(k == 0), stop=(k == 7)),
             reads=[dof, dsq], writes=[dpq], pe_acc=True)
    mean, dmean = tmp['mean'].next()
    var, dvar = tmp['var'].next()
    P.op('act', lambda e: e.copy(mean[:, 0:n], pm[:, 0:n]), reads=[dpm], writes=[dmean])
    P.op('dve', lambda e: e.tensor_tensor(var[:, 0:n], mean[:, 0:n], mean[:, 0:n], ALU.mult), reads=[dmean], writes=[dvar])
    P.op('dve', lambda e: e.tensor_tensor(var[:, 0:n], pq[:, 0:n], var[:, 0:n], ALU.subtract), reads=[dpq, dvar], writes=[dvar])
    P.op('act', lambda e: e.activation(var[:, 0:n], var[:, 0:n], AF.Sqrt, bias=LN_EPS), reads=[dvar], writes=[dvar])
    P.op('dve', lambda e: e.reciprocal(var[:, 0:n], var[:, 0:n]), reads=[dvar], writes=[dvar])
    for k in range(8):
        t, dt_ = tmp['t'].next()
        P.op('dve', lambda e, k=k, t=t: e.tensor_tensor(t[:, 0:n], X[:, k, 0:n], mean[:, 0:n], ALU.subtract), reads=[dX, dmean], writes=[dt_])
        P.op('pool', lambda e, t=t: e.tensor_tensor(t[:, 0:n], t[:, 0:n], var[:, 0:n], ALU.mult), reads=[dt_, dvar], writes=[dt_])
        P.op('act', lambda e, k=k, t=t: e.activation(out[:, k, 0:n], t[:, 0:n], AF.Identity, scale=gcol(k), bias=bcol(k)),
             reads=[dt_, dpar], writes=[dout], accum=(k > 0))


def gelu_fm(P, out, dout, x, dx, n, tmp):
    a, da = tmp['g1'].next()
    b, db = tmp['g2'].next()
    P.op('act', lambda e: e.activation(a[:, 0:n], x, AF.Square), reads=[dx], writes=[da])
    P.op('dve', lambda e: e.tensor_scalar(a[:, 0:n], a[:, 0:n], 0.044715, 1.0, ALU.mult, ALU.add), reads=[da], writes=[da])
    P.op('dve', lambda e: e.tensor_tensor(a[:, 0:n], a[:, 0:n], x, ALU.mult), reads=[da, dx], writes=[da])
    P.op('act', lambda e: e.activation(b[:, 0:n], a[:, 0:n], AF.Sigmoid, scale=1.5957691216057308), reads=[da], writes=[db])
    P.op('pool', lambda e: e.tensor_tensor(out, b[:, 0:n], x, ALU.mult), reads=[db, dx], writes=[dout])


C2BLK = [(i * 256, 256) for i in range(TOK // 256)]


def build_C2(env=None):
    env = env or SoloEnv()
    nc = env.nc
    dt_in, dt_out = env.dt_in, env.dt_out
    xT = dt_in("xT", [1024, TOK])
    modT_i = dt_in("modT", [128, 96])
    br6 = dt_in("br6", [6, 512, TOK])
    ycT = dt_in("ycT", [512, TOK], BF16)
    w_gate = dt_in("w_gate", [1024, 3072])
    w_proj = dt_in("w_proj", [3, 512, 1024])
    w_out = dt_in("w_out", [1024, 1024])
    w_glu = dt_in("w_glu", [512, 1024])
    cols_i = dt_in("cols", [128, 64])
    w_rt = dt_in("w_rt", [1024, 32])
    b_rt = dt_in("b_rt", [1, 32])
    acc_o = dt_out("accT", [1024, TOK])
    v_o = dt_out("vT", [1024, TOK], BF16)
    g_o = dt_out("gateT", [32, TOK])
    with env.scope() as P:
        modT = P.sbuf("modTs2", [128, 96]); dmod = Dep("mod")
        P.dma('sp', modT[:], modT_i, writes=[dmod])
        cl = P.sbuf("cl", [128, 64]); dcl = Dep("cl")
        P.dma('sp', cl[:], cols_i, writes=[dcl])
        brt = P.sbuf("brt", [128, 32]); dbrt = Dep("brt")
        P.dma('sp', brt[:], b_rt.partition_broadcast(128), writes=[dbrt])
        wrt = P.sbuf("wrt", [128, 8, 32]); dwrt = Dep("wrt")
        P.dma('sp', wrt[:], w_rt.rearrange("(k p) e -> p k e", p=128), writes=[dwrt])
        scp = P.sbuf("scp", [128, 32]); dscp = Dep("scp")
        P.op('dve', lambda e: e.tensor_scalar(scp[:, 0:16], modT[:, 16:32], 1.0, None, ALU.add), reads=[dmod], writes=[dscp])
        P.op('dve', lambda e: e.tensor_scalar(scp[:, 16:32], modT[:, 64:80], 1.0, None, ALU.add), reads=[dmod, dscp], writes=[dscp])
        onesf = P.sbuf("onesf", [128, 128]); dof = Dep("onesf")
        P.op('pool', lambda e: e.memset(onesf[:], 1.0 / 1024.0), writes=[dof])
        idf = P.sbuf("idf", [128, 128]); did = Dep("id")
        ii = P.sbuf("ii", [128, 128], I32); dii = Dep("ii")
        cf = P.sbuf("cf", [128, 128]); dcf = Dep("cf")
        pf = P.sbuf("pf", [128, 128]); dpf = Dep("pf")
        P.op('pool', lambda e: e.iota(ii[:], pattern=[[1, 128]], base=0, channel_multiplier=0), writes=[dii])
        P.op('dve', lambda e: e.tensor_copy(cf[:], ii[:]), reads=[dii], writes=[dcf])
        P.op('pool', lambda e: e.iota(ii[:], pattern=[[0, 128]], base=0, channel_multiplier=1), reads=[dcf], writes=[dii])
        P.op('dve', lambda e: e.tensor_copy(pf[:], ii[:]), reads=[dii], writes=[dpf])
        P.op('dve', lambda e: e.tensor_tensor(idf[:], cf[:], pf[:], ALU.is_equal), reads=[dcf, dpf], writes=[did])
        wg = P.sbuf("wg", [128, 8, 3072], BF16); dwg = [Dep(f"wg{k}") for k in range(8)]
        wp = P.sbuf("wp", [128, 3, 4, 1024], BF16); dwp = Dep("wp")
        wo = P.sbuf("wo", [128, 8, 1024], BF16); dwo = Dep("wo")
        wl = P.sbuf("wl", [128, 4, 1024], BF16); dwl = Dep("wl")
        w_gate_v = w_gate.rearrange("(k p) j -> p k j", p=128)
        for k in range(8):
            for hh in range(2):
                P.dma('pool', wg[:, k, hh * 1536:(hh + 1) * 1536], w_gate_v[:, k, hh * 1536:(hh + 1) * 1536], writes=[dwg[k]], accum=True)
        for i in range(3):
            for c in range(4):
                P.dma('pool', wp[:, i, c, :], w_proj[i, c * 128:(c + 1) * 128, :], writes=[dwp], accum=True)
        for k in range(8):
            P.dma('pool', wo[:, k, :], w_out[k * 128:(k + 1) * 128, :], writes=[dwo], accum=True)
        for c in range(4):
            P.dma('pool', wl[:, c, :], w_glu[c * 128:(c + 1) * 128, :], writes=[dwl], accum=True)
        NB = 256
        R = lambda nm, n=2, d=F32, w=NB: Ring(P, nm, [128, w], d, n)
        xb_r = Ring(P, "xb", [128, 8, NB], F32, 1)
        vf_r = Ring(P, "vf", [128, 8, NB], F32, 1)
        ub_r = Ring(P, "ubf", [128, 8, NB], BF16, 1)
        vb_r = Ring(P, "vb", [128, 8, NB], BF16, 1)
        z_r = Ring(P, "z", [128, 8, NB], BF16, 1)
        y_r = [Ring(P, f"y{i}", [128, 4, NB], BF16, 2) for i in range(3)]
        gy_r = Ring(P, "gy", [128, 4, NB], BF16, 2)
        in_r = [R(f"in{i}", 2) for i in range(6)]
        tmp = dict(g1=R("g1"), g2=R("g2"), sq=R("sq"), mean=R("mean"), var=R("var"), t=R("t", 3))
        s_r, gs_r, zt_r, zz_r, ax_r = R("s"), R("gs"), R("zt"), R("zz"), R("ax")
        psA = Ring(P, "psA", [128, 512], F32, 3, psum=True)
        psB = Ring(P, "psB", [128, 512], F32, 3, psum=True)
        psS = Ring(P, "psS", [128, 512], F32, 2, psum=True)
        lg_r = Ring(P, "lg", [128, 32], F32, 2); ex_r = Ring(P, "ex", [128, 32], F32, 2); mk_r = Ring(P, "mk", [128, 32], F32, 2)
        m8_r = Ring(P, "m8", [128, 8], F32, 2); sm_r = Ring(P, "sm", [128, 2], F32, 2)
        gT_r = Ring(P, "gT", [32, NB], F32, 2)
        xT_v = xT.rearrange("(k p) t -> p k t", p=128)
        yc_v = ycT.rearrange("(c p) t -> p c t", p=128)
        CB = lambda i: cl[:, i:i + 1]
        def c2_block(t0, n):
            j = 1 if t0 < CTX else 0
            xb, dxb = xb_r.next()
            P.dma('sp', xb[:], xT_v[:, :, t0:t0 + n], writes=[dxb])
            ubf, dub = ub_r.next()
            for k in range(8):
                P.op('dve' if k % 2 else 'pool', lambda e, k=k: e.tensor_scalar(ubf[:, k, :], xb[:, k, :], scp[:, 2 * k + j:2 * k + j + 1],
                     modT[:, 2 * k + j:2 * k + j + 1], ALU.mult, ALU.add), reads=[dxb, dscp, dmod], writes=[dub], accum=(k > 0))
            ya, dya = y_r[0].next()
            for c in range(4):
                tl = []
                for i in range(3):
                    tt_, dd_ = in_r[i].next()
                    P.dma('act', tt_[:], br6[i, c * 128:(c + 1) * 128, t0:t0 + n], writes=[dd_])
                    tl.append((tt_, dd_))
                (ga, dga), (hf, dhf), (hb, dhb) = tl
                s_, ds_ = s_r.next()
                P.op('pool', lambda e, s_=s_, hf=hf, hb=hb: e.tensor_tensor(s_[:], hf[:], hb[:], ALU.add), reads=[dhf, dhb], writes=[ds_])
                gs, dgs = gs_r.next()
                gelu_fm(P, gs[:], dgs, ga[:], dga, n, tmp)
                P.op('dve', lambda e, c=c, s_=s_, gs=gs: e.tensor_tensor(ya[:, c, :], s_[:], gs[:], ALU.mult), reads=[ds_, dgs], writes=[dya], accum=(c > 0))
            gy, dgy = gy_r.next()
            for c in range(4):
                tl = []
                for i in range(3, 6):
                    tt_, dd_ = in_r[i].next()
                    P.dma('act', tt_[:], br6[i, c * 128:(c + 1) * 128, t0:t0 + n], writes=[dd_])
                    tl.append((tt_, dd_))
                (ub, dubb), (yf, dyf), (ybk, dybk) = tl
                s_, ds_ = s_r.next()
                P.op('pool', lambda e, s_=s_, yf=yf, ybk=ybk: e.tensor_tensor(s_[:], yf[:], ybk[:], ALU.add), reads=[dyf, dybk], writes=[ds_])
                P.op('dve', lambda e, s_=s_, ub=ub, c=c: e.scalar_tensor_tensor(s_[:], ub[:], CB(32 + c), s_[:], ALU.mult, ALU.add),
                     reads=[ds_, dubb, dcl], writes=[ds_])
                gelu_fm(P, gy[:, c, :], dgy, s_[:], ds_, n, tmp)
            yb, dyb = y_r[1].next()
            for oc in range(4):
                pa, dpa = psA.next()
                pb, dpb = psB.next()
                for c in range(4):
                    P.op('pe', lambda e, pa=pa, c=c, oc=oc: e.matmul(pa[:, 0:n], wl[:, c, oc * 128:(oc + 1) * 128], gy[:, c, :], start=(c == 0), stop=(c == 3)),
                         reads=[dwl, dgy], writes=[dpa], pe_acc=True)
                for c in range(4):
                    P.op('pe', lambda e, pb=pb, c=c, oc=oc: e.matmul(pb[:, 0:n], wl[:, c, 512 + oc * 128:512 + (oc + 1) * 128], gy[:, c, :], start=(c == 0), stop=(c == 3)),
                         reads=[dwl, dgy], writes=[dpb], pe_acc=True)
                gs, dgs = gs_r.next()
                P.op('act', lambda e, gs=gs, pb=pb, oc=oc: e.activation(gs[:], pb[:, 0:n], AF.Sigmoid, bias=CB(24 + 4 + oc)), reads=[dpb, dcl], writes=[dgs])
                P.op('dve', lambda e, gs=gs, pa=pa, oc=oc: e.scalar_tensor_tensor(yb[:, oc, :], pa[:, 0:n], CB(24 + oc), gs[:], ALU.add, ALU.mult),
                     reads=[dpa, dgs, dcl], writes=[dyb], accum=(oc > 0))
            yc, dyc = y_r[2].next()
            P.dma('sp', yc[:], yc_v[:, :, t0:t0 + n], writes=[dyc])
            ys = [(ya, dya), (yb, dyb), (yc, dyc)]
            z, dz = z_r.next()
            for oc in range(8):
                zz, dzz = zz_r.next()
                for i in range(3):
                    pa, dpa = psA.next()
                    for k in range(8):
                        P.op('pe', lambda e, pa=pa, k=k, i=i, oc=oc: e.matmul(pa[:, 0:n], wg[:, k, i * 1024 + oc * 128:i * 1024 + (oc + 1) * 128], ubf[:, k, :],
                                                                       start=(k == 0), stop=(k == 7)), reads=[dwg[k], dub], writes=[dpa], pe_acc=True)
                    gs, dgs = gs_r.next()
                    P.op('act', lambda e, gs=gs, pa=pa, i=i, oc=oc: e.activation(gs[:], pa[:, 0:n], AF.Sigmoid, bias=CB(i * 8 + oc)), reads=[dpa, dcl], writes=[dgs])
                    pb, dpb = psB.next()
                    yi, dyi = ys[i]
                    for c in range(4):
                        P.op('pe', lambda e, pb=pb, c=c, i=i, oc=oc, yi=yi: e.matmul(pb[:, 0:n], wp[:, i, c, oc * 128:(oc + 1) * 128], yi[:, c, :],
                                                                              start=(c == 0), stop=(c == 3)), reads=[dwp, dyi], writes=[dpb], pe_acc=True)
                    if i == 0:
                        P.op('dve', lambda e, zz=zz, gs=gs, pb=pb: e.tensor_tensor(zz[:], pb[:, 0:n], gs[:], ALU.mult), reads=[dpb, dgs], writes=[dzz])
                    else:
                        zt, dzt = zt_r.next()
                        P.op('dve', lambda e, zt=zt, gs=gs, pb=pb: e.tensor_tensor(zt[:], pb[:, 0:n], gs[:], ALU.mult), reads=[dpb, dgs], writes=[dzt])
                        if i == 1:
                            P.op('pool', lambda e, zz=zz, zt=zt: e.tensor_tensor(zz[:], zz[:], zt[:], ALU.add), reads=[dzz, dzt], writes=[dzz])
                        else:
                            P.op('pool', lambda e, zz=zz, zt=zt, oc=oc: e.tensor_tensor(z[:, oc, :], zz[:], zt[:], ALU.add), reads=[dzz, dzt], writes=[dz], accum=(oc > 0))
            for oc in range(8):
                pa, dpa = psA.next()
                for k in range(8):
                    P.op('pe', lambda e, pa=pa, k=k, oc=oc: e.matmul(pa[:, 0:n], wo[:, k, oc * 128:(oc + 1) * 128], z[:, k, :], start=(k == 0), stop=(k == 7)),
                         reads=[dwo, dz], writes=[dpa], pe_acc=True)
                ax, dax = ax_r.next()
                P.op('pool', lambda e, ax=ax, oc=oc: e.tensor_scalar(ax[:], xb[:, oc, :], DN_ALPHA, None, ALU.mult), reads=[dxb], writes=[dax])
                P.op('dve', lambda e, ax=ax, pa=pa, oc=oc: e.scalar_tensor_tensor(xb[:, oc, :], pa[:, 0:n], modT[:, 32 + 2 * oc + j:32 + 2 * oc + j + 1], ax[:],
                                                                             ALU.mult, ALU.add), reads=[dpa, dax, dmod, dxb], writes=[dxb])
            vf, dvf = vf_r.next()
            ln_fm(P, xb, dxb, n, onesf, dof, lambda k: CB(36 + k), lambda k: CB(44 + k), dcl, vf, dvf, psS, tmp)
            P.op('pool', lambda e: e.tensor_scalar(xb[:], vf[:], DN_ALPHA, None, ALU.mult), reads=[dvf, dxb], writes=[dxb])
            P.dma('sp', acc_o.rearrange("(k p) t -> p k t", p=128)[:, :, t0:t0 + n], xb[:], reads=[dxb])
            for k in range(8):
                P.op('dve', lambda e, k=k: e.tensor_scalar(vf[:, k, :], vf[:, k, :], scp[:, 16 + 2 * k + j:16 + 2 * k + j + 1],
                     modT[:, 48 + 2 * k + j:48 + 2 * k + j + 1], ALU.mult, ALU.add), reads=[dvf, dscp, dmod, dxb], writes=[dvf])
            vb, dvb = vb_r.next()
            P.op('act', lambda e: e.copy(vb[:], vf[:]), reads=[dvf], writes=[dvb])
            P.dma('act', v_o.rearrange("(k p) t -> p k t", p=128)[:, :, t0:t0 + n], vb[:], reads=[dvb])
            gT, dgT = gT_r.next()
            for tt in range(n // 128):
                pl, dpl = psS.next()
                for k in range(8):
                    P.op('pe', lambda e, pl=pl, k=k, tt=tt: e.matmul(pl[:, 0:32], vf[:, k, tt * 128:(tt + 1) * 128], wrt[:, k, :], start=(k == 0), stop=(k == 7)),
                         reads=[dvf, dwrt], writes=[dpl], pe_acc=True)
                lg, dlg = lg_r.next(); ex, dex = ex_r.next(); mk, dmk = mk_r.next(); m8, dm8 = m8_r.next(); sm, dsm = sm_r.next()
                P.op('dve', lambda e, lg=lg, pl=pl: e.tensor_tensor(lg[:], pl[:, 0:32], brt[:], ALU.add), reads=[dpl, dbrt], writes=[dlg])
                P.op('dve', lambda e, lg=lg, m8=m8: e.max(m8[:], lg[:]), reads=[dlg], writes=[dm8])
                P.op('dve', lambda e, lg=lg, m8=m8, mk=mk: e.tensor_scalar(mk[:], lg[:], m8[:, 3:4], None, ALU.is_ge), reads=[dlg, dm8], writes=[dmk])
                P.op('dve', lambda e, m8=m8, sm=sm: e.tensor_scalar(sm[:, 0:1], m8[:, 0:1], -1.0, None, ALU.mult), reads=[dm8], writes=[dsm])
                P.op('act', lambda e, ex=ex, lg=lg, sm=sm: e.activation(ex[:], lg[:], AF.Exp, bias=sm[:, 0:1]), reads=[dlg, dsm], writes=[dex])
                P.op('dve', lambda e, ex=ex, mk=mk: e.tensor_tensor(ex[:], ex[:], mk[:], ALU.mult), reads=[dex, dmk], writes=[dex])
                P.op('dve', lambda e, ex=ex, sm=sm: e.reduce_sum(sm[:, 1:2], ex[:], AX.X), reads=[dex, dsm], writes=[dsm])
                P.op('dve', lambda e, sm=sm: e.reciprocal(sm[:, 1:2], sm[:, 1:2]), reads=[dsm], writes=[dsm])
                P.op('dve', lambda e, ex=ex, sm=sm: e.tensor_scalar(ex[:], ex[:], sm[:, 1:2], None, ALU.mult), reads=[dex, dsm], writes=[dex])
                pt, dpt = psS.next()
                P.op('pe', lambda e, pt=pt, ex=ex: e.transpose(pt[0:32, 0:128], ex[:], idf[:]), reads=[dex, did], writes=[dpt])
                P.op('act', lambda e, pt=pt, tt=tt: e.copy(gT[:, tt * 128:(tt + 1) * 128], pt[0:32, 0:128]), reads=[dpt], writes=[dgT], accum=(tt > 0))
            P.dma('sp', g_o[:, t0:t0 + n], gT[:], reads=[dgT])

        for (t0_, n_) in C2BLK:
            c2_block(t0_, n_)
        env.end(P)
    return nc


def build_C3(env=None):
    env = env or SoloEnv()
    nc = env.nc
    dt_in, dt_out = env.dt_in, env.dt_out
    accT = dt_in("accT", [1024, TOK])
    vT = dt_in("vT", [1024, TOK], BF16)
    gateT = dt_in("gateT", [32, TOK])
    w_gu = dt_in("w_gu", [32, 1024, 2048])
    w_dn = dt_in("w_dn", [32, 1024, 1024])
    b_guT = dt_in("b_guT", [128, 32 * 16])
    b_dn = dt_in("b_dn", [32, 1024])
    cols_i = dt_in("cols", [128, 32])
    x2_o = dt_out("x2T", [1024, TOK])
    with env.scope() as P:
        acc = P.sbuf("acc", [128, 8, TOK]); dacc = [Dep(f"acc{b}") for b in range(len(BLOCKS))]
        vb = P.sbuf("vb", [128, 8, TOK], BF16); dvb = Dep("vb")
        acc_v = accT.rearrange("(k p) t -> p k t", p=128)
        v_v = vT.rearrange("(k p) t -> p k t", p=128)
        for bi, (t0, n) in enumerate(BLOCKS):
            P.dma('sp', acc[:, :, t0:t0 + n], acc_v[:, :, t0:t0 + n], writes=[dacc[bi]])
            P.dma('act', vb[:, :, t0:t0 + n], v_v[:, :, t0:t0 + n], writes=[dvb], accum=(bi > 0))
        gt = P.sbuf("gt", [32, TOK]); dgt = Dep("gt")
        P.dma('sp', gt[:], gateT, writes=[dgt])
        bg = P.sbuf("bg", [128, 512]); dbg = Dep("bg")
        P.dma('sp', bg[:], b_guT, writes=[dbg])
        bgv = bg[:].rearrange("p (e c) -> p e c", c=16)
        P.op('dve', lambda e: e.tensor_scalar(bgv[:, :, 8:16], bgv[:, :, 8:16], 1.0, None, ALU.add), reads=[dbg], writes=[dbg])
        bd = P.sbuf("bd", [32, 1024]); dbd = Dep("bd")
        P.dma('sp', bd[:], b_dn, writes=[dbd])
        cl = P.sbuf("cl", [128, 32]); dcl = Dep("cl")
        P.dma('sp', cl[:], cols_i, writes=[dcl])
        onesf = P.sbuf("onesf", [128, 128]); dof = Dep("onesf")
        P.op('pool', lambda e: e.memset(onesf[:], 1.0 / 1024.0), writes=[dof])
        ii = P.sbuf("ii", [32, 32], I32); dii = Dep("ii")
        cf = P.sbuf("cf", [32, 32]); dcf = Dep("cf")
        pf = P.sbuf("pf", [32, 32]); dpf = Dep("pf")
        idn = P.sbuf("idn", [32, 32]); did = Dep("idn")
        P.op('pool', lambda e: e.iota(ii[:], pattern=[[1, 32]], base=0, channel_multiplier=0), writes=[dii])
        P.op('dve', lambda e: e.tensor_copy(cf[:], ii[:]), reads=[dii], writes=[dcf])
        P.op('pool', lambda e: e.iota(ii[:], pattern=[[0, 32]], base=0, channel_multiplier=1), reads=[dcf], writes=[dii])
        P.op('dve', lambda e: e.tensor_copy(pf[:], ii[:]), reads=[dii], writes=[dpf])
        P.op('dve', lambda e: e.tensor_tensor(idn[:], cf[:], pf[:], ALU.is_equal), reads=[dcf, dpf], writes=[did])
        dsel = did
        psG = Ring(P, "psG", [128, 512], F32, 2, psum=True)
        psL = Ring(P, "psL", [128, 512], F32, 2, psum=True)
        psD = Ring(P, "psD", [128, 512], F32, 2, psum=True)
        psX = Ring(P, "psX", [128, 512], F32, 2, psum=True)
        gbc_r = Ring(P, "gbc", [128, TOK], F32, 1)
        wgl_r = Ring(P, "wgl", [128, 2, 8, 128], BF16, 2)
        wd_r = Ring(P, "wd", [128, 4, 1024], BF16, 2)
        act_r = Ring(P, "act", [128, 4, TOK], BF16, 1)
        R = lambda nm, n=2, d=F32: Ring(P, nm, [128, 512], d, n)
        hg_r, s1_r, hl_r, t_r = R("hg"), R("s1"), R("hl"), R("tt")
        for ex in range(32):
            gbc, dgbc = gbc_r.next()
            for bi, (t0, n) in enumerate(BLOCKS):
                px, dpx = psX.next()
                P.op('pe', lambda e, px=px, ex=ex, t0=t0, n=n: e.matmul(px[:, 0:n], idn[:, ex:ex + 1].to_broadcast([32, 128]), gt[:, t0:t0 + n], start=True, stop=True),
                     reads=[dsel, dgt], writes=[dpx])
                P.op('act', lambda e, px=px, gbc=gbc, t0=t0, n=n: e.activation(gbc[:, t0:t0 + n], px[:, 0:n], AF.Copy, scale=1.0 / 1.702),
                     reads=[dpx], writes=[dgbc], accum=(bi > 0))
            for half in range(2):
                wd, dwd = wd_r.next()
                for f4 in range(4):
                    fc = half * 4 + f4
                    P.dma('pool', wd[:, f4, :], w_dn[ex, fc * 128:(fc + 1) * 128, :], writes=[dwd], accum=(f4 > 0))
                at, dat = act_r.next()
                for f4 in range(4):
                    fc = half * 4 + f4
                    wgl, dwgl = wgl_r.next()
                    P.dma('pool', wgl[:, 0, :, :], w_gu[ex, :, fc * 128:(fc + 1) * 128].rearrange("(k p) j -> p k j", p=128), writes=[dwgl])
                    P.dma('pool', wgl[:, 1, :, :], w_gu[ex, :, 1024 + fc * 128:1024 + (fc + 1) * 128].rearrange("(k p) j -> p k j", p=128),
                          writes=[dwgl], accum=True)
                    for bi, (t0, n) in enumerate(BLOCKS):
                        pg, dpg = psG.next()
                        pl, dpl = psL.next()
                        for k in range(8):
                            P.op('pe', lambda e, pg=pg, k=k, wgl=wgl, t0=t0, n=n: e.matmul(pg[:, 0:n], wgl[:, 0, k, :], vb[:, k, t0:t0 + n], start=(k == 0), stop=(k == 7)),
                                 reads=[dwgl, dvb], writes=[dpg], pe_acc=True)
                        for k in range(8):
                            P.op('pe', lambda e, pl=pl, k=k, wgl=wgl, t0=t0, n=n: e.matmul(pl[:, 0:n], wgl[:, 1, k, :], vb[:, k, t0:t0 + n], start=(k == 0), stop=(k == 7)),
                                 reads=[dwgl, dvb], writes=[dpl], pe_acc=True)
                        hg, dhg = hg_r.next(); s1, ds1 = s1_r.next(); hl, dhl = hl_r.next(); tt, dtt = t_r.next()
                        P.op('dve', lambda e, hg=hg, pg=pg, n=n, ex=ex, fc=fc: e.tensor_scalar(hg[:, 0:n], pg[:, 0:n], bgv[:, ex, fc:fc + 1], 7.0, ALU.add, ALU.min),
                             reads=[dpg, dbg], writes=[dhg])
                        P.op('act', lambda e, s1=s1, hg=hg, n=n: e.activation(s1[:, 0:n], hg[:, 0:n], AF.Silu, scale=1.702), reads=[dhg], writes=[ds1])
                        P.op('dve', lambda e, hl=hl, pl=pl, n=n, ex=ex, fc=fc: e.tensor_scalar(hl[:, 0:n], pl[:, 0:n], bgv[:, ex, 8 + fc:9 + fc], 8.0, ALU.add, ALU.min),
                             reads=[dpl, dbg], writes=[dhl])
                        P.op('dve', lambda e, tt=tt, hl=hl, s1=s1, n=n: e.scalar_tensor_tensor(tt[:, 0:n], hl[:, 0:n], -6.0, s1[:, 0:n], ALU.max, ALU.mult),
                             reads=[dhl, ds1], writes=[dtt])
                        P.op('pool', lambda e, at=at, tt=tt, gbc=gbc, f4=f4, t0=t0, n=n: e.tensor_tensor(at[:, f4, t0:t0 + n], tt[:, 0:n], gbc[:, t0:t0 + n], ALU.mult),
                             reads=[dtt, dgbc], writes=[dat], accum=not (f4 == 0 and bi == 0))
                for bi, (t0, n) in enumerate(BLOCKS):
                    jj = 1 if t0 < CTX else 0
                    for oc in range(8):
                        pd, dpd = psD.next()
                        for f4 in range(4):
                            P.op('pe', lambda e, pd=pd, f4=f4, oc=oc, t0=t0, n=n, wd=wd, at=at: e.matmul(pd[:, 0:n], wd[:, f4, oc * 128:(oc + 1) * 128], at[:, f4, t0:t0 + n],
                                 start=(f4 == 0), stop=(f4 == 3)), reads=[dwd, dat], writes=[dpd], pe_acc=True)
                        P.op('dve', lambda e, pd=pd, oc=oc, t0=t0, n=n, jj=jj: e.scalar_tensor_tensor(acc[:, oc, t0:t0 + n], pd[:, 0:n], cl[:, jj * 8 + oc:jj * 8 + oc + 1],
                             acc[:, oc, t0:t0 + n], ALU.mult, ALU.add), reads=[dpd, dcl, dacc[bi]], writes=[dacc[bi]])
        for bi, (t0, n) in enumerate(BLOCKS):
            jj = 1 if t0 < CTX else 0
            for oc in range(8):
                pd, dpd = psD.next()
                P.op('pe', lambda e, pd=pd, oc=oc, t0=t0, n=n: e.matmul(pd[:, 0:n], bd[:, oc * 128:(oc + 1) * 128], gt[:, t0:t0 + n], start=True, stop=True),
                     reads=[dbd, dgt], writes=[dpd])
                P.op('dve', lambda e, pd=pd, oc=oc, t0=t0, n=n, jj=jj: e.scalar_tensor_tensor(acc[:, oc, t0:t0 + n], pd[:, 0:n], cl[:, jj * 8 + oc:jj * 8 + oc + 1],
                     acc[:, oc, t0:t0 + n], ALU.mult, ALU.add), reads=[dpd, dcl, dacc[bi]], writes=[dacc[bi]])
        tmp = dict(sq=R("sq"), mean=R("mean", 1), var=R("var", 1), t=R("t", 2))
        x2_v = x2_o.rearrange("(k p) t -> p k t", p=128)
        for bi, (t0, n) in enumerate(BLOCKS):
            av = acc[:, :, t0:t0 + n]
            ln_fm(P, av, dacc[bi], n, onesf, dof, lambda k: cl[:, 16 + k:17 + k], lambda k: cl[:, 24 + k:25 + k], dcl, av, dacc[bi], psX, tmp)
            P.dma('sp', x2_v[:, :, t0:t0 + n], av, reads=[dacc[bi]])
        env.end(P)
    return nc


def _colT(vec, n):
    return np.ascontiguousarray(np.asarray(vec, np.float32).reshape(n, 128).T)


def host_C2(inp, l, xT_full, modT, xgu_full, Hf, Hb, Yf, Yb, yc_cores):
    cols = np.zeros((128, 64), np.float32)
    for i in range(3):
        cols[:, i * 8:(i + 1) * 8] = _colT(inp['b_gate'][l, i], 8)
    cols[:, 24:32] = _colT(inp['s5_b_glu'][l], 8)
    cols[:, 32:36] = _colT(inp['s5_d'][l], 4)
    cols[:, 36:44] = _colT(inp['ln_g'][l, 0], 8)
    cols[:, 44:52] = _colT(inp['ln_b'][l, 0], 8)
    w_gate = np.ascontiguousarray(inp['w_in'][l][:, 3072:6144])
    maps = []
    for c in range(NCORE):
        cc = np.concatenate([np.arange(CTX), CTX + c * LAT_PC + np.arange(LAT_PC)])
        br6 = np.stack([xgu_full[512:1024][:, cc], Hf[:, cc], Hb[:, cc], xgu_full[1024:1536][:, cc], Yf[:, cc], Yb[:, cc]], 0)
        maps.append(dict(xT=np.ascontiguousarray(xT_full[:, cc]), modT=modT, br6=np.ascontiguousarray(br6, dtype=np.float32),
                         ycT=yc_cores[c], w_gate=w_gate, w_proj=inp['w_proj'][l], w_out=inp['w_out'][l],
                         w_glu=inp['s5_w_glu'][l], cols=cols, w_rt=inp['w_router'][l],
                         b_rt=np.ascontiguousarray(inp['b_router'][l][None, :])))
    return maps


def host_C3(inp, l, modT, c2_res):
    cols = np.zeros((128, 32), np.float32)
    m3 = modT.reshape(128, 48, 2)
    cols[:, 0:8] = m3[:, 40:48, 0]
    cols[:, 8:16] = m3[:, 40:48, 1]
    cols[:, 16:24] = _colT(inp['ln_g'][l, 1], 8)
    cols[:, 24:32] = _colT(inp['ln_b'][l, 1], 8)
    bgu = inp['b_gu'][l]
    b_guT = np.ascontiguousarray(bgu.reshape(32, 16, 128).transpose(2, 0, 1).reshape(128, 512))
    maps = []
    for c in range(NCORE):
        r = c2_res[c]
        maps.append(dict(accT=r['accT'], vT=r['vT'], gateT=r['gateT'], w_gu=inp['w_gu'][l], w_dn=inp['w_down'][l],
                         b_guT=b_guT, b_dn=inp['b_down'][l], cols=cols))
    return maps


def _gather_cols(per_core, rows, dtype):
    out = np.zeros((rows, TT), dtype)
    for c in range(NCORE):
        r = per_core[c]
        if c == 0:
            out[:, :CTX] = r[:, :CTX]
        out[:, CTX + c * LAT_PC:CTX + (c + 1) * LAT_PC] = r[:, CTX:]
    return out


_PROGS = {}


def build_fused(with_A):
    env = FusedEnv()
    yc = env.scratch("h_ycT", [512, TOK], BF16)
    acc = env.scratch("h_accT", [1024, TOK], F32)
    vT = env.scratch("h_vT", [1024, TOK], BF16)
    gT = env.scratch("h_gateT", [32, TOK], F32)
    env.begin("c1_", {"ycT": yc})
    build_C1(env)
    env.begin("c2_", {"ycT": yc, "accT": acc, "vT": vT, "gateT": gT})
    build_C2(env)
    if with_A:
        x2 = env.nc.dram_tensor("c3_x2T", [1024, TOK], F32, kind="ExternalOutput").ap()
        env.begin("c3_", {"accT": acc, "vT": vT, "gateT": gT, "x2T": x2})
        build_C3(env)
        env.begin("a_", {"xT": x2})
        build_A(env)
    else:
        env.begin("c3_", {"accT": acc, "vT": vT, "gateT": gT})
        build_C3(env)
    return env.close()


def _prog(name):
    if name not in _PROGS:
        _PROGS[name] = dict(A=build_A, B=build_B, CCCA=lambda: build_fused(True), CCC=lambda: build_fused(False))[name]()
    return _PROGS[name]


def _run(name, maps):
    res = run_bass_kernel_spmd(_prog(name), maps, core_ids=list(range(NCORE)))
    return [{k: np.asarray(v) for k, v in r.items()} for r in res.results]


def _pref(p, m, drop=()):
    return {p + k: v for k, v in m.items() if k not in drop}


def kernel(**inp):
    inp = {k: np.asarray(v) for k, v in inp.items()}
    xT_full = np.ascontiguousarray(np.concatenate([inp['ctx'][0], inp['x'][0]], 0).T.astype(np.float32))
    rev = np.concatenate([np.arange(CTX)[::-1], CTX + np.arange(SEQ)[::-1]])
    ra = _run('A', host_A(inp, 0, xT_full))
    out = None
    for l in range(2):
        modT = ra[0]['modT']
        xgu_full = _gather_cols([r['xguT'] for r in ra], 1536, np.float32)
        qk_full = _gather_cols([r['qkT'] for r in ra], 1024, ra[0]['qkT'].dtype)
        v_full = np.zeros((TT, 512), ra[0]['v'].dtype)
        for c in range(NCORE):
            v_full[:CTX] = ra[c]['v'][:CTX]
            v_full[CTX + c * LAT_PC:CTX + (c + 1) * LAT_PC] = ra[c]['v'][CTX:]
        rb = _run('B', host_B(inp, l, xgu_full))
        Hf = np.concatenate([r['h'][0:64] for r in rb], 0)
        Hb = np.concatenate([r['h'][64:128][:, rev] for r in rb], 0)
        Yf = np.concatenate([r['ys5'][0:64] for r in rb], 0)
        Yb = np.concatenate([r['ys5'][64:128][:, rev] for r in rb], 0)
        m1 = host_C1(inp, l, qk_full, v_full)
        m2 = host_C2(inp, l, xT_full, modT, xgu_full, Hf, Hb, Yf, Yb, [None] * NCORE)
        m3 = host_C3(inp, l, modT, [dict(accT=None, vT=None, gateT=None)] * NCORE)
        last = (l == 1)
        ma = None if last else host_A(inp, l + 1, xT_full)
        maps = []
        for c in range(NCORE):
            d = {}
            d.update(_pref("c1_", m1[c]))
            d.update(_pref("c2_", m2[c], drop=("ycT",)))
            d.update(_pref("c3_", m3[c], drop=("accT", "vT", "gateT")))
            if not last:
                d.update(_pref("a_", ma[c], drop=("xT",)))
            maps.append(d)
        rf = _run('CCC' if last else 'CCCA', maps)
        xT_full = _gather_cols([r['c3_x2T'] for r in rf], 1024, np.float32)
        if last:
            out = np.ascontiguousarray(xT_full[:, CTX:].T)[None].astype(np.float32)
        else:
            ra = [{k[2:]: v for k, v in r.items() if k.startswith('a_')} for r in rf]
    return out
```
